# Optimizing a Trainium2 kernel written in Bass

```python
import jax, jax.numpy as jnp
from jax import lax
import numpy as np


D_MODEL = 1024
BATCH = 8
SEQ = 2048
DEPTH = 4

HEAD_DIM = 64
ROPE_DIM = HEAD_DIM // 4
ROPE_THETA = 500000.0
N_FOX_HEADS = 8
N_DSA_HEADS = 8
N_MOBA_HEADS = D_MODEL // HEAD_DIM
DSA_KV_RANK = 128
DSA_NOPE_DIM = HEAD_DIM - ROPE_DIM
DSA_V_DIM = HEAD_DIM
N_IDX_HEADS = 4
IDX_DIM = 64
DSA_TOPK_MAX = 256
MOBA_BLOCK = 256
MOBA_TOPB_MAX = 3
D_FF = 2816
CONV_WIDTH = 3
Q_BLOCK = 128
MOBA_Q_CHUNK = 16
EPS = 1e-6
N_EVEN = (DEPTH + 1) // 2
N_ODD = DEPTH // 2
FOX_WIDTH = N_FOX_HEADS * HEAD_DIM
DSA_WIDTH = N_DSA_HEADS * HEAD_DIM
EVEN_SIZES = (FOX_WIDTH, FOX_WIDTH, FOX_WIDTH, N_FOX_HEADS, DSA_WIDTH, DSA_KV_RANK, ROPE_DIM,
              N_IDX_HEADS * IDX_DIM, IDX_DIM, N_IDX_HEADS)
EVEN_IN_DIM = sum(EVEN_SIZES)
ODD_IN_DIM = 3 * N_MOBA_HEADS * HEAD_DIM

kernel_name = "fox_dsa_moba_convglu_hybrid"


def rmsnorm(x, g):
    xf = x.astype(jnp.float32)
    y = xf * lax.rsqrt(jnp.mean(xf * xf, axis=-1, keepdims=True) + EPS) * g.astype(jnp.float32)
    return y.astype(x.dtype)


def rope_tables(positions):
    inv_freq = ROPE_THETA ** (-jnp.arange(0, ROPE_DIM, 2, dtype=jnp.float32) / ROPE_DIM)
    ang = positions.astype(jnp.float32)[..., None] * inv_freq
    return jnp.cos(ang), jnp.sin(ang)


def apply_partial_rope(t, cos, sin):
    half = ROPE_DIM // 2
    tf = t[..., :ROPE_DIM].astype(jnp.float32)
    t1, t2 = tf[..., :half], tf[..., half:]
    c, s = cos[:, :, None, :], sin[:, :, None, :]
    rot = jnp.concatenate([t1 * c - t2 * s, t2 * c + t1 * s], axis=-1).astype(t.dtype)
    return jnp.concatenate([rot, t[..., ROPE_DIM:]], axis=-1)


def to_blocks(a, nb, blk):
    return jnp.swapaxes(a.reshape(a.shape[0], nb, blk, *a.shape[2:]), 0, 1)


def from_blocks(a):
    a = jnp.swapaxes(a, 0, 1)
    return a.reshape(a.shape[0], a.shape[1] * a.shape[2], *a.shape[3:])


def fox_attention(q, k, v, log_f):
    B, S, H, D = q.shape
    nqb = S // Q_BLOCK
    c = jnp.cumsum(log_f, axis=1)
    c_k = jnp.swapaxes(c, 1, 2)
    key_pos = jnp.arange(S)
    scale = D ** -0.5

    def block(args):
        i, q_i, c_i = args
        q_pos = i * Q_BLOCK + jnp.arange(Q_BLOCK)
        s = jnp.einsum('bqhd,bkhd->bhqk', q_i, k, preferred_element_type=jnp.float32) * scale
        s = s + jnp.swapaxes(c_i, 1, 2)[..., None] - c_k[:, :, None, :]
        s = jnp.where(key_pos[None, :] <= q_pos[:, None], s, -jnp.inf)
        p = jax.nn.softmax(s, axis=-1)
        return jnp.einsum('bhqk,bkhd->bqhd', p.astype(v.dtype), v)

    out = lax.map(block, (jnp.arange(nqb), to_blocks(q, nqb, Q_BLOCK), to_blocks(c, nqb, Q_BLOCK)))
    return from_blocks(out)


def dsa_attention(q_lat, q_rope, c_kv, k_rope, iq, ik, iw, topk):
    B, S, H, R = q_lat.shape
    nqb = S // Q_BLOCK
    key_pos = jnp.arange(S)
    bidx = jnp.arange(B)[:, None, None]
    scale = HEAD_DIM ** -0.5
    idx_scale = (IDX_DIM ** -0.5) * (N_IDX_HEADS ** -0.5)

    def block(args):
        i, ql_i, qr_i, iq_i, iw_i = args
        q_pos = i * Q_BLOCK + jnp.arange(Q_BLOCK)
        causal = key_pos[None, :] <= q_pos[:, None]
        dots = jnp.einsum('bqhd,bkd->bqhk', iq_i, ik, preferred_element_type=jnp.float32)
        score = jnp.einsum('bqh,bqhk->bqk', iw_i.astype(jnp.float32), jax.nn.relu(dots)) * idx_scale
        score = jnp.where(causal[None], score, -jnp.inf)
        _, sel = lax.top_k(score, topk)
        valid = sel <= q_pos[None, :, None]
        c_sel = c_kv[bidx, sel]
        kr_sel = k_rope[bidx, sel]
        logits = (jnp.einsum('bqhr,bqkr->bqhk', ql_i, c_sel, preferred_element_type=jnp.float32)
                  + jnp.einsum('bqhd,bqkd->bqhk', qr_i, kr_sel, preferred_element_type=jnp.float32)) * scale
        logits = jnp.where(valid[:, :, None, :], logits, -jnp.inf)
        p = jax.nn.softmax(logits, axis=-1)
        return jnp.einsum('bqhk,bqkr->bqhr', p.astype(c_kv.dtype), c_sel)

    out = lax.map(block, (jnp.arange(nqb), to_blocks(q_lat, nqb, Q_BLOCK), to_blocks(q_rope, nqb, Q_BLOCK),
                          to_blocks(iq, nqb, Q_BLOCK), to_blocks(iw, nqb, Q_BLOCK)))
    return from_blocks(out)


def moba_attention(q, k, v):
    B, S, H, D = q.shape
    nb = -(-S // MOBA_BLOCK)
    topb = min(MOBA_TOPB_MAX, nb - 1)
    pad = nb * MOBA_BLOCK - S
    qh = jnp.swapaxes(q, 1, 2)
    k_pad = jnp.pad(jnp.swapaxes(k, 1, 2), ((0, 0), (0, 0), (0, pad), (0, 0)))
    v_pad = jnp.pad(jnp.swapaxes(v, 1, 2), ((0, 0), (0, 0), (0, pad), (0, 0)))
    kb = k_pad.reshape(B, H, nb, MOBA_BLOCK, D)
    vb = v_pad.reshape(B, H, nb, MOBA_BLOCK, D)
    k_mean = jnp.mean(kb.astype(jnp.float32), axis=3)
    bi = jnp.arange(B)[:, None, None, None]
    hi = jnp.arange(H)[None, :, None, None]
    scale = D ** -0.5
    n_chunks = S // MOBA_Q_CHUNK
    q_chunks = jnp.transpose(qh.reshape(B, H, n_chunks, MOBA_Q_CHUNK, D), (2, 0, 1, 3, 4))

    def chunk(args):
        i, q_i = args
        q_pos = i * MOBA_Q_CHUNK + jnp.arange(MOBA_Q_CHUNK)
        own = (i * MOBA_Q_CHUNK) // MOBA_BLOCK
        k_own = lax.dynamic_slice_in_dim(k_pad, own * MOBA_BLOCK, MOBA_BLOCK, axis=2)
        v_own = lax.dynamic_slice_in_dim(v_pad, own * MOBA_BLOCK, MOBA_BLOCK, axis=2)
        own_pos = own * MOBA_BLOCK + jnp.arange(MOBA_BLOCK)
        s_own = jnp.einsum('bhqd,bhkd->bhqk', q_i, k_own, preferred_element_type=jnp.float32) * scale
        s_own = jnp.where(own_pos[None, :] <= q_pos[:, None], s_own, -jnp.inf)
        if topb == 0:
            p = jax.nn.softmax(s_own, axis=-1)
            return jnp.einsum('bhqk,bhkd->bhqd', p.astype(v.dtype), v_own)
        gate = jnp.einsum('bhqd,bhnd->bhqn', q_i.astype(jnp.float32), k_mean)
        gate = jnp.where(jnp.arange(nb) < own, gate, -jnp.inf)
        _, sel = lax.top_k(gate, topb)
        valid = sel < own
        k_sel = kb[bi, hi, sel]
        v_sel = vb[bi, hi, sel]
        s_sel = jnp.einsum('bhqd,bhqnkd->bhqnk', q_i, k_sel, preferred_element_type=jnp.float32) * scale
        s_sel = jnp.where(valid[..., None], s_sel, -jnp.inf).reshape(B, H, MOBA_Q_CHUNK, topb * MOBA_BLOCK)
        p = jax.nn.softmax(jnp.concatenate([s_sel, s_own], axis=-1), axis=-1).astype(v.dtype)
        p_sel = p[..., :topb * MOBA_BLOCK].reshape(B, H, MOBA_Q_CHUNK, topb, MOBA_BLOCK)
        p_own = p[..., topb * MOBA_BLOCK:]
        return (jnp.einsum('bhqnk,bhqnkd->bhqd', p_sel, v_sel)
                + jnp.einsum('bhqk,bhkd->bhqd', p_own, v_own))

    out = lax.map(chunk, (jnp.arange(n_chunks), q_chunks))
    return jnp.transpose(out, (1, 0, 3, 2, 4)).reshape(B, S, H, D)


def conv_glu_ffn(x, w_up, conv_w, conv_b, w_down):
    h = x @ w_up
    g, u = h[..., :D_FF], h[..., D_FF:]
    g = lax.conv_general_dilated(g, conv_w[:, None, :].astype(g.dtype), window_strides=(1,),
                                 padding=[(CONV_WIDTH - 1, 0)], dimension_numbers=('NWC', 'WIO', 'NWC'),
                                 feature_group_count=D_FF) + conv_b
    return (jax.nn.silu(g) * u) @ w_down


def setup_inputs(seed: int = 0) -> dict:
    key = jax.random.key(seed)
    ks = jax.random.split(key, 20)
    f32 = jnp.float32
    nrm = lambda k, shape, s: jax.random.normal(k, shape, f32) * s
    res_scale = (2 * DEPTH) ** -0.5
    x = jax.random.normal(ks[0], (BATCH, SEQ, D_MODEL), f32)
    positions = jnp.broadcast_to(jnp.arange(SEQ, dtype=jnp.int32)[None, :], (BATCH, SEQ))
    return {
        'x': x,
        'positions': positions,
        'attn_norm': 1.0 + nrm(ks[1], (DEPTH, D_MODEL), 0.02),
        'ffn_norm': 1.0 + nrm(ks[2], (DEPTH, D_MODEL), 0.02),
        'final_norm': 1.0 + nrm(ks[3], (D_MODEL,), 0.02),
        'w_in_even': nrm(ks[4], (N_EVEN, D_MODEL, EVEN_IN_DIM), D_MODEL ** -0.5),
        'b_fox_f': 3.0 + nrm(ks[5], (N_EVEN, N_FOX_HEADS), 0.5),
        'g_ckv': 1.0 + nrm(ks[6], (N_EVEN, DSA_KV_RANK), 0.02),
        'w_uk': nrm(ks[7], (N_EVEN, DSA_KV_RANK, N_DSA_HEADS, DSA_NOPE_DIM), DSA_KV_RANK ** -0.5),
        'w_uv': nrm(ks[8], (N_EVEN, DSA_KV_RANK, N_DSA_HEADS, DSA_V_DIM), DSA_KV_RANK ** -0.5),
        'w_out_even': nrm(ks[9], (N_EVEN, FOX_WIDTH + DSA_WIDTH, D_MODEL), (FOX_WIDTH + DSA_WIDTH) ** -0.5 * res_scale),
        'w_in_odd': nrm(ks[10], (N_ODD, D_MODEL, ODD_IN_DIM), D_MODEL ** -0.5),
        'w_out_odd': nrm(ks[11], (N_ODD, N_MOBA_HEADS * HEAD_DIM, D_MODEL), (N_MOBA_HEADS * HEAD_DIM) ** -0.5 * res_scale),
        'w_up': nrm(ks[12], (DEPTH, D_MODEL, 2 * D_FF), D_MODEL ** -0.5),
        'conv_w': nrm(ks[13], (DEPTH, CONV_WIDTH, D_FF), CONV_WIDTH ** -0.5),
        'conv_b': nrm(ks[14], (DEPTH, D_FF), 0.01),
        'w_down': nrm(ks[15], (DEPTH, D_FF, D_MODEL), D_FF ** -0.5 * res_scale),
    }


def reference(x, positions, attn_norm, ffn_norm, final_norm, w_in_even, b_fox_f, g_ckv, w_uk, w_uv,
              w_out_even, w_in_odd, w_out_odd, w_up, conv_w, conv_b, w_down):
    B, S, _ = x.shape
    cos, sin = rope_tables(positions)
    dsa_topk = min(DSA_TOPK_MAX, S // 4)
    split_points = [sum(EVEN_SIZES[:j + 1]) for j in range(len(EVEN_SIZES) - 1)]
    for layer in range(DEPTH):
        h = rmsnorm(x, attn_norm[layer])
        if layer % 2 == 0:
            i = layer // 2
            proj = h @ w_in_even[i]
            fq, fk, fv, ff, dq, ckv, dkr, iq, ik, iw = jnp.split(proj, split_points, axis=-1)
            log_f = jax.nn.log_sigmoid(ff.astype(jnp.float32) + b_fox_f[i].astype(jnp.float32))
            fox_out = fox_attention(fq.reshape(B, S, N_FOX_HEADS, HEAD_DIM),
                                    fk.reshape(B, S, N_FOX_HEADS, HEAD_DIM),
                                    fv.reshape(B, S, N_FOX_HEADS, HEAD_DIM), log_f)
            dq = apply_partial_rope(dq.reshape(B, S, N_DSA_HEADS, HEAD_DIM), cos, sin)
            q_rope, q_nope = dq[..., :ROPE_DIM], dq[..., ROPE_DIM:]
            c_kv = rmsnorm(ckv, g_ckv[i])
            k_rope = apply_partial_rope(dkr[:, :, None, :], cos, sin)[:, :, 0, :]
            iq = apply_partial_rope(iq.reshape(B, S, N_IDX_HEADS, IDX_DIM), cos, sin)
            ik = apply_partial_rope(ik[:, :, None, :], cos, sin)[:, :, 0, :]
            q_lat = jnp.einsum('bshn,rhn->bshr', q_nope, w_uk[i])
            o_lat = dsa_attention(q_lat, q_rope, c_kv, k_rope, iq, ik, iw, dsa_topk)
            dsa_out = jnp.einsum('bshr,rhv->bshv', o_lat, w_uv[i])
            mixed = jnp.concatenate([fox_out.reshape(B, S, FOX_WIDTH), dsa_out.reshape(B, S, DSA_WIDTH)], axis=-1)
            x = x + mixed @ w_out_even[i]
        else:
            i = layer // 2
            proj = h @ w_in_odd[i]
            mq, mk, mv = jnp.split(proj, 3, axis=-1)
            mq = apply_partial_rope(mq.reshape(B, S, N_MOBA_HEADS, HEAD_DIM), cos, sin)
            mk = apply_partial_rope(mk.reshape(B, S, N_MOBA_HEADS, HEAD_DIM), cos, sin)
            mv = mv.reshape(B, S, N_MOBA_HEADS, HEAD_DIM)
            moba_out = moba_attention(mq, mk, mv)
            x = x + moba_out.reshape(B, S, N_MOBA_HEADS * HEAD_DIM) @ w_out_odd[i]
        x = x + conv_glu_ffn(rmsnorm(x, ffn_norm[layer]), w_up[layer], conv_w[layer], conv_b[layer], w_down[layer])
    return rmsnorm(x, final_norm)
```

```python
import numpy as np
import concourse.bass as bass
import concourse.mybir as mybir
from concourse.bass_utils import run_bass_kernel_spmd

F32 = mybir.dt.float32
BF16 = mybir.dt.bfloat16
I32 = mybir.dt.int32
ALU = mybir.AluOpType
AF = mybir.ActivationFunctionType
AX = mybir.AxisListType

SEM_ROT = 16000
import os
STOP = os.environ.get('EVEN_STOP', '')


class Buf:
    __slots__ = ("w", "r", "name")

    def __init__(self, name=""):
        self.w = None
        self.r = {}
        self.name = name


class EngCtx:
    def __init__(self, fw, name, eng, order_raw):
        self.fw, self.name, self.eng = fw, name, eng
        self.sem = fw.new_sem(name)
        self.count = 0
        self.seen = {}
        self.order_raw = order_raw
        self.nrot = 0
        self.prev = None

    def rotate(self):
        if self.count >= SEM_ROT:
            self.nrot += 1
            self.prev = (self.sem, self.count)
            self.sem = self.fw.new_sem("%s_r%d" % (self.name, self.nrot))
            self.count = 0


class FW:
    def __init__(self, nc):
        self.nc = nc
        self.nsem = 0
        self.pe = EngCtx(self, "pe", nc.tensor, False)
        self.act = EngCtx(self, "act", nc.scalar, True)
        self.dve = EngCtx(self, "dve", nc.vector, True)
        self.pool = EngCtx(self, "pool", nc.gpsimd, True)
        self.sp = EngCtx(self, "sp", nc.sync, True)
        self.dma_pools = {}
        for q in (self.sp, self.pool):
            self.dma_pools[id(q)] = [[[self.new_sem("dma_%s%d" % (q.name, i)), 0] for i in range(12)], 0]
        self.n_ops = 0

    def new_sem(self, name):
        self.nsem += 1
        return self.nc.alloc_semaphore(name="s_%s_%d" % (name, self.nsem))

    def _deps(self, E, order_raw, seen, reads, writes):
        deps = {}

        def add(d, raw):
            if d is None:
                return
            key, sem, val = d
            if E is not None and key is E:
                if not order_raw:
                    return
            k = id(sem)
            if k not in deps or deps[k][1] < val:
                deps[k] = (sem, val)

        for b in reads:
            add(b.w, True)
        for b in writes:
            add(b.w, False)
            for d in b.r.values():
                add(d, False)
        out = []
        for k, (sem, val) in deps.items():
            if seen.get(k, -1) >= val:
                continue
            out.append((sem, val))
        return out

    def _emit_waits(self, E, ins_fn, deps):
        for sem, val in deps[1:]:
            E.eng.wait_ge(sem, val)
            E.seen[id(sem)] = max(E.seen.get(id(sem), -1), val)
        ins = ins_fn()
        if deps:
            sem, val = deps[0]
            ins._wait_ge(sem, val)
            E.seen[id(sem)] = max(E.seen.get(id(sem), -1), val)
        return ins

    def op(self, E, fn, reads=(), writes=(), inc=True):
        self.n_ops += 1
        deps = self._deps(E, E.order_raw, E.seen, reads, writes)
        ins = self._emit_waits(E, fn, deps)
        val = E.count + 1
        if inc:
            ins.then_inc(E.sem, 1)
            E.count = val
        rec = (E, E.sem, val)
        for b in writes:
            b.w = rec
            b.r = {}
        for b in reads:
            b.r[id(E)] = rec
        if inc:
            E.rotate()
        return ins

    def dma(self, Q, out_ap, in_ap, reads=(), writes=(), **kw):
        self.n_ops += 1
        pool = self.dma_pools[id(Q)]
        slot = pool[0][pool[1]]
        pool[1] = (pool[1] + 1) % len(pool[0])
        if slot[1] >= SEM_ROT:
            Q.eng.wait_ge(slot[0], slot[1])
            slot[0] = self.new_sem("dmar")
            slot[1] = 0
        sem = slot[0]
        deps = self._deps(None, True, Q.seen, reads, writes)
        if slot[1] > 0 and Q.seen.get(id(sem), -1) < slot[1]:
            deps = [d for d in deps if d[0] is not sem]
            deps.append((sem, slot[1]))
        deps = [d for d in deps if Q.seen.get(id(d[0]), -1) < d[1]]
        ins = self._emit_waits(Q, lambda: Q.eng.dma_start(out=out_ap, in_=in_ap, **kw), deps)
        slot[1] += 16
        ins.then_inc(sem, 16)
        rec = (slot, sem, slot[1])
        for b in writes:
            b.w = rec
            b.r = {}
        for b in reads:
            b.r[id(slot)] = rec
        return ins

    def wait_all(self, E, bufs):
        deps = self._deps(None, True, E.seen, bufs, ())
        for sem, val in deps:
            E.eng.wait_ge(sem, val)
            E.seen[id(sem)] = val


    def barrier(self):
        engs = [self.pe, self.act, self.dve, self.pool, self.sp]
        items = [(e.sem, e.count) for e in engs if e.count > 0]
        for pl in self.dma_pools.values():
            items += [(s[0], s[1]) for s in pl[0] if s[1] > 0]
        items += [e.prev for e in engs if e.prev is not None]
        for E in engs:
            for sem, val in items:
                if sem is E.sem:
                    continue
                if E.seen.get(id(sem), -1) >= val:
                    continue
                E.eng.wait_ge(sem, val)
                E.seen[id(sem)] = val


S = 2048
D = 1024
NT = 16
DFF = 2816
NFC = 22
EPS = 1e-6
NEG = -30000.0
NIT = 16
EVEN_IN = 2524
IDX_SCALE = (64 ** -0.5) * (4 ** -0.5)
MOFF = [0, 4, 12, 24]

A_X = 0
A_HT = 65536
A_MIX = 106496
A_R1 = 122880
A_R3 = 139264
A_CKV = 176128
A_W = 180224
A_C = 196608
A_END = 208896


def _sz(dt):
    return 4 if dt in (F32, I32) else 2


class Prog:
    def __init__(self, nc, dr, dbg=None):
        self.nc = nc
        self.dr = dr
        self.fw = FW(nc)
        self.arena = nc.alloc_sbuf_tensor("arena", [128, A_END // 4], F32)
        fw = self.fw
        self.PE, self.ACT, self.DVE, self.POOL, self.SP = fw.pe, fw.act, fw.dve, fw.pool, fw.sp
        self.ps = {}
        for n in ("A0", "A1", "S0", "S1", "O0", "O1"):
            self.ps[n] = (nc.alloc_psum_tensor("p" + n, [128, 512], F32)[:, :], Buf("p" + n))
        for n in ("T0", "T1"):
            self.ps[n] = (nc.alloc_psum_tensor("p" + n, [128, 1024], BF16)[:, :], Buf("p" + n))
        self.rot = {}
        self.dbg = dbg if dbg is not None else {}
        self.dbg_out = []
        self._consts()

    def view(self, off, shape, dt):
        n = int(np.prod(shape))
        sz = _sz(dt)
        assert off % 4 == 0 and (n * sz) % 4 == 0, (off, shape)
        a = self.arena[:, off // 4:(off + n * sz) // 4]
        if dt != F32:
            a = a.bitcast(dt)
        if len(shape) == 2:
            a = a.rearrange("p (a b) -> p a b", b=shape[1])
        elif len(shape) == 3:
            a = a.rearrange("p (a b c) -> p a b c", b=shape[1], c=shape[2])
        elif len(shape) == 4:
            a = a.rearrange("p (a b c d) -> p a b c d", b=shape[1], c=shape[2], d=shape[3])
        return a

    def bank(self, kind):
        k = self.rot.get(kind, 0)
        self.rot[kind] = k ^ 1
        return self.ps[kind + str(k)]

    def op(self, E, meth, reads, writes, inc=True, **kw):
        return self.fw.op(E, lambda: getattr(E.eng, meth)(**kw), reads, writes, inc)

    def mm(self, out, lhsT, rhs, start, stop, reads, writes, inc=True, skip=False):
        kw = dict(out=out, lhsT=lhsT, rhs=rhs, start=start, stop=stop)
        if skip:
            kw["skip_group_check"] = True
        return self.fw.op(self.PE, lambda: self.nc.tensor.matmul(**kw), reads, writes, inc)

    def tr(self, out, in_, reads, writes, inc=True):
        return self.fw.op(self.PE, lambda: self.nc.tensor.transpose(out=out, in_=in_, identity=self.ident),
                          list(reads) + [self.b_const], writes, inc)

    def dma(self, Q, out, in_, reads=(), writes=(), **kw):
        return self.fw.dma(Q, out, in_, reads, writes, **kw)

    def _consts(self):
        nc, dr = self.nc, self.dr
        o = [A_C]

        def take(shape, dt):
            n = int(np.prod(shape)) * _sz(dt)
            n = (n + 31) // 32 * 32
            v = self.view(o[0], shape, dt)
            o[0] += n
            assert o[0] <= A_END, o[0]
            return v
        self.b_const = Buf("const")
        bc = [self.b_const]
        self.ident = take([128], BF16)
        self.triT = take([128], BF16)
        self.zeros = take([264], BF16)
        self.tri32 = take([128], F32)
        self.U32 = take([128], F32)
        self.ones32 = take([128], F32)
        self.anorm = take([4, 8], F32)
        self.fnorm = take([4, 8], F32)
        self.cos2 = take([16, 16], F32)
        self.sin2 = take([16, 16], F32)
        self.invf = take([16], F32)
        self.pow2 = take([NIT], F32)
        self.pos_i = take([16], I32)
        self.pos_f = take([16], F32)
        self.ang = self.view(A_R3, [16, 16], F32)
        self.angk = self.view(A_R3 + 1024, [16, 16], F32)
        self.angi = self.view(A_R3 + 2048, [16, 16], I32)
        self.ss = take([16], F32)
        self.rstd = take([16], F32)
        self.convw = take([4, 3, NFC], F32)
        self.convb = take([4, NFC], F32)
        self.bfox = take([2, 8], F32)
        self.gckv = take([2], F32)
        self.fb = take([16, 4, 8], F32)
        self.csb = take([16, 8], F32)
        self.cref = take([4, 8], F32)
        self.zf = self.view(A_R3, [16, 8], F32)
        self.zf2 = self.view(A_R3 + 512, [16, 8], F32)
        self.iw = take([16, 4], F32)
        self.st = take([64], F32)
        self.b_st = Buf("st")
        self.dkr = take([16, 16], BF16)
        self.b_dkr = Buf("dkr")
        self.bis = take([2, 32], F32)
        self.kmT32 = take([2, 8], F32)
        self.kmT = take([2, 8], BF16)
        self.b_km = Buf("km")
        self.b_fb = Buf("fb")
        self.ptk = 0
        SP, POOL = self.SP, self.POOL
        self.dma(POOL, self.ident, dr["c_ident"], writes=bc)
        self.dma(POOL, self.triT, dr["c_triT"], writes=bc)
        self.dma(SP, self.tri32, dr["c_tri32"], writes=bc)
        self.dma(SP, self.U32, dr["c_U32"], writes=bc)
        self.dma(SP, self.ones32, dr["c_ones32"], writes=bc)
        self.dma(SP, self.anorm, dr["attn_norm"].rearrange("l (c p) -> p l c", p=128), writes=bc,
                 allow_slow_non_contiguous=True)
        self.dma(SP, self.fnorm, dr["ffn_norm"].rearrange("l (c p) -> p l c", p=128), writes=bc,
                 allow_slow_non_contiguous=True)
        self.dma(SP, self.invf, dr["c_invf"].partition_broadcast(128), writes=bc)
        self.dma(SP, self.pow2, dr["c_pow2"].partition_broadcast(128), writes=bc)
        self.dma(SP, self.pos_i, dr["positions"].rearrange("(t p) -> p t", p=128), writes=bc,
                 allow_slow_non_contiguous=True)
        self.dma(SP, self.convw, dr["conv_w"].rearrange("l j (c p) -> p l j c", p=128), writes=bc,
                 allow_slow_non_contiguous=True)
        self.dma(SP, self.convb, dr["conv_b"].rearrange("l (c p) -> p l c", p=128), writes=bc,
                 allow_slow_non_contiguous=True)
        self.dma(SP, self.bfox.rearrange("p a b -> p (a b)"), dr["b_fox_f"].rearrange("a b -> (a b)").partition_broadcast(128), writes=bc)
        self.dma(SP, self.gckv, dr["g_ckv"].rearrange("l p -> p l"), writes=bc, allow_slow_non_contiguous=True)
        self.op(self.DVE, "memset", [], bc, ap=self.zeros, constant=0.0)
        self.x = self.view(A_X, [16, 1024], F32)
        self.bx = [Buf("x%d" % t) for t in range(16)]
        self.hT = self.view(A_HT, [8, 2048], BF16)
        self.b_hT = Buf("hT")
        self.mixT = self.view(A_MIX, [4, 2048], BF16)
        self.b_mixT = Buf("mixT")
        self.ckvT = self.view(A_CKV, [2048], BF16)
        self.b_ckvT = Buf("ckvT")
        self.wslots = [(A_W + 4096 * i, Buf("w%d" % i)) for i in range(4)]
        self.wnext = 0

    def wslot(self, shape=(8, 256)):
        off, b = self.wslots[self.wnext]
        self.wnext = (self.wnext + 1) % 4
        return self.view(off, list(shape), BF16), b

    def load_x(self):
        xr = self.dr["x"].rearrange("(t p) d -> p t d", p=128)
        for t in range(16):
            self.dma(self.SP, self.x[:, t, :], xr[:, t, :], writes=[self.bx[t]])

    def rope_tables(self):
        V, A = self.DVE, self.ACT
        bc = [self.b_const]
        TWO_PI = 2.0 * np.pi
        C1 = 6.28125
        C2 = TWO_PI - C1
        self.op(V, "tensor_copy", bc, bc, out=self.pos_f, in_=self.pos_i)
        for which, dst in ((0, self.sin2), (1, self.cos2)):
            self.op(V, "tensor_tensor", bc, bc, out=self.ang,
                    in0=self.pos_f.unsqueeze(2).to_broadcast([128, 16, 16]),
                    in1=self.invf.unsqueeze(1).to_broadcast([128, 16, 16]), op=ALU.mult)
            if which == 1:
                self.op(V, "tensor_scalar", bc, bc, out=self.ang, in0=self.ang, scalar1=float(np.pi / 2), scalar2=None, op0=ALU.add)
            self.op(V, "tensor_scalar", bc, bc, out=self.angk, in0=self.ang, scalar1=float(1.0 / TWO_PI), scalar2=None, op0=ALU.mult)
            self.op(V, "tensor_copy", bc, bc, out=self.angi, in_=self.angk)
            self.op(V, "tensor_copy", bc, bc, out=self.angk, in_=self.angi)
            self.op(V, "scalar_tensor_tensor", bc, bc, out=self.ang, in0=self.angk, scalar=-C1, in1=self.ang, op0=ALU.mult, op1=ALU.add)
            self.op(V, "scalar_tensor_tensor", bc, bc, out=self.ang, in0=self.angk, scalar=-C2, in1=self.ang, op0=ALU.mult, op1=ALU.add)
            self.op(V, "tensor_scalar", bc, bc, out=self.ang, in0=self.ang, scalar1=3.1415925, scalar2=-3.1415925, op0=ALU.min, op1=ALU.max)
            self.op(A, "activation", bc, bc, out=dst, in_=self.ang, func=AF.Sin)

    def norm_to_hT(self, gain):
        A, V, G = self.ACT, self.DVE, self.POOL
        xh = [(self.view(A_R3 + 2048 * i, [1024], BF16), Buf("xh%d" % i)) for i in range(2)]
        junk = (self.view(A_R3 + 4096, [1024], BF16), Buf("junk"))
        bst = self.b_st
        for t in range(16):
            self.op(A, "activation", [self.bx[t]], [junk[1], bst], out=junk[0], in_=self.x[:, t, :], func=AF.Square,
                    accum_out=self.ss[:, t:t + 1])
        self.op(A, "activation", [bst], [bst], out=self.rstd, in_=self.ss, func=AF.Sqrt, scale=1.0 / D, bias=EPS)
        self.op(V, "reciprocal", [bst], [bst], out=self.rstd, in_=self.rstd)
        for t in range(16):
            xa, xb = xh[t % 2]
            self.op(A, "activation", [self.bx[t], bst], [xb], out=xa, in_=self.x[:, t, :], func=AF.Copy, scale=self.rstd[:, t:t + 1])
            pT, bT = self.bank("T")
            for c in range(8):
                self.tr(pT[:, c * 128:(c + 1) * 128], xa[:, c * 128:(c + 1) * 128], [xb], [bT], inc=(c == 7))
            self.op(V, "tensor_tensor", [bT, self.b_const], [self.b_hT], out=self.hT[:, :, t * 128:(t + 1) * 128],
                    in0=pT.rearrange("p (c k) -> p c k", k=128), in1=gain.unsqueeze(2).to_broadcast([128, 8, 128]), op=ALU.mult)

    def ffn(self, layer):
        A, V, G = self.ACT, self.DVE, self.POOL
        dr = self.dr
        bc = self.b_const
        self.norm_to_hT(self.fnorm[:, layer, :])
        self.fw.barrier()
        base = A_MIX
        actT = [(self.view(base + 24576 * i, [6, 2048], BF16), Buf("actT%d" % i)) for i in range(2)]
        gsb = [(self.view(base + 49152 + 8224 * i, [2056], F32), [Buf("gsb%d_%d" % (i, k)) for k in range(4)]) for i in range(2)]
        sp = A_HT + 32768
        acc = [(self.view(sp + 2048 * i, [512], F32), Buf("acc%d" % i)) for i in range(2)]
        sg = [(self.view(sp + 4096 + 2048 * i, [512], F32), Buf("sg%d" % i)) for i in range(2)]
        for gs, bgs in gsb:
            self.op(V, "memset", [], [bgs[0]], ap=gs[:, 0:2], constant=0.0)
        wup = dr["w_up"][layer]
        wdn = dr["w_down"][layer]
        groups = [(0, 6), (6, 12), (12, 18), (18, 22)]
        k = 0
        for g, (f0, f1) in enumerate(groups):
            aT, baT = actT[g % 2]
            for fc in range(f0, f1):
                ws, bws = self.wslot()
                self.dma(G, ws[:, :, 0:128], wup[:, fc * 128:(fc + 1) * 128].rearrange("(c p) n -> p c n", p=128), writes=[bws])
                self.dma(G, ws[:, :, 128:256], wup[:, DFF + fc * 128:DFF + (fc + 1) * 128].rearrange("(c p) n -> p c n", p=128), writes=[bws])
                gs, bgs = gsb[fc % 2]
                for tc in range(4):
                    tsl = slice(tc * 512, (tc + 1) * 512)
                    pG, bG = self.bank("A")
                    pU, bU = self.bank("S")
                    for c in range(8):
                        self.mm(pG, ws[:, c, 0:128], self.hT[:, c, tsl], c == 0, c == 7, [bws, self.b_hT], [bG], inc=(c == 7))
                    for c in range(8):
                        self.mm(pU, ws[:, c, 128:256], self.hT[:, c, tsl], c == 0, c == 7, [bws, self.b_hT], [bU], inc=(c == 7))
                    self.op(A, "activation", [bG], [bgs[tc]], out=gs[:, 2 + tc * 512:2 + (tc + 1) * 512], in_=pG, func=AF.Copy)
                    ac, bac = acc[k % 2]
                    sgt, bsg = sg[k % 2]
                    k += 1
                    rd = [bgs[tc], bc] + ([bgs[tc - 1]] if tc > 0 else [bgs[0]])
                    self.op(V, "tensor_scalar", rd, [bac], out=ac, in0=gs[:, 2 + tc * 512:2 + (tc + 1) * 512],
                            scalar1=self.convw[:, layer, 2, fc:fc + 1], scalar2=self.convb[:, layer, fc:fc + 1], op0=ALU.mult, op1=ALU.add)
                    self.op(V, "scalar_tensor_tensor", rd + [bac], [bac], out=ac, in0=gs[:, 1 + tc * 512:1 + (tc + 1) * 512],
                            scalar=self.convw[:, layer, 1, fc:fc + 1], in1=ac, op0=ALU.mult, op1=ALU.add)
                    self.op(V, "scalar_tensor_tensor", rd + [bac], [bac], out=ac, in0=gs[:, tc * 512:(tc + 1) * 512],
                            scalar=self.convw[:, layer, 0, fc:fc + 1], in1=ac, op0=ALU.mult, op1=ALU.add)
                    self.op(A, "activation", [bac], [bsg], out=sgt, in_=ac, func=AF.Silu)
                    self.op(V, "tensor_tensor", [bsg, bU], [baT], out=aT[:, fc - f0, tsl], in0=sgt, in1=pU, op=ALU.mult)
            nf = f1 - f0
            wd = []
            for s in range((nf + 1) // 2):
                ws, bws = self.wslot((2, 1024))
                r0 = (f0 + 2 * s) * 128
                self.dma(G, ws, wdn[r0:r0 + 256, :].rearrange("(c p) n -> p c n", p=128), writes=[bws])
                wd.append((ws, bws))
            for t in range(16):
                for nh in range(2):
                    pO, bO = self.bank("O")
                    nsl = slice(nh * 512, (nh + 1) * 512)
                    for fl in range(nf):
                        ws, bws = wd[fl // 2]
                        self.mm(pO, aT[:, fl, t * 128:(t + 1) * 128], ws[:, fl % 2, nsl], fl == 0, fl == nf - 1, [baT, bws], [bO],
                                inc=(fl == nf - 1))
                    self.op(V, "tensor_tensor", [bO, self.bx[t]], [self.bx[t]], out=self.x[:, t, nsl], in0=self.x[:, t, nsl], in1=pO, op=ALU.add)
        self.fw.barrier()

    def attend(self, qT, kT, rows, rd_qk, v, rd_v, stage, bstage, scol, PT, mode, bias=None, mask=None, qbias=None):
        A, V = self.ACT, self.DVE
        for qc in range(4):
            pO, bO = self.bank("O")
            self.mm(pO[:, 0:264], self.zeros[:, 0:128], self.zeros[:, 0:264], True, True, [self.b_const], [bO])
            pend = None

            def pv(args):
                j, q0, pt, bpt = args
                for t in range(q0 // 128, 4 * qc + 4):
                    c0 = (t - 4 * qc) * 66
                    self.mm(pO[:, c0:c0 + 65], pt[:, t * 128 - q0:t * 128 - q0 + 128], v[:, j, 0:65], False, (j == t),
                            [bpt] + rd_v, [bO], inc=(t == 4 * qc + 3), skip=True)
            for j in range(4 * qc + 4):
                q0 = max(512 * qc, 128 * j)
                N = 512 * qc + 512 - q0
                pS, bS = self.bank("S")
                diag = (mode != "dsa") and (j >= 4 * qc)
                ksl = slice(j * 128, (j + 1) * 128)
                groups = [(0, 128, True), (128, N - 128, False)] if diag else [(0, N, False)]
                groups = [g for g in groups if g[1] > 0]
                for gi, (c0, n, tri) in enumerate(groups):
                    terms = []
                    if tri:
                        terms.append((self.ident, self.triT, [self.b_const]))
                    if mode == "dsa":
                        m_ap, m_b = mask(qc, j, q0 - 512 * qc + c0, n)
                        terms.append((self.ident, m_ap, [self.b_const, m_b]))
                    if qbias is not None:
                        l_ap, r_ap, r_b = qbias(q0 + c0, n)
                        terms.append((l_ap, r_ap, r_b))
                    lastg = (gi == len(groups) - 1)
                    self.mm(pS[:, c0:c0 + n], kT[rows, ksl], qT[rows, q0 + c0:q0 + c0 + n], True, not terms, rd_qk, [bS],
                            inc=(lastg and not terms))
                    for ti, (l_ap, r_ap, r_b) in enumerate(terms):
                        lt = (ti == len(terms) - 1)
                        self.mm(pS[:, c0:c0 + n], l_ap, r_ap, False, lt, r_b, [bS], inc=(lastg and lt))
                pt, bpt = PT[self.ptk % len(PT)]
                self.ptk += 1
                kw = dict(out=pt[:, 0:N], in_=pS[:, 0:N], func=AF.Exp, scale=0.125)
                rds = [bS]
                if bias is not None:
                    kw["bias"] = bias(j, qc)
                    rds.append(self.b_fb)
                self.op(A, "activation", rds, [bpt], **kw)
                if pend is not None:
                    pv(pend)
                pend = (j, q0, pt, bpt)
            pv(pend)
            for t in range(4 * qc, 4 * qc + 4):
                c0 = (t - 4 * qc) * 66
                ri = self.st[:, (t % 8):(t % 8) + 1]
                self.op(V, "reciprocal", [bO], [self.b_st], out=ri, in_=pO[:, c0 + 64:c0 + 65])
                self.op(V, "tensor_scalar", [bO, self.b_st], [bstage], out=stage[:, t, scol:scol + 64], in0=pO[:, c0:c0 + 64],
                        scalar1=ri, scalar2=None, op0=ALU.mult)

    def rope_inplace(self, E, data, tmp, t0, nt, nh, rd, wr, btmp):
        cs = self.cos2[:, t0:t0 + nt, :].unsqueeze(2).to_broadcast([128, nt, nh, 16])
        sn = self.sin2[:, t0:t0 + nt, :].unsqueeze(2).to_broadcast([128, nt, nh, 16])
        bc = self.b_const
        self.op(E, "tensor_tensor", rd + [bc], [btmp], out=tmp, in0=data, in1=sn, op=ALU.mult)
        self.op(E, "tensor_tensor", rd + [bc], wr, out=data, in0=data, in1=cs, op=ALU.mult)
        self.op(E, "tensor_tensor", rd + [btmp], wr, out=data[:, :, :, 0:8], in0=data[:, :, :, 0:8], in1=tmp[:, :, :, 8:16], op=ALU.subtract)
        self.op(E, "tensor_tensor", rd + [btmp], wr, out=data[:, :, :, 8:16], in0=data[:, :, :, 8:16], in1=tmp[:, :, :, 0:8], op=ALU.add)

    def outproj_half(self, w, half):
        V, G = self.DVE, self.POOL
        for ng in range(2):
            ws, bws = self.wslot((4, 512))
            self.dma(G, ws, w[half * 512:(half + 1) * 512, ng * 512:(ng + 1) * 512].rearrange("(c p) n -> p c n", p=128), writes=[bws])
            nsl = slice(ng * 512, (ng + 1) * 512)
            for t in range(16):
                pA, bA = self.bank("A")
                for c in range(4):
                    self.mm(pA, self.mixT[:, c, t * 128:(t + 1) * 128], ws[:, c, :], c == 0, c == 3, [self.b_mixT, bws], [bA], inc=(c == 3))
                self.op(V, "tensor_tensor", [bA, self.bx[t]], [self.bx[t]], out=self.x[:, t, nsl], in0=self.x[:, t, nsl], in1=pA, op=ALU.add)

    def mixed_to_T(self, mst, bmst, slot):
        for half in range(2):
            pT, bT = self.bank("T")
            for tt in range(8):
                self.tr(pT[:, tt * 128:(tt + 1) * 128], mst[:, half * 8 + tt, :], [bmst], [bT], inc=(tt == 7))
            self.op(self.ACT, "activation", [bT], [self.b_mixT], out=self.mixT[:, slot, half * 1024:(half + 1) * 1024], in_=pT, func=AF.Copy)

    def odd(self, i):
        A, V, G = self.ACT, self.DVE, self.POOL
        layer = 2 * i + 1
        dr = self.dr
        bc = self.b_const
        win = dr["w_in_odd"][i]
        self.norm_to_hT(self.anorm[:, layer, :])
        self.fw.barrier()
        qT = [(self.view(A_R1 + 4096 * k, [2048], BF16), Buf("qT%d" % k)) for k in range(4)]
        kT = [(self.view(A_R3 + 4096 * k, [2048], BF16), Buf("kT%d" % k)) for k in range(4)]
        o = A_R3 + 16384
        vaug = [(self.view(o + 4224 * k, [16, 2, 66], BF16), Buf("v%d" % k)) for k in range(2)]
        o += 8448
        stgq, bsq = self.view(o, [16, 2, 72], BF16), Buf("stgq")
        o += 4608
        stgk, bsk = self.view(o, [16, 2, 72], BF16), Buf("stgk")
        o += 4608
        s32, bs32 = self.view(o, [512], F32), Buf("s32")
        o += 2048
        assert o <= A_R3 + 36864
        sp = A_HT + 32768
        PT = [(self.view(sp + 1024 * k, [512], BF16), Buf("PT%d" % k)) for k in range(4)]
        mst, bmst = self.view(sp + 4096, [16, 128], BF16), Buf("mst")
        tmp, btmp = self.view(A_CKV, [4, 2, 16], F32), Buf("ropetmp")
        gt, bgt = self.view(A_CKV + 1024, [8, 2, 8], F32), Buf("gt")
        rank, brk = self.view(A_CKV + 2048, [8, 2, 8], F32), Buf("rank")
        cmp_, bcmp = self.view(A_CKV + 3072, [8, 2, 8], F32), Buf("cmp")
        for va, bva in vaug:
            self.op(G, "memset", [], [bva], ap=va[:, :, :, 64:66], constant=1.0)
        self.op(G, "memset", [], [bsq], ap=stgq[:, :, :, 64:72], constant=0.0)
        self.op(G, "memset", [], [bsk], ap=stgk[:, :, :, 64:72], constant=0.0)
        for b in range(8):
            self.op(G, "memset", [], [bsk], ap=stgk[:, 2 * b:2 * b + 2, :, 64 + b:65 + b], constant=1.0)
        for qa, bqa in qT:
            self.op(G, "memset", [], [bqa], ap=qa[64:128, :], constant=0.0)
        for ka_, bka_ in kT:
            self.op(G, "memset", [], [bka_], ap=ka_[64:128, :], constant=0.0)
        def load_pair(hp):
            wqk, bwqk = self.wslot()
            wv, bwv = self.wslot()
            self.dma(G, wqk[:, :, 0:128], win[:, hp * 128:(hp + 1) * 128].rearrange("(c p) n -> p c n", p=128), writes=[bwqk])
            self.dma(G, wqk[:, :, 128:256], win[:, 1024 + hp * 128:1024 + (hp + 1) * 128].rearrange("(c p) n -> p c n", p=128), writes=[bwqk])
            self.dma(G, wv[:, :, 0:128], win[:, 2048 + hp * 128:2048 + (hp + 1) * 128].rearrange("(c p) n -> p c n", p=128), writes=[bwv])
            return wqk, bwqk, wv, bwv
        wts = {0: load_pair(0)}

        def prep1(hp):
            st_ = hp % 2
            wqk, bwqk, wv, bwv = wts[hp]
            va, bva = vaug[st_]
            for tb in range(4):
                for which, stg, bstg in ((0, stgq, bsq), (1, stgk, bsk)):
                    pA, bA = self.bank("A")
                    for tt in range(4):
                        t = 4 * tb + tt
                        for c in range(8):
                            self.mm(pA[:, tt * 128:(tt + 1) * 128], self.hT[:, c, t * 128:(t + 1) * 128], wqk[:, c, which * 128:(which + 1) * 128],
                                    c == 0, c == 7, [self.b_hT, bwqk], [bA], inc=(tt == 3 and c == 7))
                    self.op(A, "activation", [bA], [bs32], out=s32, in_=pA, func=AF.Copy)
                    s4 = s32.rearrange("p (a h d) -> p a h d", h=2, d=64)
                    self.rope_inplace(V, s4[:, :, :, 0:16], tmp, 4 * tb, 4, 2, [bs32], [bs32], btmp)
                    self.op(G, "tensor_copy", [bs32], [bstg], out=stg[:, 4 * tb:4 * tb + 4, :, 0:64], in_=s4)
                pA, bA = self.bank("A")
                for tt in range(4):
                    t = 4 * tb + tt
                    for c in range(8):
                        self.mm(pA[:, tt * 128:(tt + 1) * 128], self.hT[:, c, t * 128:(t + 1) * 128], wv[:, c, 0:128],
                                c == 0, c == 7, [self.b_hT, bwv], [bA], inc=(tt == 3 and c == 7))
                self.op(A, "activation", [bA], [bva], out=va[:, 4 * tb:4 * tb + 4, :, 0:64],
                        in_=pA.rearrange("p (a h d) -> p a h d", h=2, d=64), func=AF.Copy)
            if hp < 7:
                wts[hp + 1] = load_pair(hp + 1)
            for hh in range(2):
                qa, bqa = qT[2 * st_ + hh]
                ka, bka = kT[2 * st_ + hh]
                for half in range(2):
                    pT, bT = self.bank("T")
                    for tt in range(8):
                        self.tr(pT[0:64, tt * 128:(tt + 1) * 128], stgq[:, half * 8 + tt, hh, 0:64], [bsq], [bT], inc=(tt == 7))
                    self.op(A, "activation", [bT], [bqa], out=qa[0:64, half * 1024:(half + 1) * 1024], in_=pT[0:64, :], func=AF.Copy)
                    pT, bT = self.bank("T")
                    for tt in range(8):
                        self.tr(pT[0:72, tt * 128:(tt + 1) * 128], stgk[:, half * 8 + tt, hh, 0:72], [bsk], [bT], inc=(tt == 7))
                    self.op(V, "tensor_copy", [bT], [bka], out=ka[0:72, half * 1024:(half + 1) * 1024], in_=pT[0:72, :])
                self.op(V, "tensor_reduce", [bka], [self.b_km], out=self.kmT32[0:64, hh, :],
                        in_=ka[0:64, :].rearrange("p (n k) -> p n k", k=256), axis=AX.X, op=ALU.add)
            self.op(V, "tensor_scalar", [self.b_km], [self.b_km], out=self.kmT[0:64, :, :], in0=self.kmT32[0:64, :, :], scalar1=1.0 / 256, scalar2=None, op0=ALU.mult)
            pA, bA = self.bank("A")
            for t in range(8, 16):
                for hh in range(2):
                    qa, bqa = qT[2 * st_ + hh]
                    c0 = ((t - 8) * 2 + hh) * 8
                    self.mm(pA[:, c0:c0 + 8], qa[0:64, t * 128:(t + 1) * 128], self.kmT[0:64, hh, :], True, True, [bqa, self.b_km], [bA],
                            inc=(t == 15 and hh == 1))
            self.op(V, "tensor_copy", [bA], [bgt], out=gt, in_=pA[:, 0:128].rearrange("p (a h n) -> p a h n", h=2, n=8))
            for n1 in range(7):
                a0 = max(0, n1 - 3) * 2 if n1 >= 4 else 0
                na = 8 - a0
                dst = rank if n1 == 0 else cmp_
                bd = brk if n1 == 0 else bcmp
                self.op(V, "tensor_tensor", [bgt], [bd], out=dst[:, a0:8, :, :], in0=gt[:, a0:8, :, n1:n1 + 1].to_broadcast([128, na, 2, 8]),
                        in1=gt[:, a0:8, :, :], op=ALU.is_gt)
                if n1 > 0:
                    self.op(V, "tensor_tensor", [brk, bcmp], [brk], out=rank[:, a0:8, :, :], in0=rank[:, a0:8, :, :], in1=cmp_[:, a0:8, :, :], op=ALU.add)
            for t in range(8, 16):
                ob = t // 2
                self.op(V, "tensor_scalar", [brk], [bsq], out=stgq[:, t, :, 64:64 + ob], in0=rank[:, t - 8, :, 0:ob], scalar1=2.5, scalar2=NEG,
                        op0=ALU.is_gt, op1=ALU.mult)

        def prep2(hp):
            st_ = hp % 2
            for hh in range(2):
                qa, bqa = qT[2 * st_ + hh]
                pT, bT = self.bank("T")
                for tt in range(8):
                    self.tr(pT[0:72, tt * 128:(tt + 1) * 128], stgq[:, 8 + tt, hh, 0:72], [bsq], [bT], inc=(tt == 7))
                self.op(A, "activation", [bT], [bqa], out=qa[64:72, 1024:2048], in_=pT[64:72, :], func=AF.Copy)

        def att(hp):
            st_ = hp % 2
            va, bva = vaug[st_]
            for hh in range(2):
                qa, bqa = qT[2 * st_ + hh]
                ka, bka = kT[2 * st_ + hh]
                self.attend(qa, ka, slice(0, 128), [bqa, bka], va[:, :, hh, :], [bva], mst, bmst, hh * 64, PT, "tri")
            self.mixed_to_T(mst, bmst, hp % 4)
            if hp % 4 == 3:
                self.outproj_half(dr["w_out_odd"][i], hp // 4)

        prep1(0)
        prep2(0)
        for hp in range(8):
            if hp < 7:
                prep1(hp + 1)
            att(hp)
            if hp < 7:
                prep2(hp + 1)
        self.fw.barrier()

    def even(self, i):
        A, V, G = self.ACT, self.DVE, self.POOL
        layer = 2 * i
        dr = self.dr
        bc = self.b_const
        win = dr["w_in_even"][i]
        wout = dr["w_out_even"][i]

        def wcols(c0, n):
            return win[:, c0:c0 + n].rearrange("(c p) n -> p c n", p=128)
        self.norm_to_hT(self.anorm[:, layer, :])
        self.fw.barrier()
        kpad = [(self.view(A_R1 + 4096 * k, [2048], BF16), Buf("fk%d" % k)) for k in range(4)]
        for k_, (ka_, bka_) in enumerate(kpad):
            zr = slice(64, 128) if k_ % 2 == 0 else slice(0, 64)
            self.op(G, "memset", [], [bka_], ap=ka_[zr, :], constant=0.0)
        o = A_R3
        fq = [(self.view(o + 4096 * k, [2048], BF16), Buf("fq%d" % k)) for k in range(2)]
        o += 8192
        vaug = [(self.view(o + 4224 * k, [16, 2, 66], BF16), Buf("v%d" % k)) for k in range(2)]
        o += 8448
        PT = [(self.view(o + 1024 * k, [512], BF16), Buf("PT%d" % k)) for k in range(4)]
        o += 4096
        mst, bmst = self.view(o, [16, 128], BF16), Buf("mst")
        o += 4096
        zf, zf2, bzf = self.view(o, [16, 8], F32), self.view(o + 512, [16, 8], F32), Buf("zf")
        o += 1024
        for va, bva in vaug:
            self.op(G, "memset", [], [bva], ap=va[:, :, :, 64:66], constant=1.0)
        wff, bwff = self.wslot()
        self.dma(G, wff[:, :, 0:128], wcols(1536, 128), writes=[bwff])
        pA, bA = self.bank("A")
        for t in range(16):
            for c in range(8):
                self.mm(pA[:, t * 8:(t + 1) * 8], self.hT[:, c, t * 128:(t + 1) * 128], wff[:, c, 0:8], c == 0, c == 7, [self.b_hT, bwff], [bA],
                        inc=(t == 15 and c == 7))
        self.op(V, "tensor_tensor", [bA, bc], [bzf], out=zf, in0=pA[:, 0:128].rearrange("p (t h) -> p t h", h=8),
                in1=self.bfox[:, i, :].unsqueeze(1).to_broadcast([128, 16, 8]), op=ALU.add)
        self.op(V, "tensor_scalar", [bzf], [bzf], out=zf2, in0=zf, scalar1=-1.0, scalar2=None, op0=ALU.mult)
        self.op(V, "tensor_tensor", [bzf], [bzf], out=zf2, in0=zf2, in1=zf, op=ALU.min)
        self.op(A, "activation", [bzf], [bzf], out=zf2, in_=zf2, func=AF.Exp)
        self.op(A, "activation", [bzf], [bzf], out=zf2, in_=zf2, func=AF.Ln, bias=1.0)
        self.op(V, "tensor_scalar", [bzf], [bzf], out=zf, in0=zf, scalar1=0.0, scalar2=None, op0=ALU.min)
        self.op(V, "tensor_tensor", [bzf], [bzf], out=zf, in0=zf, in1=zf2, op=ALU.subtract)
        pC, bC = self.bank("A")
        for t in range(16):
            for j in range(t):
                self.mm(pC[:, t * 8:(t + 1) * 8], self.ones32, zf[:, j, :], j == 0, False, [bzf, bc], [bC], inc=False)
            self.mm(pC[:, t * 8:(t + 1) * 8], self.U32, zf[:, t, :], t == 0, True, [bzf, bc], [bC], inc=False)
        for qc in range(4):
            for j in range(4 * qc + 4):
                self.mm(pC[:, 128 + qc * 8:128 + (qc + 1) * 8], self.ones32, zf[:, j, :], j == 0, j == 4 * qc + 3, [bzf, bc], [bC],
                        inc=(qc == 3 and j == 15))
        self.op(V, "tensor_copy", [bC], [self.b_fb], out=self.csb, in_=pC[:, 0:128].rearrange("p (t h) -> p t h", h=8))
        self.op(V, "tensor_copy", [bC], [self.b_fb], out=self.cref, in_=pC[:, 128:160].rearrange("p (t h) -> p t h", h=8))
        self.op(V, "tensor_tensor", [self.b_fb], [self.b_fb], out=self.fb, in0=self.cref.unsqueeze(1).to_broadcast([128, 16, 4, 8]),
                in1=self.csb.unsqueeze(2).to_broadcast([128, 16, 4, 8]), op=ALU.subtract)
        if STOP == "gate":
            self.fw.barrier()
            return
        rtm, brtm = self.view(o, [16, 8], BF16), Buf("rtm")
        o += 256
        selT, bsel = self.view(o, [8, 128], BF16), Buf("selT")
        o += 2048
        rT, brT = self.view(o, [2048], BF16), Buf("rT")
        o += 4096
        self.dma(G, selT, dr["c_sel"].rearrange("p (h k) -> p h k", k=128), writes=[bsel])
        self.op(G, "memset", [], [brT], ap=rT, constant=0.0)
        self.op(V, "tensor_tensor", [self.b_fb], [brtm], out=rtm.rearrange("p (q t) h -> p q t h", t=4),
                in0=self.csb.rearrange("p (q t) h -> p q t h", t=4),
                in1=self.cref.unsqueeze(2).to_broadcast([128, 4, 4, 8]), op=ALU.subtract)
        self.op(V, "tensor_scalar", [brtm], [brtm], out=rtm, in0=rtm, scalar1=8.0, scalar2=None, op0=ALU.mult)
        for qc in range(4):
            pR, bR = self.bank("A")
            for tt in range(4):
                self.mm(pR[0:8, tt * 128:(tt + 1) * 128], rtm[:, 4 * qc + tt, :], self.ident, True, True, [brtm, bc], [bR], inc=(tt == 3))
            self.op(V, "tensor_copy", [bR], [brT], out=rT[0:8, qc * 512:(qc + 1) * 512], in_=pR[0:8, :])
        def load_fox(hp):
            wqk, bwqk = self.wslot()
            wv, bwv = self.wslot()
            self.dma(G, wqk[:, :, 0:128], wcols(hp * 128, 128), writes=[bwqk])
            self.dma(G, wqk[:, :, 128:256], wcols(512 + hp * 128, 128), writes=[bwqk])
            self.dma(G, wv[:, :, 0:128], wcols(1024 + hp * 128, 128), writes=[bwv])
            return wqk, bwqk, wv, bwv
        nxt = load_fox(0)
        for hp in range(4):
            st_ = hp % 2
            qa, bqa = fq[st_]
            kp = [kpad[2 * st_], kpad[2 * st_ + 1]]
            va, bva = vaug[st_]
            wqk, bwqk, wv, bwv = nxt
            for which in (0, 1):
                for tc in range(4):
                    pA, bA = self.bank("A")
                    tsl = slice(tc * 512, (tc + 1) * 512)
                    for c in range(8):
                        self.mm(pA, wqk[:, c, which * 128:(which + 1) * 128], self.hT[:, c, tsl], c == 0, c == 7,
                                [self.b_hT, bwqk], [bA], inc=(c == 7))
                    if which == 0:
                        self.op(A, "activation", [bA], [bqa], out=qa[:, tsl], in_=pA, func=AF.Copy)
                    else:
                        self.op(V, "tensor_copy", [bA], [kp[0][1]], out=kp[0][0][0:64, tsl], in_=pA[0:64, :])
                        self.op(V, "tensor_copy", [bA], [kp[1][1]], out=kp[1][0][64:128, tsl], in_=pA[64:128, :])
            for tb in range(4):
                pA, bA = self.bank("A")
                for tt in range(4):
                    t = 4 * tb + tt
                    for c in range(8):
                        self.mm(pA[:, tt * 128:(tt + 1) * 128], self.hT[:, c, t * 128:(t + 1) * 128], wv[:, c, 0:128],
                                c == 0, c == 7, [self.b_hT, bwv], [bA], inc=(tt == 3 and c == 7))
                self.op(A, "activation", [bA], [bva], out=va[:, 4 * tb:4 * tb + 4, :, 0:64],
                        in_=pA.rearrange("p (a h d) -> p a h d", h=2, d=64), func=AF.Copy)
            if hp < 3:
                nxt = load_fox(hp + 1)
            for hh in range(2):
                h = 2 * hp + hh
                self.attend(qa, kp[hh][0], slice(0, 128), [bqa, kp[hh][1]], va[:, :, hh, :], [bva], mst, bmst, hh * 64, PT, "fox",
                            bias=lambda j, qc, h=h: self.fb[:, j, qc, h:h + 1],
                            qbias=lambda c0, n, h=h: (selT[:, h, :], rT[:, c0:c0 + n], [bsel, brT]))
            self.mixed_to_T(mst, bmst, hp)
        self.outproj_half(wout, 0)
        self.fw.barrier()
        if STOP == "fox":
            return
        dqT = self.view(A_R1, [4, 2048], BF16)
        b_dqT = Buf("dqT")
        iqT, b_iqT = self.view((A_MIX + 8192) if os.environ.get("IQ_MIX") else (A_R3 + 24576), [2, 2048], BF16), Buf("iqT")
        ikT2, b_ikT = self.view(A_R3 + 32768, [2048], BF16), Buf("ikT2")
        s32s = [(self.view(A_R3 + 2048 * k, [512], F32), Buf("s32_%d" % k)) for k in range(2)]
        stgs = [(self.view(A_R3 + 4096 + 1088 * k, [528], BF16), Buf("stg%d" % k)) for k in range(2)]
        tmps = [(self.view(A_R3 + 6272 + 1024 * k, [4, 4, 16], F32), Buf("ropetmp%d" % k)) for k in range(2)]
        wA, bwA = self.wslot()
        wB, bwB = self.wslot()
        self.dma(G, wA, wcols(2056, 256), writes=[bwA])
        self.dma(G, wB[:, :, 0:212], wcols(2312, 212), writes=[bwB])
        bst = self.b_st
        for t in range(16):
            tsl = slice(t * 128, (t + 1) * 128)
            s32, bs32 = s32s[t % 2]
            stg, bstg = stgs[t % 2]
            tmp, btmp = tmps[t % 2]
            pA, bA = self.bank("A")
            for c in range(8):
                self.mm(pA[:, 0:256], self.hT[:, c, tsl], wA[:, c, :], c == 0, c == 7, [self.b_hT, bwA], [bA], inc=False)
            for c in range(8):
                self.mm(pA[:, 256:468], self.hT[:, c, tsl], wB[:, c, 0:212], c == 0, c == 7, [self.b_hT, bwB], [bA], inc=(c == 7))
            self.op(A, "activation", [bA], [bs32], out=s32[:, 0:468], in_=pA[:, 0:468], func=AF.Copy)
            if STOP == "A0":
                continue
            self.op(A, "activation", [bs32], [bstg, bst], out=stg[:, 0:128], in_=s32[:, 0:128], func=AF.Square, accum_out=self.st[:, 8:9])
            self.op(A, "activation", [bst], [bst], out=self.st[:, 9:10], in_=self.st[:, 8:9], func=AF.Sqrt, scale=1.0 / 128, bias=EPS)
            self.op(V, "reciprocal", [bst], [bst], out=self.st[:, 10:11], in_=self.st[:, 9:10])
            self.op(V, "tensor_scalar", [bs32, bst], [bstg], out=stg[:, 0:128], in0=s32[:, 0:128], scalar1=self.st[:, 10:11], scalar2=None, op0=ALU.mult)
            self.rope_inplace(V, s32[:, 128:144].rearrange("p (a h d) -> p a h d", a=1, h=1), tmp[:, 0:1, 0:1, :], t, 1, 1, [bs32], [bs32], btmp)
            t5 = tmp.rearrange("p a h d -> p (a h d)")[:, 0:80].rearrange("p (a h d) -> p a h d", a=1, h=5)
            self.rope_inplace(V, s32[:, 144:464].rearrange("p (a h d) -> p a h d", a=1, h=5)[:, :, :, 0:16], t5, t, 1, 5, [bs32], [bs32], btmp)
            if STOP == "A0b":
                continue
            self.op(G, "tensor_copy", [bs32], [bstg], out=stg[:, 128:448], in_=s32[:, 144:464])
            self.op(G, "tensor_copy", [bs32], [bstg], out=stg[:, 448:512], in_=s32[:, 400:464])
            self.op(G, "tensor_copy", [bs32], [self.b_dkr], out=self.dkr[:, t, :], in_=s32[:, 128:144])
            self.op(G, "tensor_scalar", [bs32], [self.b_dkr], out=self.iw[:, t, :], in0=s32[:, 464:468], scalar1=IDX_SCALE, scalar2=None, op0=ALU.mult)
            if STOP == "A0c":
                continue
            pT, bT = self.bank("T")
            self.tr(pT[:, 0:128], stg[:, 0:128], [bstg], [bT], inc=False)
            self.tr(pT[:, 128:256], stg[:, 384:512], [bstg], [bT], inc=True)
            self.op(A, "activation", [bT, bc], [self.b_ckvT], out=self.ckvT[:, tsl], in_=pT[:, 0:128], func=AF.Identity, scale=self.gckv[:, i:i + 1])
            self.op(A, "activation", [bT], [b_ikT], out=ikT2[:, tsl], in_=pT[:, 128:256], func=AF.Copy)
            pT, bT = self.bank("T")
            self.tr(pT[:, 0:128], stg[:, 128:256], [bstg], [bT], inc=False)
            self.tr(pT[:, 128:256], stg[:, 256:384], [bstg], [bT], inc=True)
            self.op(V, "tensor_copy", [bT], [b_iqT], out=iqT[:, 0, tsl], in_=pT[:, 0:128])
            self.op(V, "tensor_copy", [bT], [b_iqT], out=iqT[:, 1, tsl], in_=pT[:, 128:256])
        if STOP == "A1":
            self.fw.barrier()
            return
        stqs = [(self.view(A_R3 + 8320 + 1024 * k, [4, 128], BF16), Buf("stq%d" % k)) for k in range(2)]
        wqs = []
        for hp in range(4):
            wq, bwq = self.wslot()
            self.dma(G, wq[:, :, 0:128], wcols(1544 + hp * 128, 128), writes=[bwq])
            wqs.append((wq, bwq))
        kk = 0
        for hp in range(4):
            wq, bwq = wqs[hp]
            for tb in range(4):
                s32, bs32 = s32s[kk % 2]
                tmp, btmp = tmps[kk % 2]
                stq, bstq = stqs[kk % 2]
                kk += 1
                pA, bA = self.bank("A")
                for tt in range(4):
                    t = 4 * tb + tt
                    for c in range(8):
                        self.mm(pA[:, tt * 128:(tt + 1) * 128], self.hT[:, c, t * 128:(t + 1) * 128], wq[:, c, 0:128],
                                c == 0, c == 7, [self.b_hT, bwq], [bA], inc=(tt == 3 and c == 7))
                self.op(A, "activation", [bA], [bs32], out=s32, in_=pA, func=AF.Copy)
                s4 = s32.rearrange("p (a h d) -> p a h d", h=2, d=64)
                self.rope_inplace(V, s4[:, :, :, 0:16], tmp[:, :, 0:2, :], 4 * tb, 4, 2, [bs32], [bs32], btmp)
                self.op(G, "tensor_copy", [bs32], [bstq], out=stq, in_=s32.rearrange("p (a k) -> p a k", k=128))
                pT, bT = self.bank("T")
                for tt in range(4):
                    self.tr(pT[:, tt * 128:(tt + 1) * 128], stq[:, tt, :], [bstq], [bT], inc=(tt == 3))
                self.op(A, "activation", [bT], [b_dqT], out=dqT[:, hp, tb * 512:(tb + 1) * 512], in_=pT[:, 0:512], func=AF.Copy)
        self.fw.barrier()
        if STOP == "A":
            return
        maskT = self.view(A_HT, [40, 512], BF16)
        b_mask = Buf("maskT")
        scs = [(self.view(A_R3 + 8192 * k, [2048], F32), Buf("sc%d" % k)) for k in range(3)]
        mbs = [(self.view(A_MIX + 4096 + 4096 * k, [2048], BF16), Buf("mb%d" % k)) for k in range(3)]
        rls = [(self.view(A_MIX + 2048 * k, [512], F32), Buf("rl%d" % k)) for k in range(2)]
        bbiss = [Buf("bis0"), Buf("bis1")]
        self.op(V, "tensor_copy", [bc], [b_mask], out=maskT[:, 0, 0:128], in_=self.triT)
        self.op(V, "memset", [], [b_mask], ap=maskT[:, 0, 128:256], constant=0.0)
        self.op(V, "tensor_copy", [bc], [b_mask], out=maskT[:, 1, 128:256], in_=self.triT)
        rkc = [0]

        def scores(qi):
            sc, bsc = scs[qi % 3]
            nk = (qi + 1) * 128
            qsl = slice(qi * 128, (qi + 1) * 128)
            for h in range(4):
                rows = slice(64 * (h % 2), 64 * (h % 2) + 64)
                for kc in range((nk + 511) // 512):
                    n = min(512, nk - kc * 512)
                    ksl = slice(kc * 512, kc * 512 + n)
                    pS, bS = self.bank("S")
                    self.mm(pS[:, 0:n], iqT[rows, h // 2, qsl], ikT2[rows, ksl], True, True, [b_iqT, b_ikT], [bS])
                    rl, brl = rls[rkc[0] % 2]
                    rkc[0] += 1
                    self.op(A, "activation", [bS], [brl], out=rl[:, 0:n], in_=pS[:, 0:n], func=AF.Relu)
                    if h == 0:
                        self.op(V, "tensor_scalar", [brl, self.b_dkr], [bsc], out=sc[:, ksl], in0=rl[:, 0:n], scalar1=self.iw[:, qi, 0:1], scalar2=None, op0=ALU.mult)
                    else:
                        self.op(V, "scalar_tensor_tensor", [brl, self.b_dkr, bsc], [bsc], out=sc[:, ksl], in0=rl[:, 0:n], scalar=self.iw[:, qi, h:h + 1],
                                in1=sc[:, ksl], op0=ALU.mult, op1=ALU.add)
            self.op(V, "tensor_tensor", [bsc, bc], [bsc], out=sc[:, qsl], in0=sc[:, qsl], in1=self.tri32, op=ALU.add)

        def bisect(qi):
            sc, bsc = scs[qi % 3]
            junk, bjunk = mbs[qi % 3]
            bis = self.bis[:, qi % 2, :]
            bbis = bbiss[qi % 2]
            nk = (qi + 1) * 128
            on_act = (qi % 2 == 1)
            self.op(V, "tensor_reduce", [bsc], [bbis], out=bis[:, 20:21], in_=sc[:, 0:nk], axis=AX.X, op=ALU.max)
            self.op(V, "tensor_reduce", [bsc], [bbis], out=bis[:, 21:22], in_=sc[:, 0:qi * 128], axis=AX.X, op=ALU.min)
            self.op(V, "tensor_tensor", [bbis], [bbis], out=bis[:, 22:23], in0=bis[:, 20:21], in1=bis[:, 21:22], op=ALU.subtract)
            if not on_act:
                self.op(V, "tensor_scalar", [bbis, bc], [bbis], out=bis[:, 0:NIT], in0=self.pow2, scalar1=bis[:, 22:23], scalar2=None, op0=ALU.mult)
                self.op(V, "scalar_tensor_tensor", [bbis], [bbis], out=bis[:, 25:26], in0=bis[:, 22:23], scalar=0.5, in1=bis[:, 21:22], op0=ALU.mult, op1=ALU.add)
                for k in range(NIT):
                    self.op(V, "tensor_scalar", [bsc, bbis], [bjunk, bbis], out=junk[:, 0:nk], in0=sc[:, 0:nk], scalar1=bis[:, 25:26], scalar2=None,
                            op0=ALU.is_ge, op1=ALU.add, accum_out=bis[:, 23:24])
                    last = (k == NIT - 1)
                    self.op(V, "tensor_scalar", [bbis], [bbis], out=bis[:, 24:25], in0=bis[:, 23:24], scalar1=255.5, scalar2=(-1.0 if last else -0.5),
                            op0=ALU.is_ge, op1=ALU.add)
                    self.op(V, "scalar_tensor_tensor", [bbis], [bbis], out=(bis[:, 26:27] if last else bis[:, 25:26]), in0=bis[:, 24:25],
                            scalar=bis[:, k:k + 1], in1=bis[:, 25:26], op0=ALU.mult, op1=ALU.add)
            else:
                self.op(V, "tensor_scalar", [bbis, bc], [bbis], out=bis[:, 0:NIT], in0=self.pow2, scalar1=bis[:, 22:23], scalar2=-0.5, op0=ALU.mult, op1=ALU.mult)
                self.op(V, "scalar_tensor_tensor", [bbis], [bbis], out=bis[:, 25:26], in0=bis[:, 22:23], scalar=-0.5, in1=bis[:, 21:22], op0=ALU.mult, op1=ALU.subtract)
                cur, oth = 25, 28
                for k in range(NIT):
                    self.op(A, "activation", [bsc, bbis], [bjunk, bbis], out=junk[:, 0:nk], in_=sc[:, 0:nk], func=AF.Sign, bias=bis[:, cur:cur + 1],
                            accum_out=bis[:, 23:24])
                    self.op(A, "activation", [bbis], [bbis], out=bis[:, 24:25], in_=bis[:, 23:24], func=AF.Sign, bias=float(nk - 511))
                    last = (k == NIT - 1)
                    dst = 27 if last else oth
                    self.op(A, "activation", [bbis], [bbis], out=bis[:, dst:dst + 1], in_=bis[:, 24:25], func=AF.Identity, scale=bis[:, k:k + 1],
                            bias=bis[:, cur:cur + 1])
                    cur, oth = oth, cur
                self.op(V, "tensor_tensor", [bbis], [bbis], out=bis[:, 26:27], in0=bis[:, NIT - 1:NIT], in1=bis[:, 27:28], op=ALU.subtract)

        def finish(qi):
            sc, bsc = scs[qi % 3]
            mb, bmb = mbs[qi % 3]
            bis = self.bis[:, qi % 2, :]
            bbis = bbiss[qi % 2]
            nk = (qi + 1) * 128
            self.op(V, "tensor_scalar", [bsc, bbis], [bmb], out=mb[:, 0:nk], in0=sc[:, 0:nk], scalar1=bis[:, 26:27], scalar2=NEG, op0=ALU.is_lt, op1=ALU.mult)
            qc = qi // 4
            for j0 in range(0, qi + 1, 8):
                nb = min(8, qi + 1 - j0)
                pT, bT = self.bank("T")
                for jj in range(nb):
                    self.tr(pT[:, jj * 128:(jj + 1) * 128], mb[:, (j0 + jj) * 128:(j0 + jj + 1) * 128], [bmb], [bT], inc=(jj == nb - 1))
                self.op(V, "tensor_copy", [bT], [b_mask], out=maskT[:, MOFF[qc] + j0:MOFF[qc] + j0 + nb, (qi % 4) * 128:(qi % 4 + 1) * 128],
                        in_=pT[:, 0:nb * 128].rearrange("p (a k) -> p a k", k=128))

        scores(2)
        for qi in range(2, 16):
            if qi + 1 <= 15:
                scores(qi + 1)
            bisect(qi)
            if qi - 1 >= 2:
                finish(qi - 1)
        finish(15)
        self.fw.barrier()
        if STOP == "B":
            return
        dk = [(self.view(A_R3 + 4096 * k, [2048], BF16), Buf("dk%d" % k)) for k in range(4)]
        for k_, (ka_, bka_) in enumerate(dk):
            zr = slice(64, 128) if k_ % 2 == 0 else slice(0, 64)
            self.op(G, "memset", [], [bka_], ap=ka_[zr, :], constant=0.0)
        o = A_R3 + 16384
        vaug = [(self.view(o + 4224 * k, [16, 2, 66], BF16), Buf("v%d" % k)) for k in range(2)]
        o += 8448
        PT = [(self.view(o + 1024 * k, [512], BF16), Buf("PT%d" % k)) for k in range(4)]
        o += 4096
        mst, bmst = self.view(o, [16, 128], BF16), Buf("mst")
        o += 4096
        stk, bstk = self.view(o, [4, 2, 64], BF16), Buf("stk")
        o += 1024
        for va, bva in vaug:
            self.op(G, "memset", [], [bva], ap=va[:, :, :, 64:66], constant=1.0)
        wuk, bwuk = self.wslot((2048,))
        wuv, bwuv = self.wslot((2048,))
        self.dma(G, wuk[:, 0:384], dr["w_uk"][i], writes=[bwuk])
        self.dma(G, wuv[:, 0:512], dr["w_uv"][i], writes=[bwuv])
        for hp in range(4):
            st_ = hp % 2
            kp = [dk[2 * st_], dk[2 * st_ + 1]]
            va, bva = vaug[st_]
            for tb in range(4):
                pA, bA = self.bank("A")
                pB, bB = self.bank("A")
                for tt in range(4):
                    tsl = slice((4 * tb + tt) * 128, (4 * tb + tt + 1) * 128)
                    self.mm(pA[:, tt * 96:(tt + 1) * 96], self.ckvT[:, tsl], wuk[:, hp * 96:(hp + 1) * 96], True, True, [self.b_ckvT, bwuk], [bA], inc=(tt == 3))
                for tt in range(4):
                    tsl = slice((4 * tb + tt) * 128, (4 * tb + tt + 1) * 128)
                    self.mm(pB[:, tt * 128:(tt + 1) * 128], self.ckvT[:, tsl], wuv[:, hp * 128:(hp + 1) * 128], True, True, [self.b_ckvT, bwuv], [bB], inc=(tt == 3))
                self.op(A, "activation", [bA], [bstk], out=stk[:, :, :, 16:64], in_=pA[:, 0:384].rearrange("p (a h d) -> p a h d", h=2, d=48), func=AF.Copy)
                self.op(G, "tensor_copy", [self.b_dkr], [bstk], out=stk[:, :, :, 0:16], in_=self.dkr[:, 4 * tb:4 * tb + 4, :].unsqueeze(2).to_broadcast([128, 4, 2, 16]))
                self.op(A, "activation", [bB], [bva], out=va[:, 4 * tb:4 * tb + 4, :, 0:64], in_=pB.rearrange("p (a h d) -> p a h d", h=2, d=64), func=AF.Copy)
                pT, bT = self.bank("T")
                for tt in range(4):
                    self.tr(pT[:, tt * 128:(tt + 1) * 128], stk[:, tt, :, :].rearrange("p h d -> p (h d)"), [bstk], [bT], inc=(tt == 3))
                self.op(V, "tensor_copy", [bT], [kp[0][1]], out=kp[0][0][0:64, tb * 512:(tb + 1) * 512], in_=pT[0:64, 0:512])
                self.op(V, "tensor_copy", [bT], [kp[1][1]], out=kp[1][0][64:128, tb * 512:(tb + 1) * 512], in_=pT[64:128, 0:512])
            for hh in range(2):
                self.attend(dqT[:, hp, :], kp[hh][0], slice(0, 128), [b_dqT, kp[hh][1]], va[:, :, hh, :], [bva], mst, bmst, hh * 64, PT, "dsa",
                            mask=lambda qc, j, c0, N: (maskT[:, MOFF[qc] + j, c0:c0 + N], b_mask))
            self.mixed_to_T(mst, bmst, hp)
        self.outproj_half(wout, 1)
        self.fw.barrier()

    def final_norm(self, y):
        A, V = self.ACT, self.DVE
        junk = (self.view(A_R3 + 4096, [1024], BF16), Buf("junk"))
        ot = [(self.view(A_MIX + 4096 * i, [1024], F32), Buf("ot%d" % i)) for i in range(2)]
        bst = self.b_st
        yr = y.rearrange("(t p) d -> p t d", p=128)
        self.gfin = self.view(A_MIX + 8192, [1024], F32)
        self.dma(self.SP, self.gfin, self.dr["final_norm"].partition_broadcast(128), writes=[self.b_const])
        for t in range(16):
            self.op(A, "activation", [self.bx[t]], [junk[1], bst], out=junk[0], in_=self.x[:, t, :], func=AF.Square,
                    accum_out=self.ss[:, t:t + 1])
        self.op(A, "activation", [bst], [bst], out=self.rstd, in_=self.ss, func=AF.Sqrt, scale=1.0 / D, bias=EPS)
        self.op(V, "reciprocal", [bst], [bst], out=self.rstd, in_=self.rstd)
        outs = []
        for t in range(16):
            o, bo = ot[t % 2]
            self.op(V, "scalar_tensor_tensor", [self.bx[t], bst, self.b_const], [bo], out=o, in0=self.x[:, t, :],
                    scalar=self.rstd[:, t:t + 1], in1=self.gfin, op0=ALU.mult, op1=ALU.mult)
            b = Buf("y%d" % t)
            self.dma(self.SP, yr[:, t, :], o, reads=[bo], writes=[b])
            outs.append(b)
        return outs

    def store_x(self, y):
        yr = y.rearrange("(t p) d -> p t d", p=128)
        outs = []
        for t in range(16):
            b = Buf("y%d" % t)
            self.dma(self.SP, yr[:, t, :], self.x[:, t, :], reads=[self.bx[t]], writes=[b])
            outs.append(b)
        return outs


IN_SPECS = [
    ("x", [S, D], F32), ("positions", [S], I32), ("attn_norm", [4, D], F32), ("ffn_norm", [4, D], F32),
    ("final_norm", [D], F32), ("w_in_even", [2, D, EVEN_IN], F32), ("b_fox_f", [2, 8], F32), ("g_ckv", [2, 128], F32),
    ("w_uk", [2, 128, 384], F32), ("w_uv", [2, 128, 512], F32), ("w_out_even", [2, D, D], F32),
    ("w_in_odd", [2, D, 3072], F32), ("w_out_odd", [2, D, D], F32), ("w_up", [4, D, 2 * DFF], F32),
    ("conv_w", [4, 3, DFF], F32), ("conv_b", [4, DFF], F32), ("w_down", [4, DFF, D], F32),
    ("c_ident", [128, 128], F32), ("c_triT", [128, 128], F32), ("c_tri32", [128, 128], F32),
    ("c_U32", [128, 128], F32), ("c_ones32", [128, 128], F32), ("c_sel", [128, 1024], F32), ("c_invf", [16], F32), ("c_pow2", [NIT], F32),
]


def host_consts():
    i = np.arange(128)
    c = {}
    c["c_ident"] = np.eye(128, dtype=np.float32)
    c["c_triT"] = np.where(i[:, None] > i[None, :], NEG, 0.0).astype(np.float32)
    c["c_tri32"] = np.where(i[None, :] > i[:, None], -1e30, 0.0).astype(np.float32)
    c["c_U32"] = (i[:, None] <= i[None, :]).astype(np.float32)
    c["c_ones32"] = np.ones((128, 128), np.float32)
    c["c_sel"] = np.repeat(np.eye(128, 8, dtype=np.float32), 128, axis=1)
    invf = (500000.0 ** (-np.arange(0, 16, 2, dtype=np.float32) / 16)).astype(np.float32)
    c["c_invf"] = np.concatenate([invf, invf]).astype(np.float32)
    c["c_pow2"] = (2.0 ** -(np.arange(NIT) + 1.0)).astype(np.float32)
    return c


def build_nc(parts, final=True, dbg_names=()):
    nc = bass.Bass("TRN2", target_bir_lowering=False)
    dr = {}
    for name, shape, dt in IN_SPECS:
        dr[name] = nc.dram_tensor(name, shape, dt, kind="ExternalInput").ap()
    y = nc.dram_tensor("y", [S, D], F32, kind="ExternalOutput").ap()
    p = Prog(nc, dr)
    p.load_x()
    p.rope_tables()
    p.fw.barrier()
    for part in parts:
        if part[0] == "ffn":
            p.ffn(part[1])
        elif part[0] == "even":
            p.even(part[1])
        elif part[0] == "odd":
            p.odd(part[1])
    outs = p.final_norm(y) if final else p.store_x(y)
    douts = []
    for name in dbg_names:
        ap = p.dbg[name]
        shp = [128, int(np.prod(ap.shape[1:]))]
        d = nc.dram_tensor("dbg_" + name, shp, ap.dtype, kind="ExternalOutput").ap()
        b = Buf()
        flat = ap
        if len(ap.shape) == 3:
            flat = ap.rearrange("p a b -> p (a b)")
        elif len(ap.shape) == 4:
            flat = ap.rearrange("p a b c -> p (a b c)")
        p.fw.barrier()
        p.dma(p.SP, d, flat, writes=[b])
        douts.append(b)
    p.fw.wait_all(p.SP, outs + douts)
    return nc, p


def make_in_maps(inputs, cores):
    c = host_consts()
    maps = []
    for b in cores:
        m = dict(c)
        for name, shape, dt in IN_SPECS:
            if name.startswith("c_"):
                continue
            a = np.asarray(inputs[name])
            if name in ("x", "positions"):
                a = a[b]
            a = np.ascontiguousarray(a).reshape(shape)
            m[name] = a.astype(np.int32 if dt == I32 else np.float32, copy=False)
        maps.append(m)
    return maps


ALL_PARTS = [("even", 0), ("ffn", 0), ("odd", 0), ("ffn", 1), ("even", 1), ("ffn", 2), ("odd", 1), ("ffn", 3)]


def kernel(**inputs):
    nc, _ = build_nc(ALL_PARTS, final=True)
    maps = make_in_maps(inputs, list(range(8)))
    res = run_bass_kernel_spmd(nc, maps, core_ids=list(range(8)))
    return np.stack([np.asarray(r["y"], dtype=np.float32) for r in res.results], axis=0)
```

```python
import numpy as np
import concourse.bass as bass
import concourse.mybir as mybir
from concourse.bass_utils import run_bass_kernel_spmd

F32 = mybir.dt.float32
BF16 = mybir.dt.bfloat16
I32 = mybir.dt.int32
ALU = mybir.AluOpType
AF = mybir.ActivationFunctionType
AX = mybir.AxisListType

SEM_ROT = 16000
import os
STOP = os.environ.get('EVEN_STOP', '')


class Buf:
    __slots__ = ("w", "r", "name")

    def __init__(self, name=""):
        self.w = None
        self.r = {}
        self.name = name


class EngCtx:
    def __init__(self, fw, name, eng, order_raw):
        self.fw, self.name, self.eng = fw, name, eng
        self.sem = fw.new_sem(name)
        self.count = 0
        self.seen = {}
        self.order_raw = order_raw
        self.nrot = 0
        self.prev = None

    def rotate(self):
        if self.count >= SEM_ROT:
            self.nrot += 1
            self.prev = (self.sem, self.count)
            self.sem = self.fw.new_sem("%s_r%d" % (self.name, self.nrot))
            self.count = 0


class FW:
    def __init__(self, nc):
        self.nc = nc
        self.nsem = 0
        self.pe = EngCtx(self, "pe", nc.tensor, False)
        self.act = EngCtx(self, "act", nc.scalar, True)
        self.dve = EngCtx(self, "dve", nc.vector, True)
        self.pool = EngCtx(self, "pool", nc.gpsimd, True)
        self.sp = EngCtx(self, "sp", nc.sync, True)
        self.dma_pools = {}
        for q in (self.sp, self.pool):
            self.dma_pools[id(q)] = [[[self.new_sem("dma_%s%d" % (q.name, i)), 0] for i in range(12)], 0]
        self.n_ops = 0

    def new_sem(self, name):
        self.nsem += 1
        return self.nc.alloc_semaphore(name="s_%s_%d" % (name, self.nsem))

    def _deps(self, E, order_raw, seen, reads, writes):
        deps = {}

        def add(d, raw):
            if d is None:
                return
            key, sem, val = d
            if E is not None and key is E:
                if not order_raw:
                    return
            k = id(sem)
            if k not in deps or deps[k][1] < val:
                deps[k] = (sem, val)

        for b in reads:
            add(b.w, True)
        for b in writes:
            add(b.w, False)
            for d in b.r.values():
                add(d, False)
        out = []
        for k, (sem, val) in deps.items():
            if seen.get(k, -1) >= val:
                continue
            out.append((sem, val))
        return out

    def _emit_waits(self, E, ins_fn, deps):
        for sem, val in deps[1:]:
            E.eng.wait_ge(sem, val)
            E.seen[id(sem)] = max(E.seen.get(id(sem), -1), val)
        ins = ins_fn()
        if deps:
            sem, val = deps[0]
            ins._wait_ge(sem, val)
            E.seen[id(sem)] = max(E.seen.get(id(sem), -1), val)
        return ins

    def op(self, E, fn, reads=(), writes=(), inc=True):
        self.n_ops += 1
        deps = self._deps(E, E.order_raw, E.seen, reads, writes)
        ins = self._emit_waits(E, fn, deps)
        val = E.count + 1
        if inc:
            ins.then_inc(E.sem, 1)
            E.count = val
        rec = (E, E.sem, val)
        for b in writes:
            b.w = rec
            b.r = {}
        for b in reads:
            b.r[id(E)] = rec
        if inc:
            E.rotate()
        return ins

    def dma(self, Q, out_ap, in_ap, reads=(), writes=(), **kw):
        self.n_ops += 1
        pool = self.dma_pools[id(Q)]
        slot = pool[0][pool[1]]
        pool[1] = (pool[1] + 1) % len(pool[0])
        if slot[1] >= SEM_ROT:
            Q.eng.wait_ge(slot[0], slot[1])
            slot[0] = self.new_sem("dmar")
            slot[1] = 0
        sem = slot[0]
        deps = self._deps(None, True, Q.seen, reads, writes)
        if slot[1] > 0 and Q.seen.get(id(sem), -1) < slot[1]:
            deps = [d for d in deps if d[0] is not sem]
            deps.append((sem, slot[1]))
        deps = [d for d in deps if Q.seen.get(id(d[0]), -1) < d[1]]
        ins = self._emit_waits(Q, lambda: Q.eng.dma_start(out=out_ap, in_=in_ap, **kw), deps)
        slot[1] += 16
        ins.then_inc(sem, 16)
        rec = (slot, sem, slot[1])
        for b in writes:
            b.w = rec
            b.r = {}
        for b in reads:
            b.r[id(slot)] = rec
        return ins

    def wait_all(self, E, bufs):
        deps = self._deps(None, True, E.seen, bufs, ())
        for sem, val in deps:
            E.eng.wait_ge(sem, val)
            E.seen[id(sem)] = val


    def barrier(self):
        engs = [self.pe, self.act, self.dve, self.pool, self.sp]
        items = [(e.sem, e.count) for e in engs if e.count > 0]
        for pl in self.dma_pools.values():
            items += [(s[0], s[1]) for s in pl[0] if s[1] > 0]
        items += [e.prev for e in engs if e.prev is not None]
        for E in engs:
            for sem, val in items:
                if sem is E.sem:
                    continue
                if E.seen.get(id(sem), -1) >= val:
                    continue
                E.eng.wait_ge(sem, val)
                E.seen[id(sem)] = val


S = 2048
D = 1024
NT = 16
DFF = 2816
NFC = 22
EPS = 1e-6
NEG = -30000.0
NIT = 16
EVEN_IN = 2524
IDX_SCALE = (64 ** -0.5) * (4 ** -0.5)
MOFF = [0, 4, 12, 24]

A_X = 0
A_HT = 65536
A_MIX = 106496
A_R1 = 122880
A_R3 = 139264
A_CKV = 176128
A_W = 180224
A_C = 196608
A_END = 208896


def _sz(dt):
    return 4 if dt in (F32, I32) else 2


class Prog:
    def __init__(self, nc, dr, dbg=None):
        self.nc = nc
        self.dr = dr
        self.fw = FW(nc)
        self.arena = nc.alloc_sbuf_tensor("arena", [128, A_END // 4], F32)
        fw = self.fw
        self.PE, self.ACT, self.DVE, self.POOL, self.SP = fw.pe, fw.act, fw.dve, fw.pool, fw.sp
        self.ps = {}
        for n in ("A0", "A1", "S0", "S1", "O0", "O1"):
            self.ps[n] = (nc.alloc_psum_tensor("p" + n, [128, 512], F32)[:, :], Buf("p" + n))
        for n in ("T0", "T1"):
            self.ps[n] = (nc.alloc_psum_tensor("p" + n, [128, 1024], BF16)[:, :], Buf("p" + n))
        self.rot = {}
        self.dbg = dbg if dbg is not None else {}
        self.dbg_out = []
        self._consts()

    def view(self, off, shape, dt):
        n = int(np.prod(shape))
        sz = _sz(dt)
        assert off % 4 == 0 and (n * sz) % 4 == 0, (off, shape)
        a = self.arena[:, off // 4:(off + n * sz) // 4]
        if dt != F32:
            a = a.bitcast(dt)
        if len(shape) == 2:
            a = a.rearrange("p (a b) -> p a b", b=shape[1])
        elif len(shape) == 3:
            a = a.rearrange("p (a b c) -> p a b c", b=shape[1], c=shape[2])
        elif len(shape) == 4:
            a = a.rearrange("p (a b c d) -> p a b c d", b=shape[1], c=shape[2], d=shape[3])
        return a

    def bank(self, kind):
        k = self.rot.get(kind, 0)
        self.rot[kind] = k ^ 1
        return self.ps[kind + str(k)]

    def op(self, E, meth, reads, writes, inc=True, **kw):
        return self.fw.op(E, lambda: getattr(E.eng, meth)(**kw), reads, writes, inc)

    def mm(self, out, lhsT, rhs, start, stop, reads, writes, inc=True, skip=False):
        kw = dict(out=out, lhsT=lhsT, rhs=rhs, start=start, stop=stop)
        if skip:
            kw["skip_group_check"] = True
        return self.fw.op(self.PE, lambda: self.nc.tensor.matmul(**kw), reads, writes, inc)

    def tr(self, out, in_, reads, writes, inc=True):
        return self.fw.op(self.PE, lambda: self.nc.tensor.transpose(out=out, in_=in_, identity=self.ident),
                          list(reads) + [self.b_const], writes, inc)

    def dma(self, Q, out, in_, reads=(), writes=(), **kw):
        return self.fw.dma(Q, out, in_, reads, writes, **kw)

    def _consts(self):
        nc, dr = self.nc, self.dr
        o = [A_C]

        def take(shape, dt):
            n = int(np.prod(shape)) * _sz(dt)
            n = (n + 31) // 32 * 32
            v = self.view(o[0], shape, dt)
            o[0] += n
            assert o[0] <= A_END, o[0]
            return v
        self.b_const = Buf("const")
        bc = [self.b_const]
        self.ident = take([128], BF16)
        self.triT = take([128], BF16)
        self.zeros = take([264], BF16)
        self.tri32 = take([128], F32)
        self.U32 = take([128], F32)
        self.ones32 = take([128], F32)
        self.anorm = take([4, 8], F32)
        self.fnorm = take([4, 8], F32)
        self.cos2 = take([16, 16], F32)
        self.sin2 = take([16, 16], F32)
        self.invf = take([16], F32)
        self.pow2 = take([NIT], F32)
        self.pos_i = take([16], I32)
        self.pos_f = take([16], F32)
        self.ang = self.view(A_R3, [16, 16], F32)
        self.angk = self.view(A_R3 + 1024, [16, 16], F32)
        self.angi = self.view(A_R3 + 2048, [16, 16], I32)
        self.ss = take([16], F32)
        self.rstd = take([16], F32)
        self.convw = take([4, 3, NFC], F32)
        self.convb = take([4, NFC], F32)
        self.bfox = take([2, 8], F32)
        self.gckv = take([2], F32)
        self.fb = take([16, 4, 8], F32)
        self.csb = take([16, 8], F32)
        self.cref = take([4, 8], F32)
        self.zf = self.view(A_R3, [16, 8], F32)
        self.zf2 = self.view(A_R3 + 512, [16, 8], F32)
        self.iw = take([16, 4], F32)
        self.st = take([64], F32)
        self.b_st = Buf("st")
        self.dkr = take([16, 16], BF16)
        self.b_dkr = Buf("dkr")
        self.bis = take([3, 32], F32)
        self.kmT32 = take([2, 8], F32)
        self.kmT = take([2, 8], BF16)
        self.b_km = Buf("km")
        self.b_fb = Buf("fb")
        self.ptk = 0
        SP, POOL = self.SP, self.POOL
        self.dma(POOL, self.ident, dr["c_ident"], writes=bc)
        self.dma(POOL, self.triT, dr["c_triT"], writes=bc)
        self.dma(SP, self.tri32, dr["c_tri32"], writes=bc)
        self.dma(SP, self.U32, dr["c_U32"], writes=bc)
        self.dma(SP, self.ones32, dr["c_ones32"], writes=bc)
        self.dma(SP, self.anorm, dr["attn_norm"].rearrange("l (c p) -> p l c", p=128), writes=bc,
                 allow_slow_non_contiguous=True)
        self.dma(SP, self.fnorm, dr["ffn_norm"].rearrange("l (c p) -> p l c", p=128), writes=bc,
                 allow_slow_non_contiguous=True)
        self.dma(SP, self.invf, dr["c_invf"].partition_broadcast(128), writes=bc)
        self.dma(SP, self.pow2, dr["c_pow2"].partition_broadcast(128), writes=bc)
        self.dma(SP, self.pos_i, dr["positions"].rearrange("(t p) -> p t", p=128), writes=bc,
                 allow_slow_non_contiguous=True)
        self.dma(SP, self.convw, dr["conv_w"].rearrange("l j (c p) -> p l j c", p=128), writes=bc,
                 allow_slow_non_contiguous=True)
        self.dma(SP, self.convb, dr["conv_b"].rearrange("l (c p) -> p l c", p=128), writes=bc,
                 allow_slow_non_contiguous=True)
        self.dma(SP, self.bfox.rearrange("p a b -> p (a b)"), dr["b_fox_f"].rearrange("a b -> (a b)").partition_broadcast(128), writes=bc)
        self.dma(SP, self.gckv, dr["g_ckv"].rearrange("l p -> p l"), writes=bc, allow_slow_non_contiguous=True)
        self.op(self.DVE, "memset", [], bc, ap=self.zeros, constant=0.0)
        self.x = self.view(A_X, [16, 1024], F32)
        self.bx = [Buf("x%d" % t) for t in range(16)]
        self.hT = self.view(A_HT, [8, 2048], BF16)
        self.b_hT = Buf("hT")
        self.mixT = self.view(A_MIX, [4, 2048], BF16)
        self.b_mixT = Buf("mixT")
        self.ckvT = self.view(A_CKV, [2048], BF16)
        self.b_ckvT = Buf("ckvT")
        self.wslots = [(A_W + 4096 * i, Buf("w%d" % i)) for i in range(4)]
        self.wnext = 0

    def wslot(self, shape=(8, 256)):
        off, b = self.wslots[self.wnext]
        self.wnext = (self.wnext + 1) % 4
        return self.view(off, list(shape), BF16), b

    def load_x(self):
        xr = self.dr["x"].rearrange("(t p) d -> p t d", p=128)
        for t in range(16):
            self.dma(self.SP, self.x[:, t, :], xr[:, t, :], writes=[self.bx[t]])

    def rope_tables(self):
        V, A = self.DVE, self.ACT
        bc = [self.b_const]
        TWO_PI = 2.0 * np.pi
        C1 = 6.28125
        C2 = TWO_PI - C1
        self.op(V, "tensor_copy", bc, bc, out=self.pos_f, in_=self.pos_i)
        for which, dst in ((0, self.sin2), (1, self.cos2)):
            self.op(V, "tensor_tensor", bc, bc, out=self.ang,
                    in0=self.pos_f.unsqueeze(2).to_broadcast([128, 16, 16]),
                    in1=self.invf.unsqueeze(1).to_broadcast([128, 16, 16]), op=ALU.mult)
            if which == 1:
                self.op(V, "tensor_scalar", bc, bc, out=self.ang, in0=self.ang, scalar1=float(np.pi / 2), scalar2=None, op0=ALU.add)
            self.op(V, "tensor_scalar", bc, bc, out=self.angk, in0=self.ang, scalar1=float(1.0 / TWO_PI), scalar2=None, op0=ALU.mult)
            self.op(V, "tensor_copy", bc, bc, out=self.angi, in_=self.angk)
            self.op(V, "tensor_copy", bc, bc, out=self.angk, in_=self.angi)
            self.op(V, "scalar_tensor_tensor", bc, bc, out=self.ang, in0=self.angk, scalar=-C1, in1=self.ang, op0=ALU.mult, op1=ALU.add)
            self.op(V, "scalar_tensor_tensor", bc, bc, out=self.ang, in0=self.angk, scalar=-C2, in1=self.ang, op0=ALU.mult, op1=ALU.add)
            self.op(V, "tensor_scalar", bc, bc, out=self.ang, in0=self.ang, scalar1=3.1415925, scalar2=-3.1415925, op0=ALU.min, op1=ALU.max)
            self.op(A, "activation", bc, bc, out=dst, in_=self.ang, func=AF.Sin)

    def norm_to_hT(self, gain):
        A, V, G = self.ACT, self.DVE, self.POOL
        xh = [(self.view(A_R3 + 2048 * i, [1024], BF16), Buf("xh%d" % i)) for i in range(2)]
        junk = (self.view(A_R3 + 4096, [1024], BF16), Buf("junk"))
        bst = self.b_st
        for t in range(16):
            self.op(A, "activation", [self.bx[t]], [junk[1], bst], out=junk[0], in_=self.x[:, t, :], func=AF.Square,
                    accum_out=self.ss[:, t:t + 1])
        self.op(A, "activation", [bst], [bst], out=self.rstd, in_=self.ss, func=AF.Sqrt, scale=1.0 / D, bias=EPS)
        self.op(V, "reciprocal", [bst], [bst], out=self.rstd, in_=self.rstd)
        for t in range(16):
            xa, xb = xh[t % 2]
            self.op(A, "activation", [self.bx[t], bst], [xb], out=xa, in_=self.x[:, t, :], func=AF.Copy, scale=self.rstd[:, t:t + 1])
            pT, bT = self.bank("T")
            for c in range(8):
                self.tr(pT[:, c * 128:(c + 1) * 128], xa[:, c * 128:(c + 1) * 128], [xb], [bT], inc=(c == 7))
            self.op(V, "tensor_tensor", [bT, self.b_const], [self.b_hT], out=self.hT[:, :, t * 128:(t + 1) * 128],
                    in0=pT.rearrange("p (c k) -> p c k", k=128), in1=gain.unsqueeze(2).to_broadcast([128, 8, 128]), op=ALU.mult)

    def ffn(self, layer):
        A, V, G = self.ACT, self.DVE, self.POOL
        dr = self.dr
        bc = self.b_const
        self.norm_to_hT(self.fnorm[:, layer, :])
        self.fw.barrier()
        base = A_MIX
        actT = [(self.view(base + 24576 * i, [6, 2048], BF16), Buf("actT%d" % i)) for i in range(2)]
        gsb = [(self.view(base + 49152 + 8224 * i, [2056], F32), [Buf("gsb%d_%d" % (i, k)) for k in range(4)]) for i in range(2)]
        sp = A_HT + 32768
        acc = [(self.view(sp + 2048 * i, [512], F32), Buf("acc%d" % i)) for i in range(2)]
        sg = [(self.view(sp + 4096 + 2048 * i, [512], F32), Buf("sg%d" % i)) for i in range(2)]
        for gs, bgs in gsb:
            self.op(V, "memset", [], [bgs[0]], ap=gs[:, 0:2], constant=0.0)
        wup = dr["w_up"][layer]
        wdn = dr["w_down"][layer]
        groups = [(0, 6), (6, 12), (12, 18), (18, 22)]
        k = 0
        for g, (f0, f1) in enumerate(groups):
            aT, baT = actT[g % 2]
            for fc in range(f0, f1):
                ws, bws = self.wslot()
                self.dma(G, ws[:, :, 0:128], wup[:, fc * 128:(fc + 1) * 128].rearrange("(c p) n -> p c n", p=128), writes=[bws])
                self.dma(G, ws[:, :, 128:256], wup[:, DFF + fc * 128:DFF + (fc + 1) * 128].rearrange("(c p) n -> p c n", p=128), writes=[bws])
                gs, bgs = gsb[fc % 2]
                for tc in range(4):
                    tsl = slice(tc * 512, (tc + 1) * 512)
                    pG, bG = self.bank("A")
                    pU, bU = self.bank("S")
                    for c in range(8):
                        self.mm(pG, ws[:, c, 0:128], self.hT[:, c, tsl], c == 0, c == 7, [bws, self.b_hT], [bG], inc=(c == 7))
                    for c in range(8):
                        self.mm(pU, ws[:, c, 128:256], self.hT[:, c, tsl], c == 0, c == 7, [bws, self.b_hT], [bU], inc=(c == 7))
                    self.op(A, "activation", [bG], [bgs[tc]], out=gs[:, 2 + tc * 512:2 + (tc + 1) * 512], in_=pG, func=AF.Copy)
                    ac, bac = acc[k % 2]
                    sgt, bsg = sg[k % 2]
                    k += 1
                    rd = [bgs[tc], bc] + ([bgs[tc - 1]] if tc > 0 else [bgs[0]])
                    self.op(V, "tensor_scalar", rd, [bac], out=ac, in0=gs[:, 2 + tc * 512:2 + (tc + 1) * 512],
                            scalar1=self.convw[:, layer, 2, fc:fc + 1], scalar2=self.convb[:, layer, fc:fc + 1], op0=ALU.mult, op1=ALU.add)
                    self.op(V, "scalar_tensor_tensor", rd + [bac], [bac], out=ac, in0=gs[:, 1 + tc * 512:1 + (tc + 1) * 512],
                            scalar=self.convw[:, layer, 1, fc:fc + 1], in1=ac, op0=ALU.mult, op1=ALU.add)
                    self.op(V, "scalar_tensor_tensor", rd + [bac], [bac], out=ac, in0=gs[:, tc * 512:(tc + 1) * 512],
                            scalar=self.convw[:, layer, 0, fc:fc + 1], in1=ac, op0=ALU.mult, op1=ALU.add)
                    self.op(A, "activation", [bac], [bsg], out=sgt, in_=ac, func=AF.Silu)
                    self.op(V, "tensor_tensor", [bsg, bU], [baT], out=aT[:, fc - f0, tsl], in0=sgt, in1=pU, op=ALU.mult)
            nf = f1 - f0
            wd = []
            for s in range((nf + 1) // 2):
                ws, bws = self.wslot((2, 1024))
                r0 = (f0 + 2 * s) * 128
                self.dma(G, ws, wdn[r0:r0 + 256, :].rearrange("(c p) n -> p c n", p=128), writes=[bws])
                wd.append((ws, bws))
            for t in range(16):
                for nh in range(2):
                    pO, bO = self.bank("O")
                    nsl = slice(nh * 512, (nh + 1) * 512)
                    for fl in range(nf):
                        ws, bws = wd[fl // 2]
                        self.mm(pO, aT[:, fl, t * 128:(t + 1) * 128], ws[:, fl % 2, nsl], fl == 0, fl == nf - 1, [baT, bws], [bO],
                                inc=(fl == nf - 1))
                    self.op(V, "tensor_tensor", [bO, self.bx[t]], [self.bx[t]], out=self.x[:, t, nsl], in0=self.x[:, t, nsl], in1=pO, op=ALU.add)
        self.fw.barrier()

    def attend(self, qT, kT, rows, rd_qk, v, rd_v, stage, bstage, scol, PT, mode, bias=None, mask=None, qbias=None):
        A, V = self.ACT, self.DVE
        for qc in range(4):
            pO, bO = self.bank("O")
            self.mm(pO[:, 0:264], self.zeros[:, 0:128], self.zeros[:, 0:264], True, True, [self.b_const], [bO])
            pend = None

            def pv(args):
                j, q0, pt, bpt = args
                for t in range(q0 // 128, 4 * qc + 4):
                    c0 = (t - 4 * qc) * 66
                    self.mm(pO[:, c0:c0 + 65], pt[:, t * 128 - q0:t * 128 - q0 + 128], v[:, j, 0:65], False, (j == t),
                            [bpt] + rd_v, [bO], inc=(t == 4 * qc + 3), skip=True)
            for j in range(4 * qc + 4):
                q0 = max(512 * qc, 128 * j)
                N = 512 * qc + 512 - q0
                pS, bS = self.bank("S")
                diag = (mode != "dsa") and (j >= 4 * qc)
                ksl = slice(j * 128, (j + 1) * 128)
                groups = [(0, 128, True), (128, N - 128, False)] if diag else [(0, N, False)]
                groups = [g for g in groups if g[1] > 0]
                for gi, (c0, n, tri) in enumerate(groups):
                    terms = []
                    if tri:
                        terms.append((self.ident, self.triT, [self.b_const]))
                    if mode == "dsa":
                        m_ap, m_b = mask(qc, j, q0 - 512 * qc + c0, n)
                        terms.append((self.ident, m_ap, [self.b_const, m_b]))
                    if qbias is not None:
                        l_ap, r_ap, r_b = qbias(q0 + c0, n)
                        terms.append((l_ap, r_ap, r_b))
                    lastg = (gi == len(groups) - 1)
                    self.mm(pS[:, c0:c0 + n], kT[rows, ksl], qT[rows, q0 + c0:q0 + c0 + n], True, not terms, rd_qk, [bS],
                            inc=(lastg and not terms))
                    for ti, (l_ap, r_ap, r_b) in enumerate(terms):
                        lt = (ti == len(terms) - 1)
                        self.mm(pS[:, c0:c0 + n], l_ap, r_ap, False, lt, r_b, [bS], inc=(lastg and lt))
                pt, bpt = PT[self.ptk % len(PT)]
                self.ptk += 1
                kw = dict(out=pt[:, 0:N], in_=pS[:, 0:N], func=AF.Exp, scale=0.125)
                rds = [bS]
                if bias is not None:
                    kw["bias"] = bias(j, qc)
                    rds.append(self.b_fb)
                self.op(A, "activation", rds, [bpt], **kw)
                if pend is not None:
                    pv(pend)
                pend = (j, q0, pt, bpt)
            pv(pend)
            for t in range(4 * qc, 4 * qc + 4):
                c0 = (t - 4 * qc) * 66
                ri = self.st[:, (t % 8):(t % 8) + 1]
                self.op(V, "reciprocal", [bO], [self.b_st], out=ri, in_=pO[:, c0 + 64:c0 + 65])
                self.op(V, "tensor_scalar", [bO, self.b_st], [bstage], out=stage[:, t, scol:scol + 64], in0=pO[:, c0:c0 + 64],
                        scalar1=ri, scalar2=None, op0=ALU.mult)

    def rope_inplace(self, E, data, tmp, t0, nt, nh, rd, wr, btmp):
        cs = self.cos2[:, t0:t0 + nt, :].unsqueeze(2).to_broadcast([128, nt, nh, 16])
        sn = self.sin2[:, t0:t0 + nt, :].unsqueeze(2).to_broadcast([128, nt, nh, 16])
        bc = self.b_const
        self.op(E, "tensor_tensor", rd + [bc], [btmp], out=tmp, in0=data, in1=sn, op=ALU.mult)
        self.op(E, "tensor_tensor", rd + [bc], wr, out=data, in0=data, in1=cs, op=ALU.mult)
        self.op(E, "tensor_tensor", rd + [btmp], wr, out=data[:, :, :, 0:8], in0=data[:, :, :, 0:8], in1=tmp[:, :, :, 8:16], op=ALU.subtract)
        self.op(E, "tensor_tensor", rd + [btmp], wr, out=data[:, :, :, 8:16], in0=data[:, :, :, 8:16], in1=tmp[:, :, :, 0:8], op=ALU.add)

    def outproj_half(self, w, half):
        V, G = self.DVE, self.POOL
        for ng in range(2):
            ws, bws = self.wslot((4, 512))
            self.dma(G, ws, w[half * 512:(half + 1) * 512, ng * 512:(ng + 1) * 512].rearrange("(c p) n -> p c n", p=128), writes=[bws])
            nsl = slice(ng * 512, (ng + 1) * 512)
            for t in range(16):
                pA, bA = self.bank("A")
                for c in range(4):
                    self.mm(pA, self.mixT[:, c, t * 128:(t + 1) * 128], ws[:, c, :], c == 0, c == 3, [self.b_mixT, bws], [bA], inc=(c == 3))
                self.op(V, "tensor_tensor", [bA, self.bx[t]], [self.bx[t]], out=self.x[:, t, nsl], in0=self.x[:, t, nsl], in1=pA, op=ALU.add)

    def mixed_to_T(self, mst, bmst, slot):
        for half in range(2):
            pT, bT = self.bank("T")
            for tt in range(8):
                self.tr(pT[:, tt * 128:(tt + 1) * 128], mst[:, half * 8 + tt, :], [bmst], [bT], inc=(tt == 7))
            self.op(self.ACT, "activation", [bT], [self.b_mixT], out=self.mixT[:, slot, half * 1024:(half + 1) * 1024], in_=pT, func=AF.Copy)

    def odd(self, i):
        A, V, G = self.ACT, self.DVE, self.POOL
        layer = 2 * i + 1
        dr = self.dr
        bc = self.b_const
        win = dr["w_in_odd"][i]
        self.norm_to_hT(self.anorm[:, layer, :])
        self.fw.barrier()
        qT = [(self.view(A_R1 + 4096 * k, [2048], BF16), Buf("qT%d" % k)) for k in range(4)]
        kT = [(self.view(A_R3 + 4096 * k, [2048], BF16), Buf("kT%d" % k)) for k in range(4)]
        o = A_R3 + 16384
        vaug = [(self.view(o + 4224 * k, [16, 2, 66], BF16), Buf("v%d" % k)) for k in range(2)]
        o += 8448
        stgq, bsq = self.view(o, [16, 2, 72], BF16), Buf("stgq")
        o += 4608
        stgk, bsk = self.view(o, [16, 2, 72], BF16), Buf("stgk")
        o += 4608
        s32, bs32 = self.view(o, [512], F32), Buf("s32")
        o += 2048
        assert o <= A_R3 + 36864
        sp = A_HT + 32768
        PT = [(self.view(sp + 1024 * k, [512], BF16), Buf("PT%d" % k)) for k in range(4)]
        mst, bmst = self.view(sp + 4096, [16, 128], BF16), Buf("mst")
        tmp, btmp = self.view(A_CKV, [4, 2, 16], F32), Buf("ropetmp")
        gt, bgt = self.view(A_CKV + 1024, [8, 2, 8], F32), Buf("gt")
        rank, brk = self.view(A_CKV + 2048, [8, 2, 8], F32), Buf("rank")
        cmp_, bcmp = self.view(A_CKV + 3072, [8, 2, 8], F32), Buf("cmp")
        for va, bva in vaug:
            self.op(G, "memset", [], [bva], ap=va[:, :, :, 64:66], constant=1.0)
        self.op(G, "memset", [], [bsq], ap=stgq[:, :, :, 64:72], constant=0.0)
        self.op(G, "memset", [], [bsk], ap=stgk[:, :, :, 64:72], constant=0.0)
        for b in range(8):
            self.op(G, "memset", [], [bsk], ap=stgk[:, 2 * b:2 * b + 2, :, 64 + b:65 + b], constant=1.0)
        for qa, bqa in qT:
            self.op(G, "memset", [], [bqa], ap=qa[64:128, :], constant=0.0)
        for ka_, bka_ in kT:
            self.op(G, "memset", [], [bka_], ap=ka_[64:128, :], constant=0.0)
        def load_pair(hp):
            wqk, bwqk = self.wslot()
            wv, bwv = self.wslot()
            self.dma(G, wqk[:, :, 0:128], win[:, hp * 128:(hp + 1) * 128].rearrange("(c p) n -> p c n", p=128), writes=[bwqk])
            self.dma(G, wqk[:, :, 128:256], win[:, 1024 + hp * 128:1024 + (hp + 1) * 128].rearrange("(c p) n -> p c n", p=128), writes=[bwqk])
            self.dma(G, wv[:, :, 0:128], win[:, 2048 + hp * 128:2048 + (hp + 1) * 128].rearrange("(c p) n -> p c n", p=128), writes=[bwv])
            return wqk, bwqk, wv, bwv
        wts = {0: load_pair(0)}

        def prep1(hp):
            st_ = hp % 2
            wqk, bwqk, wv, bwv = wts[hp]
            va, bva = vaug[st_]
            for tb in range(4):
                for which, stg, bstg in ((0, stgq, bsq), (1, stgk, bsk)):
                    pA, bA = self.bank("A")
                    for tt in range(4):
                        t = 4 * tb + tt
                        for c in range(8):
                            self.mm(pA[:, tt * 128:(tt + 1) * 128], self.hT[:, c, t * 128:(t + 1) * 128], wqk[:, c, which * 128:(which + 1) * 128],
                                    c == 0, c == 7, [self.b_hT, bwqk], [bA], inc=(tt == 3 and c == 7))
                    self.op(A, "activation", [bA], [bs32], out=s32, in_=pA, func=AF.Copy)
                    s4 = s32.rearrange("p (a h d) -> p a h d", h=2, d=64)
                    self.rope_inplace(V, s4[:, :, :, 0:16], tmp, 4 * tb, 4, 2, [bs32], [bs32], btmp)
                    self.op(G, "tensor_copy", [bs32], [bstg], out=stg[:, 4 * tb:4 * tb + 4, :, 0:64], in_=s4)
                pA, bA = self.bank("A")
                for tt in range(4):
                    t = 4 * tb + tt
                    for c in range(8):
                        self.mm(pA[:, tt * 128:(tt + 1) * 128], self.hT[:, c, t * 128:(t + 1) * 128], wv[:, c, 0:128],
                                c == 0, c == 7, [self.b_hT, bwv], [bA], inc=(tt == 3 and c == 7))
                self.op(A, "activation", [bA], [bva], out=va[:, 4 * tb:4 * tb + 4, :, 0:64],
                        in_=pA.rearrange("p (a h d) -> p a h d", h=2, d=64), func=AF.Copy)
            if hp < 7:
                wts[hp + 1] = load_pair(hp + 1)
            for hh in range(2):
                qa, bqa = qT[2 * st_ + hh]
                ka, bka = kT[2 * st_ + hh]
                for half in range(2):
                    pT, bT = self.bank("T")
                    for tt in range(8):
                        self.tr(pT[0:64, tt * 128:(tt + 1) * 128], stgq[:, half * 8 + tt, hh, 0:64], [bsq], [bT], inc=(tt == 7))
                    self.op(A, "activation", [bT], [bqa], out=qa[0:64, half * 1024:(half + 1) * 1024], in_=pT[0:64, :], func=AF.Copy)
                    pT, bT = self.bank("T")
                    for tt in range(8):
                        self.tr(pT[0:72, tt * 128:(tt + 1) * 128], stgk[:, half * 8 + tt, hh, 0:72], [bsk], [bT], inc=(tt == 7))
                    self.op(V, "tensor_copy", [bT], [bka], out=ka[0:72, half * 1024:(half + 1) * 1024], in_=pT[0:72, :])
                self.op(V, "tensor_reduce", [bka], [self.b_km], out=self.kmT32[0:64, hh, :],
                        in_=ka[0:64, :].rearrange("p (n k) -> p n k", k=256), axis=AX.X, op=ALU.add)
            self.op(V, "tensor_scalar", [self.b_km], [self.b_km], out=self.kmT[0:64, :, :], in0=self.kmT32[0:64, :, :], scalar1=1.0 / 256, scalar2=None, op0=ALU.mult)
            pA, bA = self.bank("A")
            for t in range(8, 16):
                for hh in range(2):
                    qa, bqa = qT[2 * st_ + hh]
                    c0 = ((t - 8) * 2 + hh) * 8
                    self.mm(pA[:, c0:c0 + 8], qa[0:64, t * 128:(t + 1) * 128], self.kmT[0:64, hh, :], True, True, [bqa, self.b_km], [bA],
                            inc=(t == 15 and hh == 1))
            self.op(V, "tensor_copy", [bA], [bgt], out=gt, in_=pA[:, 0:128].rearrange("p (a h n) -> p a h n", h=2, n=8))
            for n1 in range(7):
                a0 = max(0, n1 - 3) * 2 if n1 >= 4 else 0
                na = 8 - a0
                dst = rank if n1 == 0 else cmp_
                bd = brk if n1 == 0 else bcmp
                self.op(V, "tensor_tensor", [bgt], [bd], out=dst[:, a0:8, :, :], in0=gt[:, a0:8, :, n1:n1 + 1].to_broadcast([128, na, 2, 8]),
                        in1=gt[:, a0:8, :, :], op=ALU.is_gt)
                if n1 > 0:
                    self.op(V, "tensor_tensor", [brk, bcmp], [brk], out=rank[:, a0:8, :, :], in0=rank[:, a0:8, :, :], in1=cmp_[:, a0:8, :, :], op=ALU.add)
            for t in range(8, 16):
                ob = t // 2
                self.op(V, "tensor_scalar", [brk], [bsq], out=stgq[:, t, :, 64:64 + ob], in0=rank[:, t - 8, :, 0:ob], scalar1=2.5, scalar2=NEG,
                        op0=ALU.is_gt, op1=ALU.mult)

        def prep2(hp):
            st_ = hp % 2
            for hh in range(2):
                qa, bqa = qT[2 * st_ + hh]
                pT, bT = self.bank("T")
                for tt in range(8):
                    self.tr(pT[0:72, tt * 128:(tt + 1) * 128], stgq[:, 8 + tt, hh, 0:72], [bsq], [bT], inc=(tt == 7))
                self.op(A, "activation", [bT], [bqa], out=qa[64:72, 1024:2048], in_=pT[64:72, :], func=AF.Copy)

        def att(hp):
            st_ = hp % 2
            va, bva = vaug[st_]
            for hh in range(2):
                qa, bqa = qT[2 * st_ + hh]
                ka, bka = kT[2 * st_ + hh]
                self.attend(qa, ka, slice(0, 128), [bqa, bka], va[:, :, hh, :], [bva], mst, bmst, hh * 64, PT, "tri")
            self.mixed_to_T(mst, bmst, hp % 4)
            if hp % 4 == 3:
                self.outproj_half(dr["w_out_odd"][i], hp // 4)

        prep1(0)
        prep2(0)
        for hp in range(8):
            if hp < 7:
                prep1(hp + 1)
            att(hp)
            if hp < 7:
                prep2(hp + 1)
        self.fw.barrier()

    def even(self, i):
        A, V, G = self.ACT, self.DVE, self.POOL
        layer = 2 * i
        dr = self.dr
        bc = self.b_const
        win = dr["w_in_even"][i]
        wout = dr["w_out_even"][i]

        def wcols(c0, n):
            return win[:, c0:c0 + n].rearrange("(c p) n -> p c n", p=128)
        self.norm_to_hT(self.anorm[:, layer, :])
        self.fw.barrier()
        kpad = [(self.view(A_R1 + 4096 * k, [2048], BF16), Buf("fk%d" % k)) for k in range(4)]
        for k_, (ka_, bka_) in enumerate(kpad):
            zr = slice(64, 128) if k_ % 2 == 0 else slice(0, 64)
            self.op(G, "memset", [], [bka_], ap=ka_[zr, :], constant=0.0)
        o = A_R3
        fq = [(self.view(o + 4096 * k, [2048], BF16), Buf("fq%d" % k)) for k in range(2)]
        o += 8192
        vaug = [(self.view(o + 4224 * k, [16, 2, 66], BF16), Buf("v%d" % k)) for k in range(2)]
        o += 8448
        PT = [(self.view(o + 1024 * k, [512], BF16), Buf("PT%d" % k)) for k in range(4)]
        o += 4096
        mst, bmst = self.view(o, [16, 128], BF16), Buf("mst")
        o += 4096
        zf, zf2, bzf = self.view(o, [16, 8], F32), self.view(o + 512, [16, 8], F32), Buf("zf")
        o += 1024
        for va, bva in vaug:
            self.op(G, "memset", [], [bva], ap=va[:, :, :, 64:66], constant=1.0)
        wff, bwff = self.wslot()
        self.dma(G, wff[:, :, 0:128], wcols(1536, 128), writes=[bwff])
        pA, bA = self.bank("A")
        for t in range(16):
            for c in range(8):
                self.mm(pA[:, t * 8:(t + 1) * 8], self.hT[:, c, t * 128:(t + 1) * 128], wff[:, c, 0:8], c == 0, c == 7, [self.b_hT, bwff], [bA],
                        inc=(t == 15 and c == 7))
        self.op(V, "tensor_tensor", [bA, bc], [bzf], out=zf, in0=pA[:, 0:128].rearrange("p (t h) -> p t h", h=8),
                in1=self.bfox[:, i, :].unsqueeze(1).to_broadcast([128, 16, 8]), op=ALU.add)
        self.op(V, "tensor_scalar", [bzf], [bzf], out=zf2, in0=zf, scalar1=-1.0, scalar2=None, op0=ALU.mult)
        self.op(V, "tensor_tensor", [bzf], [bzf], out=zf2, in0=zf2, in1=zf, op=ALU.min)
        self.op(A, "activation", [bzf], [bzf], out=zf2, in_=zf2, func=AF.Exp)
        self.op(A, "activation", [bzf], [bzf], out=zf2, in_=zf2, func=AF.Ln, bias=1.0)
        self.op(V, "tensor_scalar", [bzf], [bzf], out=zf, in0=zf, scalar1=0.0, scalar2=None, op0=ALU.min)
        self.op(V, "tensor_tensor", [bzf], [bzf], out=zf, in0=zf, in1=zf2, op=ALU.subtract)
        pC, bC = self.bank("A")
        for t in range(16):
            for j in range(t):
                self.mm(pC[:, t * 8:(t + 1) * 8], self.ones32, zf[:, j, :], j == 0, False, [bzf, bc], [bC], inc=False)
            self.mm(pC[:, t * 8:(t + 1) * 8], self.U32, zf[:, t, :], t == 0, True, [bzf, bc], [bC], inc=False)
        for qc in range(4):
            for j in range(4 * qc + 4):
                self.mm(pC[:, 128 + qc * 8:128 + (qc + 1) * 8], self.ones32, zf[:, j, :], j == 0, j == 4 * qc + 3, [bzf, bc], [bC],
                        inc=(qc == 3 and j == 15))
        self.op(V, "tensor_copy", [bC], [self.b_fb], out=self.csb, in_=pC[:, 0:128].rearrange("p (t h) -> p t h", h=8))
        self.op(V, "tensor_copy", [bC], [self.b_fb], out=self.cref, in_=pC[:, 128:160].rearrange("p (t h) -> p t h", h=8))
        self.op(V, "tensor_tensor", [self.b_fb], [self.b_fb], out=self.fb, in0=self.cref.unsqueeze(1).to_broadcast([128, 16, 4, 8]),
                in1=self.csb.unsqueeze(2).to_broadcast([128, 16, 4, 8]), op=ALU.subtract)
        if STOP == "gate":
            self.fw.barrier()
            return
        rtm, brtm = self.view(o, [16, 8], BF16), Buf("rtm")
        o += 256
        selT, bsel = self.view(o, [8, 128], BF16), Buf("selT")
        o += 2048
        rT, brT = self.view(o, [2048], BF16), Buf("rT")
        o += 4096
        self.dma(G, selT, dr["c_sel"].rearrange("p (h k) -> p h k", k=128), writes=[bsel])
        self.op(G, "memset", [], [brT], ap=rT, constant=0.0)
        self.op(V, "tensor_tensor", [self.b_fb], [brtm], out=rtm.rearrange("p (q t) h -> p q t h", t=4),
                in0=self.csb.rearrange("p (q t) h -> p q t h", t=4),
                in1=self.cref.unsqueeze(2).to_broadcast([128, 4, 4, 8]), op=ALU.subtract)
        self.op(V, "tensor_scalar", [brtm], [brtm], out=rtm, in0=rtm, scalar1=8.0, scalar2=None, op0=ALU.mult)
        for qc in range(4):
            pR, bR = self.bank("A")
            for tt in range(4):
                self.mm(pR[0:8, tt * 128:(tt + 1) * 128], rtm[:, 4 * qc + tt, :], self.ident, True, True, [brtm, bc], [bR], inc=(tt == 3))
            self.op(V, "tensor_copy", [bR], [brT], out=rT[0:8, qc * 512:(qc + 1) * 512], in_=pR[0:8, :])
        def load_fox(hp):
            wqk, bwqk = self.wslot()
            wv, bwv = self.wslot()
            self.dma(G, wqk[:, :, 0:128], wcols(hp * 128, 128), writes=[bwqk])
            self.dma(G, wqk[:, :, 128:256], wcols(512 + hp * 128, 128), writes=[bwqk])
            self.dma(G, wv[:, :, 0:128], wcols(1024 + hp * 128, 128), writes=[bwv])
            return wqk, bwqk, wv, bwv
        nxt = load_fox(0)
        for hp in range(4):
            st_ = hp % 2
            qa, bqa = fq[st_]
            kp = [kpad[2 * st_], kpad[2 * st_ + 1]]
            va, bva = vaug[st_]
            wqk, bwqk, wv, bwv = nxt
            for which in (0, 1):
                for tc in range(4):
                    pA, bA = self.bank("A")
                    tsl = slice(tc * 512, (tc + 1) * 512)
                    for c in range(8):
                        self.mm(pA, wqk[:, c, which * 128:(which + 1) * 128], self.hT[:, c, tsl], c == 0, c == 7,
                                [self.b_hT, bwqk], [bA], inc=(c == 7))
                    if which == 0:
                        self.op(A, "activation", [bA], [bqa], out=qa[:, tsl], in_=pA, func=AF.Copy)
                    else:
                        self.op(V, "tensor_copy", [bA], [kp[0][1]], out=kp[0][0][0:64, tsl], in_=pA[0:64, :])
                        self.op(V, "tensor_copy", [bA], [kp[1][1]], out=kp[1][0][64:128, tsl], in_=pA[64:128, :])
            for tb in range(4):
                pA, bA = self.bank("A")
                for tt in range(4):
                    t = 4 * tb + tt
                    for c in range(8):
                        self.mm(pA[:, tt * 128:(tt + 1) * 128], self.hT[:, c, t * 128:(t + 1) * 128], wv[:, c, 0:128],
                                c == 0, c == 7, [self.b_hT, bwv], [bA], inc=(tt == 3 and c == 7))
                self.op(A, "activation", [bA], [bva], out=va[:, 4 * tb:4 * tb + 4, :, 0:64],
                        in_=pA.rearrange("p (a h d) -> p a h d", h=2, d=64), func=AF.Copy)
            if hp < 3:
                nxt = load_fox(hp + 1)
            for hh in range(2):
                h = 2 * hp + hh
                self.attend(qa, kp[hh][0], slice(0, 128), [bqa, kp[hh][1]], va[:, :, hh, :], [bva], mst, bmst, hh * 64, PT, "fox",
                            bias=lambda j, qc, h=h: self.fb[:, j, qc, h:h + 1],
                            qbias=lambda c0, n, h=h: (selT[:, h, :], rT[:, c0:c0 + n], [bsel, brT]))
            self.mixed_to_T(mst, bmst, hp)
        self.outproj_half(wout, 0)
        self.fw.barrier()
        if STOP == "fox":
            return
        dqT = self.view(A_R1, [4, 2048], BF16)
        b_dqT = Buf("dqT")
        iqT, b_iqT = self.view((A_MIX + 8192) if os.environ.get("IQ_MIX") else (A_R3 + 24576), [2, 2048], BF16), Buf("iqT")
        ikT2, b_ikT = self.view(A_R3 + 32768, [2048], BF16), Buf("ikT2")
        s32s = [(self.view(A_R3 + 2048 * k, [512], F32), Buf("s32_%d" % k)) for k in range(2)]
        stgs = [(self.view(A_R3 + 4096 + 1088 * k, [528], BF16), Buf("stg%d" % k)) for k in range(2)]
        tmps = [(self.view(A_R3 + 6272 + 1024 * k, [4, 4, 16], F32), Buf("ropetmp%d" % k)) for k in range(2)]
        wA, bwA = self.wslot()
        wB, bwB = self.wslot()
        self.dma(G, wA, wcols(2056, 256), writes=[bwA])
        self.dma(G, wB[:, :, 0:212], wcols(2312, 212), writes=[bwB])
        bst = self.b_st
        for t in range(16):
            tsl = slice(t * 128, (t + 1) * 128)
            s32, bs32 = s32s[t % 2]
            stg, bstg = stgs[t % 2]
            tmp, btmp = tmps[t % 2]
            pA, bA = self.bank("A")
            for c in range(8):
                self.mm(pA[:, 0:256], self.hT[:, c, tsl], wA[:, c, :], c == 0, c == 7, [self.b_hT, bwA], [bA], inc=False)
            for c in range(8):
                self.mm(pA[:, 256:468], self.hT[:, c, tsl], wB[:, c, 0:212], c == 0, c == 7, [self.b_hT, bwB], [bA], inc=(c == 7))
            self.op(A, "activation", [bA], [bs32], out=s32[:, 0:468], in_=pA[:, 0:468], func=AF.Copy)
            if STOP == "A0":
                continue
            self.op(A, "activation", [bs32], [bstg, bst], out=stg[:, 0:128], in_=s32[:, 0:128], func=AF.Square, accum_out=self.st[:, 8:9])
            self.op(A, "activation", [bst], [bst], out=self.st[:, 9:10], in_=self.st[:, 8:9], func=AF.Sqrt, scale=1.0 / 128, bias=EPS)
            self.op(V, "reciprocal", [bst], [bst], out=self.st[:, 10:11], in_=self.st[:, 9:10])
            self.op(V, "tensor_scalar", [bs32, bst], [bstg], out=stg[:, 0:128], in0=s32[:, 0:128], scalar1=self.st[:, 10:11], scalar2=None, op0=ALU.mult)
            self.rope_inplace(V, s32[:, 128:144].rearrange("p (a h d) -> p a h d", a=1, h=1), tmp[:, 0:1, 0:1, :], t, 1, 1, [bs32], [bs32], btmp)
            t5 = tmp.rearrange("p a h d -> p (a h d)")[:, 0:80].rearrange("p (a h d) -> p a h d", a=1, h=5)
            self.rope_inplace(V, s32[:, 144:464].rearrange("p (a h d) -> p a h d", a=1, h=5)[:, :, :, 0:16], t5, t, 1, 5, [bs32], [bs32], btmp)
            if STOP == "A0b":
                continue
            self.op(G, "tensor_copy", [bs32], [bstg], out=stg[:, 128:448], in_=s32[:, 144:464])
            self.op(G, "tensor_copy", [bs32], [bstg], out=stg[:, 448:512], in_=s32[:, 400:464])
            self.op(G, "tensor_copy", [bs32], [self.b_dkr], out=self.dkr[:, t, :], in_=s32[:, 128:144])
            self.op(G, "tensor_scalar", [bs32], [self.b_dkr], out=self.iw[:, t, :], in0=s32[:, 464:468], scalar1=IDX_SCALE, scalar2=None, op0=ALU.mult)
            if STOP == "A0c":
                continue
            pT, bT = self.bank("T")
            self.tr(pT[:, 0:128], stg[:, 0:128], [bstg], [bT], inc=False)
            self.tr(pT[:, 128:256], stg[:, 384:512], [bstg], [bT], inc=True)
            self.op(A, "activation", [bT, bc], [self.b_ckvT], out=self.ckvT[:, tsl], in_=pT[:, 0:128], func=AF.Identity, scale=self.gckv[:, i:i + 1])
            self.op(A, "activation", [bT], [b_ikT], out=ikT2[:, tsl], in_=pT[:, 128:256], func=AF.Copy)
            pT, bT = self.bank("T")
            self.tr(pT[:, 0:128], stg[:, 128:256], [bstg], [bT], inc=False)
            self.tr(pT[:, 128:256], stg[:, 256:384], [bstg], [bT], inc=True)
            self.op(V, "tensor_copy", [bT], [b_iqT], out=iqT[:, 0, tsl], in_=pT[:, 0:128])
            self.op(V, "tensor_copy", [bT], [b_iqT], out=iqT[:, 1, tsl], in_=pT[:, 128:256])
        if STOP == "A1":
            self.fw.barrier()
            return
        stqs = [(self.view(A_R3 + 8320 + 1024 * k, [4, 128], BF16), Buf("stq%d" % k)) for k in range(2)]
        wqs = []
        for hp in range(4):
            wq, bwq = self.wslot()
            self.dma(G, wq[:, :, 0:128], wcols(1544 + hp * 128, 128), writes=[bwq])
            wqs.append((wq, bwq))
        kk = 0
        for hp in range(4):
            wq, bwq = wqs[hp]
            for tb in range(4):
                s32, bs32 = s32s[kk % 2]
                tmp, btmp = tmps[kk % 2]
                stq, bstq = stqs[kk % 2]
                kk += 1
                pA, bA = self.bank("A")
                for tt in range(4):
                    t = 4 * tb + tt
                    for c in range(8):
                        self.mm(pA[:, tt * 128:(tt + 1) * 128], self.hT[:, c, t * 128:(t + 1) * 128], wq[:, c, 0:128],
                                c == 0, c == 7, [self.b_hT, bwq], [bA], inc=(tt == 3 and c == 7))
                self.op(A, "activation", [bA], [bs32], out=s32, in_=pA, func=AF.Copy)
                s4 = s32.rearrange("p (a h d) -> p a h d", h=2, d=64)
                self.rope_inplace(V, s4[:, :, :, 0:16], tmp[:, :, 0:2, :], 4 * tb, 4, 2, [bs32], [bs32], btmp)
                self.op(G, "tensor_copy", [bs32], [bstq], out=stq, in_=s32.rearrange("p (a k) -> p a k", k=128))
                pT, bT = self.bank("T")
                for tt in range(4):
                    self.tr(pT[:, tt * 128:(tt + 1) * 128], stq[:, tt, :], [bstq], [bT], inc=(tt == 3))
                self.op(A, "activation", [bT], [b_dqT], out=dqT[:, hp, tb * 512:(tb + 1) * 512], in_=pT[:, 0:512], func=AF.Copy)
        self.fw.barrier()
        if STOP == "A":
            return
        maskT = self.view(A_HT, [40, 512], BF16)
        b_mask = Buf("maskT")
        scs = [(self.view(A_R3 + 8192 * k, [2048], F32), Buf("sc%d" % k)) for k in range(3)]
        mbs = [(self.view(A_MIX + 4096 + 4096 * k, [2048], BF16), Buf("mb%d" % k)) for k in range(3)]
        rls = [(self.view(A_MIX + 2048 * k, [512], F32), Buf("rl%d" % k)) for k in range(2)]
        bbiss = [Buf("bis0"), Buf("bis1"), Buf("bis2")]
        self.op(V, "tensor_copy", [bc], [b_mask], out=maskT[:, 0, 0:128], in_=self.triT)
        self.op(V, "memset", [], [b_mask], ap=maskT[:, 0, 128:256], constant=0.0)
        self.op(V, "tensor_copy", [bc], [b_mask], out=maskT[:, 1, 128:256], in_=self.triT)
        rkc = [0]

        def scores(qi):
            sc, bsc = scs[qi % 3]
            nk = (qi + 1) * 128
            qsl = slice(qi * 128, (qi + 1) * 128)
            for h in range(4):
                rows = slice(64 * (h % 2), 64 * (h % 2) + 64)
                for kc in range((nk + 511) // 512):
                    n = min(512, nk - kc * 512)
                    ksl = slice(kc * 512, kc * 512 + n)
                    pS, bS = self.bank("S")
                    self.mm(pS[:, 0:n], iqT[rows, h // 2, qsl], ikT2[rows, ksl], True, True, [b_iqT, b_ikT], [bS])
                    rl, brl = rls[rkc[0] % 2]
                    rkc[0] += 1
                    self.op(A, "activation", [bS], [brl], out=rl[:, 0:n], in_=pS[:, 0:n], func=AF.Relu)
                    if h == 0:
                        self.op(V, "tensor_scalar", [brl, self.b_dkr], [bsc], out=sc[:, ksl], in0=rl[:, 0:n], scalar1=self.iw[:, qi, 0:1], scalar2=None, op0=ALU.mult)
                    else:
                        self.op(V, "scalar_tensor_tensor", [brl, self.b_dkr, bsc], [bsc], out=sc[:, ksl], in0=rl[:, 0:n], scalar=self.iw[:, qi, h:h + 1],
                                in1=sc[:, ksl], op0=ALU.mult, op1=ALU.add)
            self.op(V, "tensor_tensor", [bsc, bc], [bsc], out=sc[:, qsl], in0=sc[:, qsl], in1=self.tri32, op=ALU.add)
            bis = self.bis[:, qi % 3, :]
            bbis = bbiss[qi % 3]
            self.op(V, "tensor_reduce", [bsc], [bbis], out=bis[:, 20:21], in_=sc[:, 0:nk], axis=AX.X, op=ALU.max)
            self.op(V, "tensor_reduce", [bsc], [bbis], out=bis[:, 21:22], in_=sc[:, 0:qi * 128], axis=AX.X, op=ALU.min)
            self.op(V, "tensor_tensor", [bbis], [bbis], out=bis[:, 22:23], in0=bis[:, 20:21], in1=bis[:, 21:22], op=ALU.subtract)
            if qi % 2 == 0:
                self.op(V, "tensor_scalar", [bbis, bc], [bbis], out=bis[:, 0:NIT], in0=self.pow2, scalar1=bis[:, 22:23], scalar2=None, op0=ALU.mult)
                self.op(V, "scalar_tensor_tensor", [bbis], [bbis], out=bis[:, 25:26], in0=bis[:, 22:23], scalar=0.5, in1=bis[:, 21:22], op0=ALU.mult, op1=ALU.add)
            else:
                self.op(V, "tensor_scalar", [bbis, bc], [bbis], out=bis[:, 0:NIT], in0=self.pow2, scalar1=bis[:, 22:23], scalar2=-0.5, op0=ALU.mult, op1=ALU.mult)
                self.op(V, "scalar_tensor_tensor", [bbis], [bbis], out=bis[:, 25:26], in0=bis[:, 22:23], scalar=-0.5, in1=bis[:, 21:22], op0=ALU.mult, op1=ALU.subtract)

        def bisect(qi):
            sc, bsc = scs[qi % 3]
            junk, bjunk = mbs[qi % 3]
            bis = self.bis[:, qi % 3, :]
            bbis = bbiss[qi % 3]
            nk = (qi + 1) * 128
            on_act = (qi % 2 == 1)
            if not on_act:
                for k in range(NIT):
                    self.op(V, "tensor_scalar", [bsc, bbis], [bjunk, bbis], out=junk[:, 0:nk], in0=sc[:, 0:nk], scalar1=bis[:, 25:26], scalar2=None,
                            op0=ALU.is_ge, op1=ALU.add, accum_out=bis[:, 23:24])
                    last = (k == NIT - 1)
                    self.op(V, "tensor_scalar", [bbis], [bbis], out=bis[:, 24:25], in0=bis[:, 23:24], scalar1=255.5, scalar2=(-1.0 if last else -0.5),
                            op0=ALU.is_ge, op1=ALU.add)
                    self.op(V, "scalar_tensor_tensor", [bbis], [bbis], out=(bis[:, 26:27] if last else bis[:, 25:26]), in0=bis[:, 24:25],
                            scalar=bis[:, k:k + 1], in1=bis[:, 25:26], op0=ALU.mult, op1=ALU.add)
            else:
                cur, oth = 25, 28
                for k in range(NIT):
                    self.op(A, "activation", [bsc, bbis], [bjunk, bbis], out=junk[:, 0:nk], in_=sc[:, 0:nk], func=AF.Sign, bias=bis[:, cur:cur + 1],
                            accum_out=bis[:, 23:24])
                    self.op(A, "activation", [bbis], [bbis], out=bis[:, 24:25], in_=bis[:, 23:24], func=AF.Sign, bias=float(nk - 511))
                    last = (k == NIT - 1)
                    dst = 27 if last else oth
                    self.op(A, "activation", [bbis], [bbis], out=bis[:, dst:dst + 1], in_=bis[:, 24:25], func=AF.Identity, scale=bis[:, k:k + 1],
                            bias=bis[:, cur:cur + 1])
                    cur, oth = oth, cur

        def finish(qi):
            sc, bsc = scs[qi % 3]
            mb, bmb = mbs[qi % 3]
            bis = self.bis[:, qi % 3, :]
            bbis = bbiss[qi % 3]
            nk = (qi + 1) * 128
            if qi % 2 == 1:
                self.op(V, "tensor_tensor", [bbis], [bbis], out=bis[:, 26:27], in0=bis[:, NIT - 1:NIT], in1=bis[:, 27:28], op=ALU.subtract)
            self.op(V, "tensor_scalar", [bsc, bbis], [bmb], out=mb[:, 0:nk], in0=sc[:, 0:nk], scalar1=bis[:, 26:27], scalar2=NEG, op0=ALU.is_lt, op1=ALU.mult)
            qc = qi // 4
            for j0 in range(0, qi + 1, 8):
                nb = min(8, qi + 1 - j0)
                pT, bT = self.bank("T")
                for jj in range(nb):
                    self.tr(pT[:, jj * 128:(jj + 1) * 128], mb[:, (j0 + jj) * 128:(j0 + jj + 1) * 128], [bmb], [bT], inc=(jj == nb - 1))
                self.op(V, "tensor_copy", [bT], [b_mask], out=maskT[:, MOFF[qc] + j0:MOFF[qc] + j0 + nb, (qi % 4) * 128:(qi % 4 + 1) * 128],
                        in_=pT[:, 0:nb * 128].rearrange("p (a k) -> p a k", k=128))

        scores(2)
        for qi in range(2, 16):
            if qi + 1 <= 15:
                scores(qi + 1)
            bisect(qi)
            if qi - 1 >= 2:
                finish(qi - 1)
        finish(15)
        self.fw.barrier()
        if STOP == "B":
            return
        dk = [(self.view(A_R3 + 4096 * k, [2048], BF16), Buf("dk%d" % k)) for k in range(4)]
        for k_, (ka_, bka_) in enumerate(dk):
            zr = slice(64, 128) if k_ % 2 == 0 else slice(0, 64)
            self.op(G, "memset", [], [bka_], ap=ka_[zr, :], constant=0.0)
        o = A_R3 + 16384
        vaug = [(self.view(o + 4224 * k, [16, 2, 66], BF16), Buf("v%d" % k)) for k in range(2)]
        o += 8448
        PT = [(self.view(o + 1024 * k, [512], BF16), Buf("PT%d" % k)) for k in range(4)]
        o += 4096
        mst, bmst = self.view(o, [16, 128], BF16), Buf("mst")
        o += 4096
        stk, bstk = self.view(o, [4, 2, 64], BF16), Buf("stk")
        o += 1024
        for va, bva in vaug:
            self.op(G, "memset", [], [bva], ap=va[:, :, :, 64:66], constant=1.0)
        wuk, bwuk = self.wslot((2048,))
        wuv, bwuv = self.wslot((2048,))
        self.dma(G, wuk[:, 0:384], dr["w_uk"][i], writes=[bwuk])
        self.dma(G, wuv[:, 0:512], dr["w_uv"][i], writes=[bwuv])
        for hp in range(4):
            st_ = hp % 2
            kp = [dk[2 * st_], dk[2 * st_ + 1]]
            va, bva = vaug[st_]
            for tb in range(4):
                pA, bA = self.bank("A")
                pB, bB = self.bank("A")
                for tt in range(4):
                    tsl = slice((4 * tb + tt) * 128, (4 * tb + tt + 1) * 128)
                    self.mm(pA[:, tt * 96:(tt + 1) * 96], self.ckvT[:, tsl], wuk[:, hp * 96:(hp + 1) * 96], True, True, [self.b_ckvT, bwuk], [bA], inc=(tt == 3))
                for tt in range(4):
                    tsl = slice((4 * tb + tt) * 128, (4 * tb + tt + 1) * 128)
                    self.mm(pB[:, tt * 128:(tt + 1) * 128], self.ckvT[:, tsl], wuv[:, hp * 128:(hp + 1) * 128], True, True, [self.b_ckvT, bwuv], [bB], inc=(tt == 3))
                self.op(A, "activation", [bA], [bstk], out=stk[:, :, :, 16:64], in_=pA[:, 0:384].rearrange("p (a h d) -> p a h d", h=2, d=48), func=AF.Copy)
                self.op(G, "tensor_copy", [self.b_dkr], [bstk], out=stk[:, :, :, 0:16], in_=self.dkr[:, 4 * tb:4 * tb + 4, :].unsqueeze(2).to_broadcast([128, 4, 2, 16]))
                self.op(A, "activation", [bB], [bva], out=va[:, 4 * tb:4 * tb + 4, :, 0:64], in_=pB.rearrange("p (a h d) -> p a h d", h=2, d=64), func=AF.Copy)
                pT, bT = self.bank("T")
                for tt in range(4):
                    self.tr(pT[:, tt * 128:(tt + 1) * 128], stk[:, tt, :, :].rearrange("p h d -> p (h d)"), [bstk], [bT], inc=(tt == 3))
                self.op(V, "tensor_copy", [bT], [kp[0][1]], out=kp[0][0][0:64, tb * 512:(tb + 1) * 512], in_=pT[0:64, 0:512])
                self.op(V, "tensor_copy", [bT], [kp[1][1]], out=kp[1][0][64:128, tb * 512:(tb + 1) * 512], in_=pT[64:128, 0:512])
            for hh in range(2):
                self.attend(dqT[:, hp, :], kp[hh][0], slice(0, 128), [b_dqT, kp[hh][1]], va[:, :, hh, :], [bva], mst, bmst, hh * 64, PT, "dsa",
                            mask=lambda qc, j, c0, N: (maskT[:, MOFF[qc] + j, c0:c0 + N], b_mask))
            self.mixed_to_T(mst, bmst, hp)
        self.outproj_half(wout, 1)
        self.fw.barrier()

    def final_norm(self, y):
        A, V = self.ACT, self.DVE
        junk = (self.view(A_R3 + 4096, [1024], BF16), Buf("junk"))
        ot = [(self.view(A_MIX + 4096 * i, [1024], F32), Buf("ot%d" % i)) for i in range(2)]
        bst = self.b_st
        yr = y.rearrange("(t p) d -> p t d", p=128)
        self.gfin = self.view(A_MIX + 8192, [1024], F32)
        self.dma(self.SP, self.gfin, self.dr["final_norm"].partition_broadcast(128), writes=[self.b_const])
        for t in range(16):
            self.op(A, "activation", [self.bx[t]], [junk[1], bst], out=junk[0], in_=self.x[:, t, :], func=AF.Square,
                    accum_out=self.ss[:, t:t + 1])
        self.op(A, "activation", [bst], [bst], out=self.rstd, in_=self.ss, func=AF.Sqrt, scale=1.0 / D, bias=EPS)
        self.op(V, "reciprocal", [bst], [bst], out=self.rstd, in_=self.rstd)
        outs = []
        for t in range(16):
            o, bo = ot[t % 2]
            self.op(V, "scalar_tensor_tensor", [self.bx[t], bst, self.b_const], [bo], out=o, in0=self.x[:, t, :],
                    scalar=self.rstd[:, t:t + 1], in1=self.gfin, op0=ALU.mult, op1=ALU.mult)
            b = Buf("y%d" % t)
            self.dma(self.SP, yr[:, t, :], o, reads=[bo], writes=[b])
            outs.append(b)
        return outs

    def store_x(self, y):
        yr = y.rearrange("(t p) d -> p t d", p=128)
        outs = []
        for t in range(16):
            b = Buf("y%d" % t)
            self.dma(self.SP, yr[:, t, :], self.x[:, t, :], reads=[self.bx[t]], writes=[b])
            outs.append(b)
        return outs


IN_SPECS = [
    ("x", [S, D], F32), ("positions", [S], I32), ("attn_norm", [4, D], F32), ("ffn_norm", [4, D], F32),
    ("final_norm", [D], F32), ("w_in_even", [2, D, EVEN_IN], F32), ("b_fox_f", [2, 8], F32), ("g_ckv", [2, 128], F32),
    ("w_uk", [2, 128, 384], F32), ("w_uv", [2, 128, 512], F32), ("w_out_even", [2, D, D], F32),
    ("w_in_odd", [2, D, 3072], F32), ("w_out_odd", [2, D, D], F32), ("w_up", [4, D, 2 * DFF], F32),
    ("conv_w", [4, 3, DFF], F32), ("conv_b", [4, DFF], F32), ("w_down", [4, DFF, D], F32),
    ("c_ident", [128, 128], F32), ("c_triT", [128, 128], F32), ("c_tri32", [128, 128], F32),
    ("c_U32", [128, 128], F32), ("c_ones32", [128, 128], F32), ("c_sel", [128, 1024], F32), ("c_invf", [16], F32), ("c_pow2", [NIT], F32),
]


def host_consts():
    i = np.arange(128)
    c = {}
    c["c_ident"] = np.eye(128, dtype=np.float32)
    c["c_triT"] = np.where(i[:, None] > i[None, :], NEG, 0.0).astype(np.float32)
    c["c_tri32"] = np.where(i[None, :] > i[:, None], -1e30, 0.0).astype(np.float32)
    c["c_U32"] = (i[:, None] <= i[None, :]).astype(np.float32)
    c["c_ones32"] = np.ones((128, 128), np.float32)
    c["c_sel"] = np.repeat(np.eye(128, 8, dtype=np.float32), 128, axis=1)
    invf = (500000.0 ** (-np.arange(0, 16, 2, dtype=np.float32) / 16)).astype(np.float32)
    c["c_invf"] = np.concatenate([invf, invf]).astype(np.float32)
    c["c_pow2"] = (2.0 ** -(np.arange(NIT) + 1.0)).astype(np.float32)
    return c


def build_nc(parts, final=True, dbg_names=()):
    nc = bass.Bass("TRN2", target_bir_lowering=False)
    dr = {}
    for name, shape, dt in IN_SPECS:
        dr[name] = nc.dram_tensor(name, shape, dt, kind="ExternalInput").ap()
    y = nc.dram_tensor("y", [S, D], F32, kind="ExternalOutput").ap()
    p = Prog(nc, dr)
    p.load_x()
    p.rope_tables()
    p.fw.barrier()
    for part in parts:
        if part[0] == "ffn":
            p.ffn(part[1])
        elif part[0] == "even":
            p.even(part[1])
        elif part[0] == "odd":
            p.odd(part[1])
    outs = p.final_norm(y) if final else p.store_x(y)
    douts = []
    for name in dbg_names:
        ap = p.dbg[name]
        shp = [128, int(np.prod(ap.shape[1:]))]
        d = nc.dram_tensor("dbg_" + name, shp, ap.dtype, kind="ExternalOutput").ap()
        b = Buf()
        flat = ap
        if len(ap.shape) == 3:
            flat = ap.rearrange("p a b -> p (a b)")
        elif len(ap.shape) == 4:
            flat = ap.rearrange("p a b c -> p (a b c)")
        p.fw.barrier()
        p.dma(p.SP, d, flat, writes=[b])
        douts.append(b)
    p.fw.wait_all(p.SP, outs + douts)
    return nc, p


def make_in_maps(inputs, cores):
    c = host_consts()
    maps = []
    for b in cores:
        m = dict(c)
        for name, shape, dt in IN_SPECS:
            if name.startswith("c_"):
                continue
            a = np.asarray(inputs[name])
            if name in ("x", "positions"):
                a = a[b]
            a = np.ascontiguousarray(a).reshape(shape)
            m[name] = a.astype(np.int32 if dt == I32 else np.float32, copy=False)
        maps.append(m)
    return maps


ALL_PARTS = [("even", 0), ("ffn", 0), ("odd", 0), ("ffn", 1), ("even", 1), ("ffn", 2), ("odd", 1), ("ffn", 3)]


def kernel(**inputs):
    nc, _ = build_nc(ALL_PARTS, final=True)
    maps = make_in_maps(inputs, list(range(8)))
    res = run_bass_kernel_spmd(nc, maps, core_ids=list(range(8)))
    return np.stack([np.asarray(r["y"], dtype=np.float32) for r in res.results], axis=0)
```

```python
import numpy as np
import concourse.bass as bass
import concourse.mybir as mybir
from concourse.bass_utils import run_bass_kernel_spmd

F32 = mybir.dt.float32
BF16 = mybir.dt.bfloat16
I32 = mybir.dt.int32
ALU = mybir.AluOpType
AF = mybir.ActivationFunctionType
AX = mybir.AxisListType

SEM_ROT = 16000
import os
STOP = os.environ.get('EVEN_STOP', '')


class Buf:
    __slots__ = ("w", "r", "name")

    def __init__(self, name=""):
        self.w = None
        self.r = {}
        self.name = name


class EngCtx:
    def __init__(self, fw, name, eng, order_raw):
        self.fw, self.name, self.eng = fw, name, eng
        self.sem = fw.new_sem(name)
        self.count = 0
        self.seen = {}
        self.order_raw = order_raw
        self.nrot = 0
        self.prev = None

    def rotate(self):
        if self.count >= SEM_ROT:
            self.nrot += 1
            self.prev = (self.sem, self.count)
            self.sem = self.fw.new_sem("%s_r%d" % (self.name, self.nrot))
            self.count = 0


class FW:
    def __init__(self, nc):
        self.nc = nc
        self.nsem = 0
        self.pe = EngCtx(self, "pe", nc.tensor, False)
        self.act = EngCtx(self, "act", nc.scalar, True)
        self.dve = EngCtx(self, "dve", nc.vector, True)
        self.pool = EngCtx(self, "pool", nc.gpsimd, True)
        self.sp = EngCtx(self, "sp", nc.sync, True)
        self.dma_pools = {}
        for q in (self.sp, self.pool):
            self.dma_pools[id(q)] = [[[self.new_sem("dma_%s%d" % (q.name, i)), 0] for i in range(12)], 0]
        self.n_ops = 0

    def new_sem(self, name):
        self.nsem += 1
        return self.nc.alloc_semaphore(name="s_%s_%d" % (name, self.nsem))

    def _deps(self, E, order_raw, seen, reads, writes):
        deps = {}

        def add(d, raw):
            if d is None:
                return
            key, sem, val = d
            if E is not None and key is E:
                if not order_raw:
                    return
            k = id(sem)
            if k not in deps or deps[k][1] < val:
                deps[k] = (sem, val)

        for b in reads:
            add(b.w, True)
        for b in writes:
            add(b.w, False)
            for d in b.r.values():
                add(d, False)
        out = []
        for k, (sem, val) in deps.items():
            if seen.get(k, -1) >= val:
                continue
            out.append((sem, val))
        return out

    def _emit_waits(self, E, ins_fn, deps):
        for sem, val in deps[1:]:
            E.eng.wait_ge(sem, val)
            E.seen[id(sem)] = max(E.seen.get(id(sem), -1), val)
        ins = ins_fn()
        if deps:
            sem, val = deps[0]
            ins._wait_ge(sem, val)
            E.seen[id(sem)] = max(E.seen.get(id(sem), -1), val)
        return ins

    def op(self, E, fn, reads=(), writes=(), inc=True):
        self.n_ops += 1
        deps = self._deps(E, E.order_raw, E.seen, reads, writes)
        ins = self._emit_waits(E, fn, deps)
        val = E.count + 1
        if inc:
            ins.then_inc(E.sem, 1)
            E.count = val
        rec = (E, E.sem, val)
        for b in writes:
            b.w = rec
            b.r = {}
        for b in reads:
            b.r[id(E)] = rec
        if inc:
            E.rotate()
        return ins

    def dma(self, Q, out_ap, in_ap, reads=(), writes=(), **kw):
        self.n_ops += 1
        pool = self.dma_pools[id(Q)]
        slot = pool[0][pool[1]]
        pool[1] = (pool[1] + 1) % len(pool[0])
        if slot[1] >= SEM_ROT:
            Q.eng.wait_ge(slot[0], slot[1])
            slot[0] = self.new_sem("dmar")
            slot[1] = 0
        sem = slot[0]
        deps = self._deps(None, True, Q.seen, reads, writes)
        if slot[1] > 0 and Q.seen.get(id(sem), -1) < slot[1]:
            deps = [d for d in deps if d[0] is not sem]
            deps.append((sem, slot[1]))
        deps = [d for d in deps if Q.seen.get(id(d[0]), -1) < d[1]]
        ins = self._emit_waits(Q, lambda: Q.eng.dma_start(out=out_ap, in_=in_ap, **kw), deps)
        slot[1] += 16
        ins.then_inc(sem, 16)
        rec = (slot, sem, slot[1])
        for b in writes:
            b.w = rec
            b.r = {}
        for b in reads:
            b.r[id(slot)] = rec
        return ins

    def wait_all(self, E, bufs):
        deps = self._deps(None, True, E.seen, bufs, ())
        for sem, val in deps:
            E.eng.wait_ge(sem, val)
            E.seen[id(sem)] = val


    def barrier(self):
        engs = [self.pe, self.act, self.dve, self.pool, self.sp]
        items = [(e.sem, e.count) for e in engs if e.count > 0]
        for pl in self.dma_pools.values():
            items += [(s[0], s[1]) for s in pl[0] if s[1] > 0]
        items += [e.prev for e in engs if e.prev is not None]
        for E in engs:
            for sem, val in items:
                if sem is E.sem:
                    continue
                if E.seen.get(id(sem), -1) >= val:
                    continue
                E.eng.wait_ge(sem, val)
                E.seen[id(sem)] = val


S = 2048
D = 1024
NT = 16
DFF = 2816
NFC = 22
EPS = 1e-6
NEG = -30000.0
NIT = 16
EVEN_IN = 2524
IDX_SCALE = (64 ** -0.5) * (4 ** -0.5)
MOFF = [0, 4, 12, 24]

A_X = 0
A_HT = 65536
A_MIX = 106496
A_R1 = 122880
A_R3 = 139264
A_CKV = 176128
A_W = 180224
A_C = 196608
A_END = 208896


def _sz(dt):
    return 4 if dt in (F32, I32) else 2


class Prog:
    def __init__(self, nc, dr, dbg=None):
        self.nc = nc
        self.dr = dr
        self.fw = FW(nc)
        self.arena = nc.alloc_sbuf_tensor("arena", [128, A_END // 4], F32)
        fw = self.fw
        self.PE, self.ACT, self.DVE, self.POOL, self.SP = fw.pe, fw.act, fw.dve, fw.pool, fw.sp
        self.ps = {}
        for n in ("A0", "A1", "S0", "S1", "O0", "O1"):
            self.ps[n] = (nc.alloc_psum_tensor("p" + n, [128, 512], F32)[:, :], Buf("p" + n))
        for n in ("T0", "T1"):
            self.ps[n] = (nc.alloc_psum_tensor("p" + n, [128, 1024], BF16)[:, :], Buf("p" + n))
        self.rot = {}
        self.dbg = dbg if dbg is not None else {}
        self.dbg_out = []
        self._consts()

    def view(self, off, shape, dt):
        n = int(np.prod(shape))
        sz = _sz(dt)
        assert off % 4 == 0 and (n * sz) % 4 == 0, (off, shape)
        a = self.arena[:, off // 4:(off + n * sz) // 4]
        if dt != F32:
            a = a.bitcast(dt)
        if len(shape) == 2:
            a = a.rearrange("p (a b) -> p a b", b=shape[1])
        elif len(shape) == 3:
            a = a.rearrange("p (a b c) -> p a b c", b=shape[1], c=shape[2])
        elif len(shape) == 4:
            a = a.rearrange("p (a b c d) -> p a b c d", b=shape[1], c=shape[2], d=shape[3])
        return a

    def bank(self, kind):
        k = self.rot.get(kind, 0)
        self.rot[kind] = k ^ 1
        return self.ps[kind + str(k)]

    def op(self, E, meth, reads, writes, inc=True, **kw):
        return self.fw.op(E, lambda: getattr(E.eng, meth)(**kw), reads, writes, inc)

    def mm(self, out, lhsT, rhs, start, stop, reads, writes, inc=True, skip=False):
        kw = dict(out=out, lhsT=lhsT, rhs=rhs, start=start, stop=stop)
        if skip:
            kw["skip_group_check"] = True
        return self.fw.op(self.PE, lambda: self.nc.tensor.matmul(**kw), reads, writes, inc)

    def tr(self, out, in_, reads, writes, inc=True):
        return self.fw.op(self.PE, lambda: self.nc.tensor.transpose(out=out, in_=in_, identity=self.ident),
                          list(reads) + [self.b_const], writes, inc)

    def dma(self, Q, out, in_, reads=(), writes=(), **kw):
        return self.fw.dma(Q, out, in_, reads, writes, **kw)

    def _consts(self):
        nc, dr = self.nc, self.dr
        o = [A_C]

        def take(shape, dt):
            n = int(np.prod(shape)) * _sz(dt)
            n = (n + 31) // 32 * 32
            v = self.view(o[0], shape, dt)
            o[0] += n
            assert o[0] <= A_END, o[0]
            return v
        self.b_const = Buf("const")
        bc = [self.b_const]
        self.ident = take([128], BF16)
        self.triT = take([128], BF16)
        self.zeros = take([264], BF16)
        self.tri32 = take([128], F32)
        self.U32 = take([128], F32)
        self.ones32 = take([128], F32)
        self.anorm = take([4, 8], F32)
        self.fnorm = take([4, 8], F32)
        self.cos2 = take([16, 16], F32)
        self.sin2 = take([16, 16], F32)
        self.invf = take([16], F32)
        self.pow2 = take([NIT], F32)
        self.pos_i = take([16], I32)
        self.pos_f = take([16], F32)
        self.ang = self.view(A_R3, [16, 16], F32)
        self.angk = self.view(A_R3 + 1024, [16, 16], F32)
        self.angi = self.view(A_R3 + 2048, [16, 16], I32)
        self.ss = take([16], F32)
        self.rstd = take([16], F32)
        self.convw = take([4, 3, NFC], F32)
        self.convb = take([4, NFC], F32)
        self.bfox = take([2, 8], F32)
        self.gckv = take([2], F32)
        self.fb = take([16, 4, 8], F32)
        self.csb = take([16, 8], F32)
        self.cref = take([4, 8], F32)
        self.zf = self.view(A_R3, [16, 8], F32)
        self.zf2 = self.view(A_R3 + 512, [16, 8], F32)
        self.iw = take([16, 4], F32)
        self.st = take([64], F32)
        self.b_st = Buf("st")
        self.dkr = take([16, 16], BF16)
        self.b_dkr = Buf("dkr")
        self.bis = take([3, 32], F32)
        self.kmT32 = take([2, 8], F32)
        self.kmT = take([2, 8], BF16)
        self.b_km = Buf("km")
        self.b_fb = Buf("fb")
        self.ptk = 0
        SP, POOL = self.SP, self.POOL
        self.dma(POOL, self.ident, dr["c_ident"], writes=bc)
        self.dma(POOL, self.triT, dr["c_triT"], writes=bc)
        self.dma(SP, self.tri32, dr["c_tri32"], writes=bc)
        self.dma(SP, self.U32, dr["c_U32"], writes=bc)
        self.dma(SP, self.ones32, dr["c_ones32"], writes=bc)
        self.dma(SP, self.anorm, dr["attn_norm"].rearrange("l (c p) -> p l c", p=128), writes=bc,
                 allow_slow_non_contiguous=True)
        self.dma(SP, self.fnorm, dr["ffn_norm"].rearrange("l (c p) -> p l c", p=128), writes=bc,
                 allow_slow_non_contiguous=True)
        self.dma(SP, self.invf, dr["c_invf"].partition_broadcast(128), writes=bc)
        self.dma(SP, self.pow2, dr["c_pow2"].partition_broadcast(128), writes=bc)
        self.dma(SP, self.pos_i, dr["positions"].rearrange("(t p) -> p t", p=128), writes=bc,
                 allow_slow_non_contiguous=True)
        self.dma(SP, self.convw, dr["conv_w"].rearrange("l j (c p) -> p l j c", p=128), writes=bc,
                 allow_slow_non_contiguous=True)
        self.dma(SP, self.convb, dr["conv_b"].rearrange("l (c p) -> p l c", p=128), writes=bc,
                 allow_slow_non_contiguous=True)
        self.dma(SP, self.bfox.rearrange("p a b -> p (a b)"), dr["b_fox_f"].rearrange("a b -> (a b)").partition_broadcast(128), writes=bc)
        self.dma(SP, self.gckv, dr["g_ckv"].rearrange("l p -> p l"), writes=bc, allow_slow_non_contiguous=True)
        self.op(self.DVE, "memset", [], bc, ap=self.zeros, constant=0.0)
        self.x = self.view(A_X, [16, 1024], F32)
        self.bx = [Buf("x%d" % t) for t in range(16)]
        self.hT = self.view(A_HT, [8, 2048], BF16)
        self.b_hT = Buf("hT")
        self.mixT = self.view(A_MIX, [4, 2048], BF16)
        self.b_mixT = Buf("mixT")
        self.ckvT = self.view(A_CKV, [2048], BF16)
        self.b_ckvT = Buf("ckvT")
        self.wslots = [(A_W + 4096 * i, Buf("w%d" % i)) for i in range(4)]
        self.wnext = 0

    def wslot(self, shape=(8, 256)):
        off, b = self.wslots[self.wnext]
        self.wnext = (self.wnext + 1) % 4
        return self.view(off, list(shape), BF16), b

    def load_x(self):
        xr = self.dr["x"].rearrange("(t p) d -> p t d", p=128)
        for t in range(16):
            self.dma(self.SP, self.x[:, t, :], xr[:, t, :], writes=[self.bx[t]])

    def rope_tables(self):
        V, A = self.DVE, self.ACT
        bc = [self.b_const]
        TWO_PI = 2.0 * np.pi
        C1 = 6.28125
        C2 = TWO_PI - C1
        self.op(V, "tensor_copy", bc, bc, out=self.pos_f, in_=self.pos_i)
        for which, dst in ((0, self.sin2), (1, self.cos2)):
            self.op(V, "tensor_tensor", bc, bc, out=self.ang,
                    in0=self.pos_f.unsqueeze(2).to_broadcast([128, 16, 16]),
                    in1=self.invf.unsqueeze(1).to_broadcast([128, 16, 16]), op=ALU.mult)
            if which == 1:
                self.op(V, "tensor_scalar", bc, bc, out=self.ang, in0=self.ang, scalar1=float(np.pi / 2), scalar2=None, op0=ALU.add)
            self.op(V, "tensor_scalar", bc, bc, out=self.angk, in0=self.ang, scalar1=float(1.0 / TWO_PI), scalar2=None, op0=ALU.mult)
            self.op(V, "tensor_copy", bc, bc, out=self.angi, in_=self.angk)
            self.op(V, "tensor_copy", bc, bc, out=self.angk, in_=self.angi)
            self.op(V, "scalar_tensor_tensor", bc, bc, out=self.ang, in0=self.angk, scalar=-C1, in1=self.ang, op0=ALU.mult, op1=ALU.add)
            self.op(V, "scalar_tensor_tensor", bc, bc, out=self.ang, in0=self.angk, scalar=-C2, in1=self.ang, op0=ALU.mult, op1=ALU.add)
            self.op(V, "tensor_scalar", bc, bc, out=self.ang, in0=self.ang, scalar1=3.1415925, scalar2=-3.1415925, op0=ALU.min, op1=ALU.max)
            self.op(A, "activation", bc, bc, out=dst, in_=self.ang, func=AF.Sin)

    def norm_to_hT(self, gain):
        A, V, G = self.ACT, self.DVE, self.POOL
        xh = [(self.view(A_R3 + 2048 * i, [1024], BF16), Buf("xh%d" % i)) for i in range(2)]
        junk = (self.view(A_R3 + 4096, [1024], BF16), Buf("junk"))
        bst = self.b_st
        for t in range(16):
            self.op(A, "activation", [self.bx[t]], [junk[1], bst], out=junk[0], in_=self.x[:, t, :], func=AF.Square,
                    accum_out=self.ss[:, t:t + 1])
        self.op(A, "activation", [bst], [bst], out=self.rstd, in_=self.ss, func=AF.Sqrt, scale=1.0 / D, bias=EPS)
        self.op(V, "reciprocal", [bst], [bst], out=self.rstd, in_=self.rstd)
        for t in range(16):
            xa, xb = xh[t % 2]
            self.op(A, "activation", [self.bx[t], bst], [xb], out=xa, in_=self.x[:, t, :], func=AF.Copy, scale=self.rstd[:, t:t + 1])
            pT, bT = self.bank("T")
            for c in range(8):
                self.tr(pT[:, c * 128:(c + 1) * 128], xa[:, c * 128:(c + 1) * 128], [xb], [bT], inc=(c == 7))
            self.op(V, "tensor_tensor", [bT, self.b_const], [self.b_hT], out=self.hT[:, :, t * 128:(t + 1) * 128],
                    in0=pT.rearrange("p (c k) -> p c k", k=128), in1=gain.unsqueeze(2).to_broadcast([128, 8, 128]), op=ALU.mult)

    def ffn(self, layer):
        A, V, G = self.ACT, self.DVE, self.POOL
        dr = self.dr
        bc = self.b_const
        self.norm_to_hT(self.fnorm[:, layer, :])
        self.fw.barrier()
        base = A_MIX
        actT = [(self.view(base + 24576 * i, [6, 2048], BF16), Buf("actT%d" % i)) for i in range(2)]
        gsb = [(self.view(base + 49152 + 8224 * i, [2056], F32), [Buf("gsb%d_%d" % (i, k)) for k in range(4)]) for i in range(2)]
        sp = A_HT + 32768
        acc = [(self.view(sp + 2048 * i, [512], F32), Buf("acc%d" % i)) for i in range(2)]
        sg = [(self.view(sp + 4096 + 2048 * i, [512], F32), Buf("sg%d" % i)) for i in range(2)]
        for gs, bgs in gsb:
            self.op(V, "memset", [], [bgs[0]], ap=gs[:, 0:2], constant=0.0)
        wup = dr["w_up"][layer]
        wdn = dr["w_down"][layer]
        groups = [(0, 6), (6, 12), (12, 18), (18, 22)]
        k = 0
        for g, (f0, f1) in enumerate(groups):
            aT, baT = actT[g % 2]
            for fc in range(f0, f1):
                ws, bws = self.wslot()
                self.dma(G, ws[:, :, 0:128], wup[:, fc * 128:(fc + 1) * 128].rearrange("(c p) n -> p c n", p=128), writes=[bws])
                self.dma(G, ws[:, :, 128:256], wup[:, DFF + fc * 128:DFF + (fc + 1) * 128].rearrange("(c p) n -> p c n", p=128), writes=[bws])
                gs, bgs = gsb[fc % 2]
                for tc in range(4):
                    tsl = slice(tc * 512, (tc + 1) * 512)
                    pG, bG = self.bank("A")
                    pU, bU = self.bank("S")
                    for c in range(8):
                        self.mm(pG, ws[:, c, 0:128], self.hT[:, c, tsl], c == 0, c == 7, [bws, self.b_hT], [bG], inc=(c == 7))
                    for c in range(8):
                        self.mm(pU, ws[:, c, 128:256], self.hT[:, c, tsl], c == 0, c == 7, [bws, self.b_hT], [bU], inc=(c == 7))
                    self.op(A, "activation", [bG], [bgs[tc]], out=gs[:, 2 + tc * 512:2 + (tc + 1) * 512], in_=pG, func=AF.Copy)
                    ac, bac = acc[k % 2]
                    sgt, bsg = sg[k % 2]
                    k += 1
                    rd = [bgs[tc], bc] + ([bgs[tc - 1]] if tc > 0 else [bgs[0]])
                    self.op(V, "tensor_scalar", rd, [bac], out=ac, in0=gs[:, 2 + tc * 512:2 + (tc + 1) * 512],
                            scalar1=self.convw[:, layer, 2, fc:fc + 1], scalar2=self.convb[:, layer, fc:fc + 1], op0=ALU.mult, op1=ALU.add)
                    self.op(V, "scalar_tensor_tensor", rd + [bac], [bac], out=ac, in0=gs[:, 1 + tc * 512:1 + (tc + 1) * 512],
                            scalar=self.convw[:, layer, 1, fc:fc + 1], in1=ac, op0=ALU.mult, op1=ALU.add)
                    self.op(V, "scalar_tensor_tensor", rd + [bac], [bac], out=ac, in0=gs[:, tc * 512:(tc + 1) * 512],
                            scalar=self.convw[:, layer, 0, fc:fc + 1], in1=ac, op0=ALU.mult, op1=ALU.add)
                    self.op(A, "activation", [bac], [bsg], out=sgt, in_=ac, func=AF.Silu)
                    self.op(V, "tensor_tensor", [bsg, bU], [baT], out=aT[:, fc - f0, tsl], in0=sgt, in1=pU, op=ALU.mult)
            nf = f1 - f0
            wd = []
            for s in range((nf + 1) // 2):
                ws, bws = self.wslot((2, 1024))
                r0 = (f0 + 2 * s) * 128
                self.dma(G, ws, wdn[r0:r0 + 256, :].rearrange("(c p) n -> p c n", p=128), writes=[bws])
                wd.append((ws, bws))
            for t in range(16):
                for nh in range(2):
                    pO, bO = self.bank("O")
                    nsl = slice(nh * 512, (nh + 1) * 512)
                    for fl in range(nf):
                        ws, bws = wd[fl // 2]
                        self.mm(pO, aT[:, fl, t * 128:(t + 1) * 128], ws[:, fl % 2, nsl], fl == 0, fl == nf - 1, [baT, bws], [bO],
                                inc=(fl == nf - 1))
                    self.op(V, "tensor_tensor", [bO, self.bx[t]], [self.bx[t]], out=self.x[:, t, nsl], in0=self.x[:, t, nsl], in1=pO, op=ALU.add)
        self.fw.barrier()

    def attend(self, qT, kT, rows, rd_qk, v, rd_v, stage, bstage, scol, PT, mode, bias=None, mask=None, qbias=None):
        A, V = self.ACT, self.DVE
        for qc in range(4):
            pO, bO = self.bank("O")
            self.mm(pO[:, 0:264], self.zeros[:, 0:128], self.zeros[:, 0:264], True, True, [self.b_const], [bO])
            pend = None

            def pv(args):
                j, q0, pt, bpt = args
                for t in range(q0 // 128, 4 * qc + 4):
                    c0 = (t - 4 * qc) * 66
                    self.mm(pO[:, c0:c0 + 65], pt[:, t * 128 - q0:t * 128 - q0 + 128], v[:, j, 0:65], False, (j == t),
                            [bpt] + rd_v, [bO], inc=(t == 4 * qc + 3), skip=True)
            for j in range(4 * qc + 4):
                q0 = max(512 * qc, 128 * j)
                N = 512 * qc + 512 - q0
                pS, bS = self.bank("S")
                diag = (mode != "dsa") and (j >= 4 * qc)
                ksl = slice(j * 128, (j + 1) * 128)
                groups = [(0, 128, True), (128, N - 128, False)] if diag else [(0, N, False)]
                groups = [g for g in groups if g[1] > 0]
                for gi, (c0, n, tri) in enumerate(groups):
                    terms = []
                    if tri:
                        terms.append((self.ident, self.triT, [self.b_const]))
                    if mode == "dsa":
                        m_ap, m_b = mask(qc, j, q0 - 512 * qc + c0, n)
                        terms.append((self.ident, m_ap, [self.b_const, m_b]))
                    if qbias is not None:
                        l_ap, r_ap, r_b = qbias(q0 + c0, n)
                        terms.append((l_ap, r_ap, r_b))
                    lastg = (gi == len(groups) - 1)
                    self.mm(pS[:, c0:c0 + n], kT[rows, ksl], qT[rows, q0 + c0:q0 + c0 + n], True, not terms, rd_qk, [bS],
                            inc=(lastg and not terms))
                    for ti, (l_ap, r_ap, r_b) in enumerate(terms):
                        lt = (ti == len(terms) - 1)
                        self.mm(pS[:, c0:c0 + n], l_ap, r_ap, False, lt, r_b, [bS], inc=(lastg and lt))
                pt, bpt = PT[self.ptk % len(PT)]
                self.ptk += 1
                kw = dict(out=pt[:, 0:N], in_=pS[:, 0:N], func=AF.Exp, scale=0.125)
                rds = [bS]
                if bias is not None:
                    kw["bias"] = bias(j, qc)
                    rds.append(self.b_fb)
                self.op(A, "activation", rds, [bpt], **kw)
                if pend is not None:
                    pv(pend)
                pend = (j, q0, pt, bpt)
            pv(pend)
            for t in range(4 * qc, 4 * qc + 4):
                c0 = (t - 4 * qc) * 66
                ri = self.st[:, (t % 8):(t % 8) + 1]
                self.op(V, "reciprocal", [bO], [self.b_st], out=ri, in_=pO[:, c0 + 64:c0 + 65])
                self.op(V, "tensor_scalar", [bO, self.b_st], [bstage], out=stage[:, t, scol:scol + 64], in0=pO[:, c0:c0 + 64],
                        scalar1=ri, scalar2=None, op0=ALU.mult)

    def rope_inplace(self, E, data, tmp, t0, nt, nh, rd, wr, btmp):
        cs = self.cos2[:, t0:t0 + nt, :].unsqueeze(2).to_broadcast([128, nt, nh, 16])
        sn = self.sin2[:, t0:t0 + nt, :].unsqueeze(2).to_broadcast([128, nt, nh, 16])
        bc = self.b_const
        self.op(E, "tensor_tensor", rd + [bc], [btmp], out=tmp, in0=data, in1=sn, op=ALU.mult)
        self.op(E, "tensor_tensor", rd + [bc], wr, out=data, in0=data, in1=cs, op=ALU.mult)
        self.op(E, "tensor_tensor", rd + [btmp], wr, out=data[:, :, :, 0:8], in0=data[:, :, :, 0:8], in1=tmp[:, :, :, 8:16], op=ALU.subtract)
        self.op(E, "tensor_tensor", rd + [btmp], wr, out=data[:, :, :, 8:16], in0=data[:, :, :, 8:16], in1=tmp[:, :, :, 0:8], op=ALU.add)

    def outproj_half(self, w, half):
        V, G = self.DVE, self.POOL
        for ng in range(2):
            ws, bws = self.wslot((4, 512))
            self.dma(G, ws, w[half * 512:(half + 1) * 512, ng * 512:(ng + 1) * 512].rearrange("(c p) n -> p c n", p=128), writes=[bws])
            nsl = slice(ng * 512, (ng + 1) * 512)
            for t in range(16):
                pA, bA = self.bank("A")
                for c in range(4):
                    self.mm(pA, self.mixT[:, c, t * 128:(t + 1) * 128], ws[:, c, :], c == 0, c == 3, [self.b_mixT, bws], [bA], inc=(c == 3))
                self.op(V, "tensor_tensor", [bA, self.bx[t]], [self.bx[t]], out=self.x[:, t, nsl], in0=self.x[:, t, nsl], in1=pA, op=ALU.add)

    def mixed_to_T(self, mst, bmst, slot):
        for half in range(2):
            pT, bT = self.bank("T")
            for tt in range(8):
                self.tr(pT[:, tt * 128:(tt + 1) * 128], mst[:, half * 8 + tt, :], [bmst], [bT], inc=(tt == 7))
            self.op(self.ACT, "activation", [bT], [self.b_mixT], out=self.mixT[:, slot, half * 1024:(half + 1) * 1024], in_=pT, func=AF.Copy)

    def odd(self, i):
        A, V, G = self.ACT, self.DVE, self.POOL
        layer = 2 * i + 1
        dr = self.dr
        bc = self.b_const
        win = dr["w_in_odd"][i]
        self.norm_to_hT(self.anorm[:, layer, :])
        self.fw.barrier()
        qT = [(self.view(A_R1 + 4096 * k, [2048], BF16), Buf("qT%d" % k)) for k in range(4)]
        kT = [(self.view(A_R3 + 4096 * k, [2048], BF16), Buf("kT%d" % k)) for k in range(4)]
        o = A_R3 + 16384
        vaug = [(self.view(o + 4224 * k, [16, 2, 66], BF16), Buf("v%d" % k)) for k in range(2)]
        o += 8448
        stgq, bsq = self.view(o, [16, 2, 72], BF16), Buf("stgq")
        o += 4608
        stgk, bsk = self.view(o, [16, 2, 72], BF16), Buf("stgk")
        o += 4608
        s32, bs32 = self.view(o, [512], F32), Buf("s32")
        o += 2048
        assert o <= A_R3 + 36864
        sp = A_HT + 32768
        PT = [(self.view(sp + 1024 * k, [512], BF16), Buf("PT%d" % k)) for k in range(4)]
        mst, bmst = self.view(sp + 4096, [16, 128], BF16), Buf("mst")
        tmp, btmp = self.view(A_CKV, [4, 2, 16], F32), Buf("ropetmp")
        gt, bgt = self.view(A_CKV + 1024, [8, 2, 8], F32), Buf("gt")
        rank, brk = self.view(A_CKV + 2048, [8, 2, 8], F32), Buf("rank")
        cmp_, bcmp = self.view(A_CKV + 3072, [8, 2, 8], F32), Buf("cmp")
        for va, bva in vaug:
            self.op(G, "memset", [], [bva], ap=va[:, :, :, 64:66], constant=1.0)
        self.op(G, "memset", [], [bsq], ap=stgq[:, :, :, 64:72], constant=0.0)
        self.op(G, "memset", [], [bsk], ap=stgk[:, :, :, 64:72], constant=0.0)
        for b in range(8):
            self.op(G, "memset", [], [bsk], ap=stgk[:, 2 * b:2 * b + 2, :, 64 + b:65 + b], constant=1.0)
        for qa, bqa in qT:
            self.op(G, "memset", [], [bqa], ap=qa[64:128, :], constant=0.0)
        for ka_, bka_ in kT:
            self.op(G, "memset", [], [bka_], ap=ka_[64:128, :], constant=0.0)
        def load_pair(hp):
            wqk, bwqk = self.wslot()
            wv, bwv = self.wslot()
            self.dma(G, wqk[:, :, 0:128], win[:, hp * 128:(hp + 1) * 128].rearrange("(c p) n -> p c n", p=128), writes=[bwqk])
            self.dma(G, wqk[:, :, 128:256], win[:, 1024 + hp * 128:1024 + (hp + 1) * 128].rearrange("(c p) n -> p c n", p=128), writes=[bwqk])
            self.dma(G, wv[:, :, 0:128], win[:, 2048 + hp * 128:2048 + (hp + 1) * 128].rearrange("(c p) n -> p c n", p=128), writes=[bwv])
            return wqk, bwqk, wv, bwv
        wts = {0: load_pair(0)}

        def prep1(hp):
            st_ = hp % 2
            wqk, bwqk, wv, bwv = wts[hp]
            va, bva = vaug[st_]
            for tb in range(4):
                for which, stg, bstg in ((0, stgq, bsq), (1, stgk, bsk)):
                    pA, bA = self.bank("A")
                    for tt in range(4):
                        t = 4 * tb + tt
                        for c in range(8):
                            self.mm(pA[:, tt * 128:(tt + 1) * 128], self.hT[:, c, t * 128:(t + 1) * 128], wqk[:, c, which * 128:(which + 1) * 128],
                                    c == 0, c == 7, [self.b_hT, bwqk], [bA], inc=(tt == 3 and c == 7))
                    self.op(A, "activation", [bA], [bs32], out=s32, in_=pA, func=AF.Copy)
                    s4 = s32.rearrange("p (a h d) -> p a h d", h=2, d=64)
                    self.rope_inplace(V, s4[:, :, :, 0:16], tmp, 4 * tb, 4, 2, [bs32], [bs32], btmp)
                    self.op(G, "tensor_copy", [bs32], [bstg], out=stg[:, 4 * tb:4 * tb + 4, :, 0:64], in_=s4)
                pA, bA = self.bank("A")
                for tt in range(4):
                    t = 4 * tb + tt
                    for c in range(8):
                        self.mm(pA[:, tt * 128:(tt + 1) * 128], self.hT[:, c, t * 128:(t + 1) * 128], wv[:, c, 0:128],
                                c == 0, c == 7, [self.b_hT, bwv], [bA], inc=(tt == 3 and c == 7))
                self.op(A, "activation", [bA], [bva], out=va[:, 4 * tb:4 * tb + 4, :, 0:64],
                        in_=pA.rearrange("p (a h d) -> p a h d", h=2, d=64), func=AF.Copy)
            if hp < 7:
                wts[hp + 1] = load_pair(hp + 1)
            for hh in range(2):
                qa, bqa = qT[2 * st_ + hh]
                ka, bka = kT[2 * st_ + hh]
                for half in range(2):
                    pT, bT = self.bank("T")
                    for tt in range(8):
                        self.tr(pT[0:64, tt * 128:(tt + 1) * 128], stgq[:, half * 8 + tt, hh, 0:64], [bsq], [bT], inc=(tt == 7))
                    self.op(A, "activation", [bT], [bqa], out=qa[0:64, half * 1024:(half + 1) * 1024], in_=pT[0:64, :], func=AF.Copy)
                    pT, bT = self.bank("T")
                    for tt in range(8):
                        self.tr(pT[0:72, tt * 128:(tt + 1) * 128], stgk[:, half * 8 + tt, hh, 0:72], [bsk], [bT], inc=(tt == 7))
                    self.op(V, "tensor_copy", [bT], [bka], out=ka[0:72, half * 1024:(half + 1) * 1024], in_=pT[0:72, :])
                self.op(V, "tensor_reduce", [bka], [self.b_km], out=self.kmT32[0:64, hh, :],
                        in_=ka[0:64, :].rearrange("p (n k) -> p n k", k=256), axis=AX.X, op=ALU.add)
            self.op(V, "tensor_scalar", [self.b_km], [self.b_km], out=self.kmT[0:64, :, :], in0=self.kmT32[0:64, :, :], scalar1=1.0 / 256, scalar2=None, op0=ALU.mult)
            pA, bA = self.bank("A")
            for t in range(8, 16):
                for hh in range(2):
                    qa, bqa = qT[2 * st_ + hh]
                    c0 = ((t - 8) * 2 + hh) * 8
                    self.mm(pA[:, c0:c0 + 8], qa[0:64, t * 128:(t + 1) * 128], self.kmT[0:64, hh, :], True, True, [bqa, self.b_km], [bA],
                            inc=(t == 15 and hh == 1))
            self.op(V, "tensor_copy", [bA], [bgt], out=gt, in_=pA[:, 0:128].rearrange("p (a h n) -> p a h n", h=2, n=8))
            for n1 in range(7):
                a0 = max(0, n1 - 3) * 2 if n1 >= 4 else 0
                na = 8 - a0
                dst = rank if n1 == 0 else cmp_
                bd = brk if n1 == 0 else bcmp
                self.op(V, "tensor_tensor", [bgt], [bd], out=dst[:, a0:8, :, :], in0=gt[:, a0:8, :, n1:n1 + 1].to_broadcast([128, na, 2, 8]),
                        in1=gt[:, a0:8, :, :], op=ALU.is_gt)
                if n1 > 0:
                    self.op(V, "tensor_tensor", [brk, bcmp], [brk], out=rank[:, a0:8, :, :], in0=rank[:, a0:8, :, :], in1=cmp_[:, a0:8, :, :], op=ALU.add)
            for t in range(8, 16):
                ob = t // 2
                self.op(V, "tensor_scalar", [brk], [bsq], out=stgq[:, t, :, 64:64 + ob], in0=rank[:, t - 8, :, 0:ob], scalar1=2.5, scalar2=NEG,
                        op0=ALU.is_gt, op1=ALU.mult)

        def prep2(hp):
            st_ = hp % 2
            for hh in range(2):
                qa, bqa = qT[2 * st_ + hh]
                pT, bT = self.bank("T")
                for tt in range(8):
                    self.tr(pT[0:72, tt * 128:(tt + 1) * 128], stgq[:, 8 + tt, hh, 0:72], [bsq], [bT], inc=(tt == 7))
                self.op(A, "activation", [bT], [bqa], out=qa[64:72, 1024:2048], in_=pT[64:72, :], func=AF.Copy)

        def att(hp):
            st_ = hp % 2
            va, bva = vaug[st_]
            for hh in range(2):
                qa, bqa = qT[2 * st_ + hh]
                ka, bka = kT[2 * st_ + hh]
                self.attend(qa, ka, slice(0, 128), [bqa, bka], va[:, :, hh, :], [bva], mst, bmst, hh * 64, PT, "tri")
            self.mixed_to_T(mst, bmst, hp % 4)
            if hp % 4 == 3:
                self.outproj_half(dr["w_out_odd"][i], hp // 4)

        prep1(0)
        prep2(0)
        for hp in range(8):
            if hp < 7:
                prep1(hp + 1)
            att(hp)
            if hp < 7:
                prep2(hp + 1)
        self.fw.barrier()

    def even(self, i):
        A, V, G = self.ACT, self.DVE, self.POOL
        layer = 2 * i
        dr = self.dr
        bc = self.b_const
        win = dr["w_in_even"][i]
        wout = dr["w_out_even"][i]

        def wcols(c0, n):
            return win[:, c0:c0 + n].rearrange("(c p) n -> p c n", p=128)
        self.norm_to_hT(self.anorm[:, layer, :])
        self.fw.barrier()
        kpad = [(self.view(A_R1 + 4096 * k, [2048], BF16), Buf("fk%d" % k)) for k in range(4)]
        for k_, (ka_, bka_) in enumerate(kpad):
            zr = slice(64, 128) if k_ % 2 == 0 else slice(0, 64)
            self.op(G, "memset", [], [bka_], ap=ka_[zr, :], constant=0.0)
        o = A_R3
        fq = [(self.view(o + 4096 * k, [2048], BF16), Buf("fq%d" % k)) for k in range(2)]
        o += 8192
        vaug = [(self.view(o + 4224 * k, [16, 2, 66], BF16), Buf("v%d" % k)) for k in range(2)]
        o += 8448
        PT = [(self.view(o + 1024 * k, [512], BF16), Buf("PT%d" % k)) for k in range(4)]
        o += 4096
        mst, bmst = self.view(o, [16, 128], BF16), Buf("mst")
        o += 4096
        zf, zf2, bzf = self.view(o, [16, 8], F32), self.view(o + 512, [16, 8], F32), Buf("zf")
        o += 1024
        for va, bva in vaug:
            self.op(G, "memset", [], [bva], ap=va[:, :, :, 64:66], constant=1.0)
        wff, bwff = self.wslot()
        self.dma(G, wff[:, :, 0:128], wcols(1536, 128), writes=[bwff])
        pA, bA = self.bank("A")
        for t in range(16):
            for c in range(8):
                self.mm(pA[:, t * 8:(t + 1) * 8], self.hT[:, c, t * 128:(t + 1) * 128], wff[:, c, 0:8], c == 0, c == 7, [self.b_hT, bwff], [bA],
                        inc=(t == 15 and c == 7))
        self.op(V, "tensor_tensor", [bA, bc], [bzf], out=zf, in0=pA[:, 0:128].rearrange("p (t h) -> p t h", h=8),
                in1=self.bfox[:, i, :].unsqueeze(1).to_broadcast([128, 16, 8]), op=ALU.add)
        self.op(V, "tensor_scalar", [bzf], [bzf], out=zf2, in0=zf, scalar1=-1.0, scalar2=None, op0=ALU.mult)
        self.op(V, "tensor_tensor", [bzf], [bzf], out=zf2, in0=zf2, in1=zf, op=ALU.min)
        self.op(A, "activation", [bzf], [bzf], out=zf2, in_=zf2, func=AF.Exp)
        self.op(A, "activation", [bzf], [bzf], out=zf2, in_=zf2, func=AF.Ln, bias=1.0)
        self.op(V, "tensor_scalar", [bzf], [bzf], out=zf, in0=zf, scalar1=0.0, scalar2=None, op0=ALU.min)
        self.op(V, "tensor_tensor", [bzf], [bzf], out=zf, in0=zf, in1=zf2, op=ALU.subtract)
        pC, bC = self.bank("A")
        for t in range(16):
            for j in range(t):
                self.mm(pC[:, t * 8:(t + 1) * 8], self.ones32, zf[:, j, :], j == 0, False, [bzf, bc], [bC], inc=False)
            self.mm(pC[:, t * 8:(t + 1) * 8], self.U32, zf[:, t, :], t == 0, True, [bzf, bc], [bC], inc=False)
        for qc in range(4):
            for j in range(4 * qc + 4):
                self.mm(pC[:, 128 + qc * 8:128 + (qc + 1) * 8], self.ones32, zf[:, j, :], j == 0, j == 4 * qc + 3, [bzf, bc], [bC],
                        inc=(qc == 3 and j == 15))
        self.op(V, "tensor_copy", [bC], [self.b_fb], out=self.csb, in_=pC[:, 0:128].rearrange("p (t h) -> p t h", h=8))
        self.op(V, "tensor_copy", [bC], [self.b_fb], out=self.cref, in_=pC[:, 128:160].rearrange("p (t h) -> p t h", h=8))
        self.op(V, "tensor_tensor", [self.b_fb], [self.b_fb], out=self.fb, in0=self.cref.unsqueeze(1).to_broadcast([128, 16, 4, 8]),
                in1=self.csb.unsqueeze(2).to_broadcast([128, 16, 4, 8]), op=ALU.subtract)
        if STOP == "gate":
            self.fw.barrier()
            return
        rtm, brtm = self.view(o, [16, 8], BF16), Buf("rtm")
        o += 256
        selT, bsel = self.view(o, [8, 128], BF16), Buf("selT")
        o += 2048
        rT, brT = self.view(o, [2048], BF16), Buf("rT")
        o += 4096
        self.dma(G, selT, dr["c_sel"].rearrange("p (h k) -> p h k", k=128), writes=[bsel])
        self.op(G, "memset", [], [brT], ap=rT, constant=0.0)
        self.op(V, "tensor_tensor", [self.b_fb], [brtm], out=rtm.rearrange("p (q t) h -> p q t h", t=4),
                in0=self.csb.rearrange("p (q t) h -> p q t h", t=4),
                in1=self.cref.unsqueeze(2).to_broadcast([128, 4, 4, 8]), op=ALU.subtract)
        self.op(V, "tensor_scalar", [brtm], [brtm], out=rtm, in0=rtm, scalar1=8.0, scalar2=None, op0=ALU.mult)
        for qc in range(4):
            pR, bR = self.bank("A")
            for tt in range(4):
                self.mm(pR[0:8, tt * 128:(tt + 1) * 128], rtm[:, 4 * qc + tt, :], self.ident, True, True, [brtm, bc], [bR], inc=(tt == 3))
            self.op(V, "tensor_copy", [bR], [brT], out=rT[0:8, qc * 512:(qc + 1) * 512], in_=pR[0:8, :])
        def load_fox(hp):
            wqk, bwqk = self.wslot()
            wv, bwv = self.wslot()
            self.dma(G, wqk[:, :, 0:128], wcols(hp * 128, 128), writes=[bwqk])
            self.dma(G, wqk[:, :, 128:256], wcols(512 + hp * 128, 128), writes=[bwqk])
            self.dma(G, wv[:, :, 0:128], wcols(1024 + hp * 128, 128), writes=[bwv])
            return wqk, bwqk, wv, bwv
        nxt = load_fox(0)
        for hp in range(4):
            st_ = hp % 2
            qa, bqa = fq[st_]
            kp = [kpad[2 * st_], kpad[2 * st_ + 1]]
            va, bva = vaug[st_]
            wqk, bwqk, wv, bwv = nxt
            for which in (0, 1):
                for tc in range(4):
                    pA, bA = self.bank("A")
                    tsl = slice(tc * 512, (tc + 1) * 512)
                    for c in range(8):
                        self.mm(pA, wqk[:, c, which * 128:(which + 1) * 128], self.hT[:, c, tsl], c == 0, c == 7,
                                [self.b_hT, bwqk], [bA], inc=(c == 7))
                    if which == 0:
                        self.op(A, "activation", [bA], [bqa], out=qa[:, tsl], in_=pA, func=AF.Copy)
                    else:
                        self.op(V, "tensor_copy", [bA], [kp[0][1]], out=kp[0][0][0:64, tsl], in_=pA[0:64, :])
                        self.op(V, "tensor_copy", [bA], [kp[1][1]], out=kp[1][0][64:128, tsl], in_=pA[64:128, :])
            for tb in range(4):
                pA, bA = self.bank("A")
                for tt in range(4):
                    t = 4 * tb + tt
                    for c in range(8):
                        self.mm(pA[:, tt * 128:(tt + 1) * 128], self.hT[:, c, t * 128:(t + 1) * 128], wv[:, c, 0:128],
                                c == 0, c == 7, [self.b_hT, bwv], [bA], inc=(tt == 3 and c == 7))
                self.op(A, "activation", [bA], [bva], out=va[:, 4 * tb:4 * tb + 4, :, 0:64],
                        in_=pA.rearrange("p (a h d) -> p a h d", h=2, d=64), func=AF.Copy)
            if hp < 3:
                nxt = load_fox(hp + 1)
            for hh in range(2):
                h = 2 * hp + hh
                self.attend(qa, kp[hh][0], slice(0, 128), [bqa, kp[hh][1]], va[:, :, hh, :], [bva], mst, bmst, hh * 64, PT, "fox",
                            bias=lambda j, qc, h=h: self.fb[:, j, qc, h:h + 1],
                            qbias=lambda c0, n, h=h: (selT[:, h, :], rT[:, c0:c0 + n], [bsel, brT]))
            self.mixed_to_T(mst, bmst, hp)
        self.outproj_half(wout, 0)
        self.fw.barrier()
        if STOP == "fox":
            return
        dqT = self.view(A_R1, [4, 2048], BF16)
        b_dqT = Buf("dqT")
        iqT, b_iqT = self.view((A_MIX + 8192) if os.environ.get("IQ_MIX") else (A_R3 + 24576), [2, 2048], BF16), Buf("iqT")
        ikT2, b_ikT = self.view(A_R3 + 32768, [2048], BF16), Buf("ikT2")
        s32s = [(self.view(A_R3 + 2048 * k, [512], F32), Buf("s32_%d" % k)) for k in range(2)]
        stgs = [(self.view(A_R3 + 4096 + 1088 * k, [528], BF16), Buf("stg%d" % k)) for k in range(2)]
        tmps = [(self.view(A_R3 + 6272 + 1024 * k, [4, 4, 16], F32), Buf("ropetmp%d" % k)) for k in range(2)]
        wA, bwA = self.wslot()
        wB, bwB = self.wslot()
        self.dma(G, wA, wcols(2056, 256), writes=[bwA])
        self.dma(G, wB[:, :, 0:212], wcols(2312, 212), writes=[bwB])
        bst = self.b_st
        for t in range(16):
            tsl = slice(t * 128, (t + 1) * 128)
            s32, bs32 = s32s[t % 2]
            stg, bstg = stgs[t % 2]
            tmp, btmp = tmps[t % 2]
            pA, bA = self.bank("A")
            for c in range(8):
                self.mm(pA[:, 0:256], self.hT[:, c, tsl], wA[:, c, :], c == 0, c == 7, [self.b_hT, bwA], [bA], inc=False)
            for c in range(8):
                self.mm(pA[:, 256:468], self.hT[:, c, tsl], wB[:, c, 0:212], c == 0, c == 7, [self.b_hT, bwB], [bA], inc=(c == 7))
            self.op(A, "activation", [bA], [bs32], out=s32[:, 0:468], in_=pA[:, 0:468], func=AF.Copy)
            if STOP == "A0":
                continue
            self.op(A, "activation", [bs32], [bstg, bst], out=stg[:, 0:128], in_=s32[:, 0:128], func=AF.Square, accum_out=self.st[:, 8:9])
            self.op(A, "activation", [bst], [bst], out=self.st[:, 9:10], in_=self.st[:, 8:9], func=AF.Sqrt, scale=1.0 / 128, bias=EPS)
            self.op(V, "reciprocal", [bst], [bst], out=self.st[:, 10:11], in_=self.st[:, 9:10])
            self.op(V, "tensor_scalar", [bs32, bst], [bstg], out=stg[:, 0:128], in0=s32[:, 0:128], scalar1=self.st[:, 10:11], scalar2=None, op0=ALU.mult)
            self.rope_inplace(V, s32[:, 128:144].rearrange("p (a h d) -> p a h d", a=1, h=1), tmp[:, 0:1, 0:1, :], t, 1, 1, [bs32], [bs32], btmp)
            t5 = tmp.rearrange("p a h d -> p (a h d)")[:, 0:80].rearrange("p (a h d) -> p a h d", a=1, h=5)
            self.rope_inplace(V, s32[:, 144:464].rearrange("p (a h d) -> p a h d", a=1, h=5)[:, :, :, 0:16], t5, t, 1, 5, [bs32], [bs32], btmp)
            if STOP == "A0b":
                continue
            self.op(G, "tensor_copy", [bs32], [bstg], out=stg[:, 128:448], in_=s32[:, 144:464])
            self.op(G, "tensor_copy", [bs32], [bstg], out=stg[:, 448:512], in_=s32[:, 400:464])
            self.op(G, "tensor_copy", [bs32], [self.b_dkr], out=self.dkr[:, t, :], in_=s32[:, 128:144])
            self.op(G, "tensor_scalar", [bs32], [self.b_dkr], out=self.iw[:, t, :], in0=s32[:, 464:468], scalar1=IDX_SCALE, scalar2=None, op0=ALU.mult)
            if STOP == "A0c":
                continue
            pT, bT = self.bank("T")
            self.tr(pT[:, 0:128], stg[:, 0:128], [bstg], [bT], inc=False)
            self.tr(pT[:, 128:256], stg[:, 384:512], [bstg], [bT], inc=True)
            self.op(A, "activation", [bT, bc], [self.b_ckvT], out=self.ckvT[:, tsl], in_=pT[:, 0:128], func=AF.Identity, scale=self.gckv[:, i:i + 1])
            self.op(A, "activation", [bT], [b_ikT], out=ikT2[:, tsl], in_=pT[:, 128:256], func=AF.Copy)
            pT, bT = self.bank("T")
            self.tr(pT[:, 0:128], stg[:, 128:256], [bstg], [bT], inc=False)
            self.tr(pT[:, 128:256], stg[:, 256:384], [bstg], [bT], inc=True)
            self.op(V, "tensor_copy", [bT], [b_iqT], out=iqT[:, 0, tsl], in_=pT[:, 0:128])
            self.op(V, "tensor_copy", [bT], [b_iqT], out=iqT[:, 1, tsl], in_=pT[:, 128:256])
        if STOP == "A1":
            self.fw.barrier()
            return
        stqs = [(self.view(A_R3 + 8320 + 1024 * k, [4, 128], BF16), Buf("stq%d" % k)) for k in range(2)]
        wqs = []
        for hp in range(4):
            wq, bwq = self.wslot()
            self.dma(G, wq[:, :, 0:128], wcols(1544 + hp * 128, 128), writes=[bwq])
            wqs.append((wq, bwq))
        kk = 0
        for hp in range(4):
            wq, bwq = wqs[hp]
            for tb in range(4):
                s32, bs32 = s32s[kk % 2]
                tmp, btmp = tmps[kk % 2]
                stq, bstq = stqs[kk % 2]
                kk += 1
                pA, bA = self.bank("A")
                for tt in range(4):
                    t = 4 * tb + tt
                    for c in range(8):
                        self.mm(pA[:, tt * 128:(tt + 1) * 128], self.hT[:, c, t * 128:(t + 1) * 128], wq[:, c, 0:128],
                                c == 0, c == 7, [self.b_hT, bwq], [bA], inc=(tt == 3 and c == 7))
                self.op(A, "activation", [bA], [bs32], out=s32, in_=pA, func=AF.Copy)
                s4 = s32.rearrange("p (a h d) -> p a h d", h=2, d=64)
                self.rope_inplace(V, s4[:, :, :, 0:16], tmp[:, :, 0:2, :], 4 * tb, 4, 2, [bs32], [bs32], btmp)
                self.op(G, "tensor_copy", [bs32], [bstq], out=stq, in_=s32.rearrange("p (a k) -> p a k", k=128))
                pT, bT = self.bank("T")
                for tt in range(4):
                    self.tr(pT[:, tt * 128:(tt + 1) * 128], stq[:, tt, :], [bstq], [bT], inc=(tt == 3))
                self.op(A, "activation", [bT], [b_dqT], out=dqT[:, hp, tb * 512:(tb + 1) * 512], in_=pT[:, 0:512], func=AF.Copy)
        self.fw.barrier()
        if STOP == "A":
            return
        maskT = self.view(A_HT, [40, 512], BF16)
        b_mask = Buf("maskT")
        scs = [(self.view(A_R3 + 8192 * k, [2048], F32), Buf("sc%d" % k)) for k in range(3)]
        mbs = [(self.view(A_MIX + 4096 + 4096 * k, [2048], BF16), Buf("mb%d" % k)) for k in range(3)]
        rls = [(self.view(A_MIX + 2048 * k, [512], F32), Buf("rl%d" % k)) for k in range(2)]
        bbiss = [Buf("bis0"), Buf("bis1"), Buf("bis2")]
        self.op(V, "tensor_copy", [bc], [b_mask], out=maskT[:, 0, 0:128], in_=self.triT)
        self.op(V, "memset", [], [b_mask], ap=maskT[:, 0, 128:256], constant=0.0)
        self.op(V, "tensor_copy", [bc], [b_mask], out=maskT[:, 1, 128:256], in_=self.triT)
        rkc = [0]

        def scores(qi):
            sc, bsc = scs[qi % 3]
            nk = (qi + 1) * 128
            qsl = slice(qi * 128, (qi + 1) * 128)
            for h in range(4):
                rows = slice(64 * (h % 2), 64 * (h % 2) + 64)
                for kc in range((nk + 511) // 512):
                    n = min(512, nk - kc * 512)
                    ksl = slice(kc * 512, kc * 512 + n)
                    pS, bS = self.bank("S")
                    self.mm(pS[:, 0:n], iqT[rows, h // 2, qsl], ikT2[rows, ksl], True, True, [b_iqT, b_ikT], [bS])
                    if h == 0:
                        self.op(V, "tensor_scalar", [bS, self.b_dkr], [bsc], out=sc[:, ksl], in0=pS[:, 0:n], scalar1=0.0, scalar2=self.iw[:, qi, 0:1],
                                op0=ALU.max, op1=ALU.mult)
                    else:
                        rl, brl = rls[rkc[0] % 2]
                        rkc[0] += 1
                        self.op(V, "tensor_scalar", [bS, self.b_dkr], [brl], out=rl[:, 0:n], in0=pS[:, 0:n], scalar1=0.0, scalar2=self.iw[:, qi, h:h + 1],
                                op0=ALU.max, op1=ALU.mult)
                        self.op(V, "tensor_tensor", [brl, bsc], [bsc], out=sc[:, ksl], in0=sc[:, ksl], in1=rl[:, 0:n], op=ALU.add)
            self.op(V, "tensor_tensor", [bsc, bc], [bsc], out=sc[:, qsl], in0=sc[:, qsl], in1=self.tri32, op=ALU.add)
            bis = self.bis[:, qi % 3, :]
            bbis = bbiss[qi % 3]
            self.op(V, "tensor_reduce", [bsc], [bbis], out=bis[:, 20:21], in_=sc[:, 0:nk], axis=AX.X, op=ALU.max)
            self.op(V, "tensor_reduce", [bsc], [bbis], out=bis[:, 21:22], in_=sc[:, 0:qi * 128], axis=AX.X, op=ALU.min)
            self.op(V, "tensor_tensor", [bbis], [bbis], out=bis[:, 22:23], in0=bis[:, 20:21], in1=bis[:, 21:22], op=ALU.subtract)
            if qi % 2 == 0:
                self.op(V, "tensor_scalar", [bbis, bc], [bbis], out=bis[:, 0:NIT], in0=self.pow2, scalar1=bis[:, 22:23], scalar2=None, op0=ALU.mult)
                self.op(V, "scalar_tensor_tensor", [bbis], [bbis], out=bis[:, 25:26], in0=bis[:, 22:23], scalar=0.5, in1=bis[:, 21:22], op0=ALU.mult, op1=ALU.add)
            else:
                self.op(V, "tensor_scalar", [bbis, bc], [bbis], out=bis[:, 0:NIT], in0=self.pow2, scalar1=bis[:, 22:23], scalar2=-0.5, op0=ALU.mult, op1=ALU.mult)
                self.op(V, "scalar_tensor_tensor", [bbis], [bbis], out=bis[:, 25:26], in0=bis[:, 22:23], scalar=-0.5, in1=bis[:, 21:22], op0=ALU.mult, op1=ALU.subtract)

        def bisect(qi):
            sc, bsc = scs[qi % 3]
            junk, bjunk = mbs[qi % 3]
            bis = self.bis[:, qi % 3, :]
            bbis = bbiss[qi % 3]
            nk = (qi + 1) * 128
            on_act = (qi % 2 == 1)
            if not on_act:
                for k in range(NIT):
                    self.op(V, "tensor_scalar", [bsc, bbis], [bjunk, bbis], out=junk[:, 0:nk], in0=sc[:, 0:nk], scalar1=bis[:, 25:26], scalar2=None,
                            op0=ALU.is_ge, op1=ALU.add, accum_out=bis[:, 23:24])
                    last = (k == NIT - 1)
                    self.op(V, "tensor_scalar", [bbis], [bbis], out=bis[:, 24:25], in0=bis[:, 23:24], scalar1=255.5, scalar2=(-1.0 if last else -0.5),
                            op0=ALU.is_ge, op1=ALU.add)
                    self.op(V, "scalar_tensor_tensor", [bbis], [bbis], out=(bis[:, 26:27] if last else bis[:, 25:26]), in0=bis[:, 24:25],
                            scalar=bis[:, k:k + 1], in1=bis[:, 25:26], op0=ALU.mult, op1=ALU.add)
            else:
                cur, oth = 25, 28
                for k in range(NIT):
                    self.op(A, "activation", [bsc, bbis], [bjunk, bbis], out=junk[:, 0:nk], in_=sc[:, 0:nk], func=AF.Sign, bias=bis[:, cur:cur + 1],
                            accum_out=bis[:, 23:24])
                    self.op(A, "activation", [bbis], [bbis], out=bis[:, 24:25], in_=bis[:, 23:24], func=AF.Sign, bias=float(nk - 511))
                    last = (k == NIT - 1)
                    dst = 27 if last else oth
                    self.op(A, "activation", [bbis], [bbis], out=bis[:, dst:dst + 1], in_=bis[:, 24:25], func=AF.Identity, scale=bis[:, k:k + 1],
                            bias=bis[:, cur:cur + 1])
                    cur, oth = oth, cur

        def finish(qi):
            sc, bsc = scs[qi % 3]
            mb, bmb = mbs[qi % 3]
            bis = self.bis[:, qi % 3, :]
            bbis = bbiss[qi % 3]
            nk = (qi + 1) * 128
            if qi % 2 == 1:
                self.op(V, "tensor_tensor", [bbis], [bbis], out=bis[:, 26:27], in0=bis[:, NIT - 1:NIT], in1=bis[:, 27:28], op=ALU.subtract)
            self.op(V, "tensor_scalar", [bsc, bbis], [bmb], out=mb[:, 0:nk], in0=sc[:, 0:nk], scalar1=bis[:, 26:27], scalar2=NEG, op0=ALU.is_lt, op1=ALU.mult)
            qc = qi // 4
            for j0 in range(0, qi + 1, 8):
                nb = min(8, qi + 1 - j0)
                pT, bT = self.bank("T")
                for jj in range(nb):
                    self.tr(pT[:, jj * 128:(jj + 1) * 128], mb[:, (j0 + jj) * 128:(j0 + jj + 1) * 128], [bmb], [bT], inc=(jj == nb - 1))
                self.op(V, "tensor_copy", [bT], [b_mask], out=maskT[:, MOFF[qc] + j0:MOFF[qc] + j0 + nb, (qi % 4) * 128:(qi % 4 + 1) * 128],
                        in_=pT[:, 0:nb * 128].rearrange("p (a k) -> p a k", k=128))

        scores(2)
        for qi in range(2, 16):
            if qi + 1 <= 15:
                scores(qi + 1)
            bisect(qi)
            if qi - 1 >= 2:
                finish(qi - 1)
        finish(15)
        self.fw.barrier()
        if STOP == "B":
            return
        dk = [(self.view(A_R3 + 4096 * k, [2048], BF16), Buf("dk%d" % k)) for k in range(4)]
        for k_, (ka_, bka_) in enumerate(dk):
            zr = slice(64, 128) if k_ % 2 == 0 else slice(0, 64)
            self.op(G, "memset", [], [bka_], ap=ka_[zr, :], constant=0.0)
        o = A_R3 + 16384
        vaug = [(self.view(o + 4224 * k, [16, 2, 66], BF16), Buf("v%d" % k)) for k in range(2)]
        o += 8448
        PT = [(self.view(o + 1024 * k, [512], BF16), Buf("PT%d" % k)) for k in range(4)]
        o += 4096
        mst, bmst = self.view(o, [16, 128], BF16), Buf("mst")
        o += 4096
        stk, bstk = self.view(o, [4, 2, 64], BF16), Buf("stk")
        o += 1024
        for va, bva in vaug:
            self.op(G, "memset", [], [bva], ap=va[:, :, :, 64:66], constant=1.0)
        wuk, bwuk = self.wslot((2048,))
        wuv, bwuv = self.wslot((2048,))
        self.dma(G, wuk[:, 0:384], dr["w_uk"][i], writes=[bwuk])
        self.dma(G, wuv[:, 0:512], dr["w_uv"][i], writes=[bwuv])
        for hp in range(4):
            st_ = hp % 2
            kp = [dk[2 * st_], dk[2 * st_ + 1]]
            va, bva = vaug[st_]
            for tb in range(4):
                pA, bA = self.bank("A")
                pB, bB = self.bank("A")
                for tt in range(4):
                    tsl = slice((4 * tb + tt) * 128, (4 * tb + tt + 1) * 128)
                    self.mm(pA[:, tt * 96:(tt + 1) * 96], self.ckvT[:, tsl], wuk[:, hp * 96:(hp + 1) * 96], True, True, [self.b_ckvT, bwuk], [bA], inc=(tt == 3))
                for tt in range(4):
                    tsl = slice((4 * tb + tt) * 128, (4 * tb + tt + 1) * 128)
                    self.mm(pB[:, tt * 128:(tt + 1) * 128], self.ckvT[:, tsl], wuv[:, hp * 128:(hp + 1) * 128], True, True, [self.b_ckvT, bwuv], [bB], inc=(tt == 3))
                self.op(A, "activation", [bA], [bstk], out=stk[:, :, :, 16:64], in_=pA[:, 0:384].rearrange("p (a h d) -> p a h d", h=2, d=48), func=AF.Copy)
                self.op(G, "tensor_copy", [self.b_dkr], [bstk], out=stk[:, :, :, 0:16], in_=self.dkr[:, 4 * tb:4 * tb + 4, :].unsqueeze(2).to_broadcast([128, 4, 2, 16]))
                self.op(A, "activation", [bB], [bva], out=va[:, 4 * tb:4 * tb + 4, :, 0:64], in_=pB.rearrange("p (a h d) -> p a h d", h=2, d=64), func=AF.Copy)
                pT, bT = self.bank("T")
                for tt in range(4):
                    self.tr(pT[:, tt * 128:(tt + 1) * 128], stk[:, tt, :, :].rearrange("p h d -> p (h d)"), [bstk], [bT], inc=(tt == 3))
                self.op(V, "tensor_copy", [bT], [kp[0][1]], out=kp[0][0][0:64, tb * 512:(tb + 1) * 512], in_=pT[0:64, 0:512])
                self.op(V, "tensor_copy", [bT], [kp[1][1]], out=kp[1][0][64:128, tb * 512:(tb + 1) * 512], in_=pT[64:128, 0:512])
            for hh in range(2):
                self.attend(dqT[:, hp, :], kp[hh][0], slice(0, 128), [b_dqT, kp[hh][1]], va[:, :, hh, :], [bva], mst, bmst, hh * 64, PT, "dsa",
                            mask=lambda qc, j, c0, N: (maskT[:, MOFF[qc] + j, c0:c0 + N], b_mask))
            self.mixed_to_T(mst, bmst, hp)
        self.outproj_half(wout, 1)
        self.fw.barrier()

    def final_norm(self, y):
        A, V = self.ACT, self.DVE
        junk = (self.view(A_R3 + 4096, [1024], BF16), Buf("junk"))
        ot = [(self.view(A_MIX + 4096 * i, [1024], F32), Buf("ot%d" % i)) for i in range(2)]
        bst = self.b_st
        yr = y.rearrange("(t p) d -> p t d", p=128)
        self.gfin = self.view(A_MIX + 8192, [1024], F32)
        self.dma(self.SP, self.gfin, self.dr["final_norm"].partition_broadcast(128), writes=[self.b_const])
        for t in range(16):
            self.op(A, "activation", [self.bx[t]], [junk[1], bst], out=junk[0], in_=self.x[:, t, :], func=AF.Square,
                    accum_out=self.ss[:, t:t + 1])
        self.op(A, "activation", [bst], [bst], out=self.rstd, in_=self.ss, func=AF.Sqrt, scale=1.0 / D, bias=EPS)
        self.op(V, "reciprocal", [bst], [bst], out=self.rstd, in_=self.rstd)
        outs = []
        for t in range(16):
            o, bo = ot[t % 2]
            self.op(V, "scalar_tensor_tensor", [self.bx[t], bst, self.b_const], [bo], out=o, in0=self.x[:, t, :],
                    scalar=self.rstd[:, t:t + 1], in1=self.gfin, op0=ALU.mult, op1=ALU.mult)
            b = Buf("y%d" % t)
            self.dma(self.SP, yr[:, t, :], o, reads=[bo], writes=[b])
            outs.append(b)
        return outs

    def store_x(self, y):
        yr = y.rearrange("(t p) d -> p t d", p=128)
        outs = []
        for t in range(16):
            b = Buf("y%d" % t)
            self.dma(self.SP, yr[:, t, :], self.x[:, t, :], reads=[self.bx[t]], writes=[b])
            outs.append(b)
        return outs


IN_SPECS = [
    ("x", [S, D], F32), ("positions", [S], I32), ("attn_norm", [4, D], F32), ("ffn_norm", [4, D], F32),
    ("final_norm", [D], F32), ("w_in_even", [2, D, EVEN_IN], F32), ("b_fox_f", [2, 8], F32), ("g_ckv", [2, 128], F32),
    ("w_uk", [2, 128, 384], F32), ("w_uv", [2, 128, 512], F32), ("w_out_even", [2, D, D], F32),
    ("w_in_odd", [2, D, 3072], F32), ("w_out_odd", [2, D, D], F32), ("w_up", [4, D, 2 * DFF], F32),
    ("conv_w", [4, 3, DFF], F32), ("conv_b", [4, DFF], F32), ("w_down", [4, DFF, D], F32),
    ("c_ident", [128, 128], F32), ("c_triT", [128, 128], F32), ("c_tri32", [128, 128], F32),
    ("c_U32", [128, 128], F32), ("c_ones32", [128, 128], F32), ("c_sel", [128, 1024], F32), ("c_invf", [16], F32), ("c_pow2", [NIT], F32),
]


def host_consts():
    i = np.arange(128)
    c = {}
    c["c_ident"] = np.eye(128, dtype=np.float32)
    c["c_triT"] = np.where(i[:, None] > i[None, :], NEG, 0.0).astype(np.float32)
    c["c_tri32"] = np.where(i[None, :] > i[:, None], -1e30, 0.0).astype(np.float32)
    c["c_U32"] = (i[:, None] <= i[None, :]).astype(np.float32)
    c["c_ones32"] = np.ones((128, 128), np.float32)
    c["c_sel"] = np.repeat(np.eye(128, 8, dtype=np.float32), 128, axis=1)
    invf = (500000.0 ** (-np.arange(0, 16, 2, dtype=np.float32) / 16)).astype(np.float32)
    c["c_invf"] = np.concatenate([invf, invf]).astype(np.float32)
    c["c_pow2"] = (2.0 ** -(np.arange(NIT) + 1.0)).astype(np.float32)
    return c


def build_nc(parts, final=True, dbg_names=()):
    nc = bass.Bass("TRN2", target_bir_lowering=False)
    dr = {}
    for name, shape, dt in IN_SPECS:
        dr[name] = nc.dram_tensor(name, shape, dt, kind="ExternalInput").ap()
    y = nc.dram_tensor("y", [S, D], F32, kind="ExternalOutput").ap()
    p = Prog(nc, dr)
    p.load_x()
    p.rope_tables()
    p.fw.barrier()
    for part in parts:
        if part[0] == "ffn":
            p.ffn(part[1])
        elif part[0] == "even":
            p.even(part[1])
        elif part[0] == "odd":
            p.odd(part[1])
    outs = p.final_norm(y) if final else p.store_x(y)
    douts = []
    for name in dbg_names:
        ap = p.dbg[name]
        shp = [128, int(np.prod(ap.shape[1:]))]
        d = nc.dram_tensor("dbg_" + name, shp, ap.dtype, kind="ExternalOutput").ap()
        b = Buf()
        flat = ap
        if len(ap.shape) == 3:
            flat = ap.rearrange("p a b -> p (a b)")
        elif len(ap.shape) == 4:
            flat = ap.rearrange("p a b c -> p (a b c)")
        p.fw.barrier()
        p.dma(p.SP, d, flat, writes=[b])
        douts.append(b)
    p.fw.wait_all(p.SP, outs + douts)
    return nc, p


def make_in_maps(inputs, cores):
    c = host_consts()
    maps = []
    for b in cores:
        m = dict(c)
        for name, shape, dt in IN_SPECS:
            if name.startswith("c_"):
                continue
            a = np.asarray(inputs[name])
            if name in ("x", "positions"):
                a = a[b]
            a = np.ascontiguousarray(a).reshape(shape)
            m[name] = a.astype(np.int32 if dt == I32 else np.float32, copy=False)
        maps.append(m)
    return maps


ALL_PARTS = [("even", 0), ("ffn", 0), ("odd", 0), ("ffn", 1), ("even", 1), ("ffn", 2), ("odd", 1), ("ffn", 3)]


def kernel(**inputs):
    nc, _ = build_nc(ALL_PARTS, final=True)
    maps = make_in_maps(inputs, list(range(8)))
    res = run_bass_kernel_spmd(nc, maps, core_ids=list(range(8)))
    return np.stack([np.asarray(r["y"], dtype=np.float32) for r in res.results], axis=0)
```

```python
import numpy as np
import concourse.bass as bass
import concourse.mybir as mybir
from concourse.bass_utils import run_bass_kernel_spmd

F32 = mybir.dt.float32
BF16 = mybir.dt.bfloat16
I32 = mybir.dt.int32
ALU = mybir.AluOpType
AF = mybir.ActivationFunctionType
AX = mybir.AxisListType

SEM_ROT = 16000
import os
STOP = os.environ.get('EVEN_STOP', '')


class Buf:
    __slots__ = ("w", "r", "name")

    def __init__(self, name=""):
        self.w = None
        self.r = {}
        self.name = name


class EngCtx:
    def __init__(self, fw, name, eng, order_raw):
        self.fw, self.name, self.eng = fw, name, eng
        self.sem = fw.new_sem(name)
        self.count = 0
        self.seen = {}
        self.order_raw = order_raw
        self.nrot = 0
        self.prev = None

    def rotate(self):
        if self.count >= SEM_ROT:
            self.nrot += 1
            self.prev = (self.sem, self.count)
            self.sem = self.fw.new_sem("%s_r%d" % (self.name, self.nrot))
            self.count = 0


class FW:
    def __init__(self, nc):
        self.nc = nc
        self.nsem = 0
        self.pe = EngCtx(self, "pe", nc.tensor, False)
        self.act = EngCtx(self, "act", nc.scalar, True)
        self.dve = EngCtx(self, "dve", nc.vector, True)
        self.pool = EngCtx(self, "pool", nc.gpsimd, True)
        self.sp = EngCtx(self, "sp", nc.sync, True)
        self.dma_pools = {}
        for q in (self.sp, self.pool):
            self.dma_pools[id(q)] = [[[self.new_sem("dma_%s%d" % (q.name, i)), 0] for i in range(12)], 0]
        self.n_ops = 0

    def new_sem(self, name):
        self.nsem += 1
        return self.nc.alloc_semaphore(name="s_%s_%d" % (name, self.nsem))

    def _deps(self, E, order_raw, seen, reads, writes):
        deps = {}

        def add(d, raw):
            if d is None:
                return
            key, sem, val = d
            if E is not None and key is E:
                if not order_raw:
                    return
            k = id(sem)
            if k not in deps or deps[k][1] < val:
                deps[k] = (sem, val)

        for b in reads:
            add(b.w, True)
        for b in writes:
            add(b.w, False)
            for d in b.r.values():
                add(d, False)
        out = []
        for k, (sem, val) in deps.items():
            if seen.get(k, -1) >= val:
                continue
            out.append((sem, val))
        return out

    def _emit_waits(self, E, ins_fn, deps):
        for sem, val in deps[1:]:
            E.eng.wait_ge(sem, val)
            E.seen[id(sem)] = max(E.seen.get(id(sem), -1), val)
        ins = ins_fn()
        if deps:
            sem, val = deps[0]
            ins._wait_ge(sem, val)
            E.seen[id(sem)] = max(E.seen.get(id(sem), -1), val)
        return ins

    def op(self, E, fn, reads=(), writes=(), inc=True):
        self.n_ops += 1
        deps = self._deps(E, E.order_raw, E.seen, reads, writes)
        ins = self._emit_waits(E, fn, deps)
        val = E.count + 1
        if inc:
            ins.then_inc(E.sem, 1)
            E.count = val
        rec = (E, E.sem, val)
        for b in writes:
            b.w = rec
            b.r = {}
        for b in reads:
            b.r[id(E)] = rec
        if inc:
            E.rotate()
        return ins

    def dma(self, Q, out_ap, in_ap, reads=(), writes=(), **kw):
        self.n_ops += 1
        pool = self.dma_pools[id(Q)]
        slot = pool[0][pool[1]]
        pool[1] = (pool[1] + 1) % len(pool[0])
        if slot[1] >= SEM_ROT:
            Q.eng.wait_ge(slot[0], slot[1])
            slot[0] = self.new_sem("dmar")
            slot[1] = 0
        sem = slot[0]
        deps = self._deps(None, True, Q.seen, reads, writes)
        if slot[1] > 0 and Q.seen.get(id(sem), -1) < slot[1]:
            deps = [d for d in deps if d[0] is not sem]
            deps.append((sem, slot[1]))
        deps = [d for d in deps if Q.seen.get(id(d[0]), -1) < d[1]]
        ins = self._emit_waits(Q, lambda: Q.eng.dma_start(out=out_ap, in_=in_ap, **kw), deps)
        slot[1] += 16
        ins.then_inc(sem, 16)
        rec = (slot, sem, slot[1])
        for b in writes:
            b.w = rec
            b.r = {}
        for b in reads:
            b.r[id(slot)] = rec
        return ins

    def wait_all(self, E, bufs):
        deps = self._deps(None, True, E.seen, bufs, ())
        for sem, val in deps:
            E.eng.wait_ge(sem, val)
            E.seen[id(sem)] = val


    def barrier(self):
        engs = [self.pe, self.act, self.dve, self.pool, self.sp]
        items = [(e.sem, e.count) for e in engs if e.count > 0]
        for pl in self.dma_pools.values():
            items += [(s[0], s[1]) for s in pl[0] if s[1] > 0]
        items += [e.prev for e in engs if e.prev is not None]
        for E in engs:
            for sem, val in items:
                if sem is E.sem:
                    continue
                if E.seen.get(id(sem), -1) >= val:
                    continue
                E.eng.wait_ge(sem, val)
                E.seen[id(sem)] = val


S = 2048
D = 1024
NT = 16
DFF = 2816
NFC = 22
EPS = 1e-6
NEG = -30000.0
NIT = 16
EVEN_IN = 2524
IDX_SCALE = (64 ** -0.5) * (4 ** -0.5)
MOFF = [0, 4, 12, 24]

A_X = 0
A_HT = 65536
A_MIX = 106496
A_R1 = 122880
A_R3 = 139264
A_CKV = 176128
A_W = 180224
A_C = 196608
A_END = 208896


def _sz(dt):
    return 4 if dt in (F32, I32) else 2


class Prog:
    def __init__(self, nc, dr, dbg=None):
        self.nc = nc
        self.dr = dr
        self.fw = FW(nc)
        self.arena = nc.alloc_sbuf_tensor("arena", [128, A_END // 4], F32)
        fw = self.fw
        self.PE, self.ACT, self.DVE, self.POOL, self.SP = fw.pe, fw.act, fw.dve, fw.pool, fw.sp
        self.ps = {}
        for n in ("A0", "A1", "S0", "S1", "O0", "O1"):
            self.ps[n] = (nc.alloc_psum_tensor("p" + n, [128, 512], F32)[:, :], Buf("p" + n))
        for n in ("T0", "T1"):
            self.ps[n] = (nc.alloc_psum_tensor("p" + n, [128, 1024], BF16)[:, :], Buf("p" + n))
        self.rot = {}
        self.dbg = dbg if dbg is not None else {}
        self.dbg_out = []
        self._consts()

    def view(self, off, shape, dt):
        n = int(np.prod(shape))
        sz = _sz(dt)
        assert off % 4 == 0 and (n * sz) % 4 == 0, (off, shape)
        a = self.arena[:, off // 4:(off + n * sz) // 4]
        if dt != F32:
            a = a.bitcast(dt)
        if len(shape) == 2:
            a = a.rearrange("p (a b) -> p a b", b=shape[1])
        elif len(shape) == 3:
            a = a.rearrange("p (a b c) -> p a b c", b=shape[1], c=shape[2])
        elif len(shape) == 4:
            a = a.rearrange("p (a b c d) -> p a b c d", b=shape[1], c=shape[2], d=shape[3])
        return a

    def bank(self, kind):
        k = self.rot.get(kind, 0)
        self.rot[kind] = k ^ 1
        return self.ps[kind + str(k)]

    def op(self, E, meth, reads, writes, inc=True, **kw):
        return self.fw.op(E, lambda: getattr(E.eng, meth)(**kw), reads, writes, inc)

    def mm(self, out, lhsT, rhs, start, stop, reads, writes, inc=True, skip=False):
        kw = dict(out=out, lhsT=lhsT, rhs=rhs, start=start, stop=stop)
        if skip:
            kw["skip_group_check"] = True
        return self.fw.op(self.PE, lambda: self.nc.tensor.matmul(**kw), reads, writes, inc)

    def tr(self, out, in_, reads, writes, inc=True):
        return self.fw.op(self.PE, lambda: self.nc.tensor.transpose(out=out, in_=in_, identity=self.ident),
                          list(reads) + [self.b_const], writes, inc)

    def dma(self, Q, out, in_, reads=(), writes=(), **kw):
        return self.fw.dma(Q, out, in_, reads, writes, **kw)

    def _consts(self):
        nc, dr = self.nc, self.dr
        o = [A_C]

        def take(shape, dt):
            n = int(np.prod(shape)) * _sz(dt)
            n = (n + 31) // 32 * 32
            v = self.view(o[0], shape, dt)
            o[0] += n
            assert o[0] <= A_END, o[0]
            return v
        self.b_const = Buf("const")
        bc = [self.b_const]
        self.ident = take([128], BF16)
        self.triT = take([128], BF16)
        self.zeros = take([264], BF16)
        self.tri32 = take([128], F32)
        self.U32 = take([128], F32)
        self.ones32 = take([128], F32)
        self.anorm = take([4, 8], F32)
        self.fnorm = take([4, 8], F32)
        self.cos2 = take([16, 16], F32)
        self.sin2 = take([16, 16], F32)
        self.invf = take([16], F32)
        self.pow2 = take([NIT], F32)
        self.pos_i = take([16], I32)
        self.pos_f = take([16], F32)
        self.ang = self.view(A_R3, [16, 16], F32)
        self.angk = self.view(A_R3 + 1024, [16, 16], F32)
        self.angi = self.view(A_R3 + 2048, [16, 16], I32)
        self.ss = take([16], F32)
        self.rstd = take([16], F32)
        self.convw = take([4, 3, NFC], F32)
        self.convb = take([4, NFC], F32)
        self.bfox = take([2, 8], F32)
        self.gckv = take([2], F32)
        self.fb = take([16, 4, 8], F32)
        self.csb = take([16, 8], F32)
        self.cref = take([4, 8], F32)
        self.zf = self.view(A_R3, [16, 8], F32)
        self.zf2 = self.view(A_R3 + 512, [16, 8], F32)
        self.iw = take([16, 4], F32)
        self.st = take([64], F32)
        self.b_st = Buf("st")
        self.dkr = take([16, 16], BF16)
        self.b_dkr = Buf("dkr")
        self.bis = take([3, 32], F32)
        self.kmT32 = take([2, 8], F32)
        self.kmT = take([2, 8], BF16)
        self.b_km = Buf("km")
        self.b_fb = Buf("fb")
        self.ptk = 0
        SP, POOL = self.SP, self.POOL
        self.dma(POOL, self.ident, dr["c_ident"], writes=bc)
        self.dma(POOL, self.triT, dr["c_triT"], writes=bc)
        self.dma(SP, self.tri32, dr["c_tri32"], writes=bc)
        self.dma(SP, self.U32, dr["c_U32"], writes=bc)
        self.dma(SP, self.ones32, dr["c_ones32"], writes=bc)
        self.dma(SP, self.anorm, dr["attn_norm"].rearrange("l (c p) -> p l c", p=128), writes=bc,
                 allow_slow_non_contiguous=True)
        self.dma(SP, self.fnorm, dr["ffn_norm"].rearrange("l (c p) -> p l c", p=128), writes=bc,
                 allow_slow_non_contiguous=True)
        self.dma(SP, self.invf, dr["c_invf"].partition_broadcast(128), writes=bc)
        self.dma(SP, self.pow2, dr["c_pow2"].partition_broadcast(128), writes=bc)
        self.dma(SP, self.pos_i, dr["positions"].rearrange("(t p) -> p t", p=128), writes=bc,
                 allow_slow_non_contiguous=True)
        self.dma(SP, self.convw, dr["conv_w"].rearrange("l j (c p) -> p l j c", p=128), writes=bc,
                 allow_slow_non_contiguous=True)
        self.dma(SP, self.convb, dr["conv_b"].rearrange("l (c p) -> p l c", p=128), writes=bc,
                 allow_slow_non_contiguous=True)
        self.dma(SP, self.bfox.rearrange("p a b -> p (a b)"), dr["b_fox_f"].rearrange("a b -> (a b)").partition_broadcast(128), writes=bc)
        self.dma(SP, self.gckv, dr["g_ckv"].rearrange("l p -> p l"), writes=bc, allow_slow_non_contiguous=True)
        self.op(self.DVE, "memset", [], bc, ap=self.zeros, constant=0.0)
        self.x = self.view(A_X, [16, 1024], F32)
        self.bx = [Buf("x%d" % t) for t in range(16)]
        self.hT = self.view(A_HT, [8, 2048], BF16)
        self.b_hT = Buf("hT")
        self.mixT = self.view(A_MIX, [4, 2048], BF16)
        self.b_mixT = Buf("mixT")
        self.ckvT = self.view(A_CKV, [2048], BF16)
        self.b_ckvT = Buf("ckvT")
        self.wslots = [(A_W + 4096 * i, Buf("w%d" % i)) for i in range(4)]
        self.wnext = 0

    def wslot(self, shape=(8, 256)):
        off, b = self.wslots[self.wnext]
        self.wnext = (self.wnext + 1) % 4
        return self.view(off, list(shape), BF16), b

    def load_x(self):
        xr = self.dr["x"].rearrange("(t p) d -> p t d", p=128)
        for t in range(16):
            self.dma(self.SP, self.x[:, t, :], xr[:, t, :], writes=[self.bx[t]])

    def rope_tables(self):
        V, A = self.DVE, self.ACT
        bc = [self.b_const]
        TWO_PI = 2.0 * np.pi
        C1 = 6.28125
        C2 = TWO_PI - C1
        self.op(V, "tensor_copy", bc, bc, out=self.pos_f, in_=self.pos_i)
        for which, dst in ((0, self.sin2), (1, self.cos2)):
            self.op(V, "tensor_tensor", bc, bc, out=self.ang,
                    in0=self.pos_f.unsqueeze(2).to_broadcast([128, 16, 16]),
                    in1=self.invf.unsqueeze(1).to_broadcast([128, 16, 16]), op=ALU.mult)
            if which == 1:
                self.op(V, "tensor_scalar", bc, bc, out=self.ang, in0=self.ang, scalar1=float(np.pi / 2), scalar2=None, op0=ALU.add)
            self.op(V, "tensor_scalar", bc, bc, out=self.angk, in0=self.ang, scalar1=float(1.0 / TWO_PI), scalar2=None, op0=ALU.mult)
            self.op(V, "tensor_copy", bc, bc, out=self.angi, in_=self.angk)
            self.op(V, "tensor_copy", bc, bc, out=self.angk, in_=self.angi)
            self.op(V, "scalar_tensor_tensor", bc, bc, out=self.ang, in0=self.angk, scalar=-C1, in1=self.ang, op0=ALU.mult, op1=ALU.add)
            self.op(V, "scalar_tensor_tensor", bc, bc, out=self.ang, in0=self.angk, scalar=-C2, in1=self.ang, op0=ALU.mult, op1=ALU.add)
            self.op(V, "tensor_scalar", bc, bc, out=self.ang, in0=self.ang, scalar1=3.1415925, scalar2=-3.1415925, op0=ALU.min, op1=ALU.max)
            self.op(A, "activation", bc, bc, out=dst, in_=self.ang, func=AF.Sin)

    def norm_to_hT(self, gain):
        A, V, G = self.ACT, self.DVE, self.POOL
        xh = [(self.view(A_R3 + 2048 * i, [1024], BF16), Buf("xh%d" % i)) for i in range(2)]
        junk = (self.view(A_R3 + 4096, [1024], BF16), Buf("junk"))
        bst = self.b_st
        for t in range(16):
            self.op(A, "activation", [self.bx[t]], [junk[1], bst], out=junk[0], in_=self.x[:, t, :], func=AF.Square,
                    accum_out=self.ss[:, t:t + 1])
        self.op(A, "activation", [bst], [bst], out=self.rstd, in_=self.ss, func=AF.Sqrt, scale=1.0 / D, bias=EPS)
        self.op(V, "reciprocal", [bst], [bst], out=self.rstd, in_=self.rstd)
        for t in range(16):
            xa, xb = xh[t % 2]
            self.op(A, "activation", [self.bx[t], bst], [xb], out=xa, in_=self.x[:, t, :], func=AF.Copy, scale=self.rstd[:, t:t + 1])
            pT, bT = self.bank("T")
            for c in range(8):
                self.tr(pT[:, c * 128:(c + 1) * 128], xa[:, c * 128:(c + 1) * 128], [xb], [bT], inc=(c == 7))
            self.op(V, "tensor_tensor", [bT, self.b_const], [self.b_hT], out=self.hT[:, :, t * 128:(t + 1) * 128],
                    in0=pT.rearrange("p (c k) -> p c k", k=128), in1=gain.unsqueeze(2).to_broadcast([128, 8, 128]), op=ALU.mult)

    def ffn(self, layer):
        A, V, G = self.ACT, self.DVE, self.POOL
        dr = self.dr
        bc = self.b_const
        self.norm_to_hT(self.fnorm[:, layer, :])
        self.fw.barrier()
        base = A_MIX
        actT = [(self.view(base + 24576 * i, [6, 2048], BF16), Buf("actT%d" % i)) for i in range(2)]
        gsb = [(self.view(base + 49152 + 8224 * i, [2056], F32), [Buf("gsb%d_%d" % (i, k)) for k in range(4)]) for i in range(2)]
        sp = A_HT + 32768
        acc = [(self.view(sp + 2048 * i, [512], F32), Buf("acc%d" % i)) for i in range(2)]
        sg = [(self.view(sp + 4096 + 2048 * i, [512], F32), Buf("sg%d" % i)) for i in range(2)]
        for gs, bgs in gsb:
            self.op(V, "memset", [], [bgs[0]], ap=gs[:, 0:2], constant=0.0)
        wup = dr["w_up"][layer]
        wdn = dr["w_down"][layer]
        groups = [(0, 6), (6, 12), (12, 18), (18, 22)]
        k = 0
        for g, (f0, f1) in enumerate(groups):
            aT, baT = actT[g % 2]
            for fc in range(f0, f1):
                ws, bws = self.wslot()
                self.dma(G, ws[:, :, 0:128], wup[:, fc * 128:(fc + 1) * 128].rearrange("(c p) n -> p c n", p=128), writes=[bws])
                self.dma(G, ws[:, :, 128:256], wup[:, DFF + fc * 128:DFF + (fc + 1) * 128].rearrange("(c p) n -> p c n", p=128), writes=[bws])
                gs, bgs = gsb[fc % 2]
                for tc in range(4):
                    tsl = slice(tc * 512, (tc + 1) * 512)
                    pG, bG = self.bank("A")
                    pU, bU = self.bank("S")
                    for c in range(8):
                        self.mm(pG, ws[:, c, 0:128], self.hT[:, c, tsl], c == 0, c == 7, [bws, self.b_hT], [bG], inc=(c == 7))
                    for c in range(8):
                        self.mm(pU, ws[:, c, 128:256], self.hT[:, c, tsl], c == 0, c == 7, [bws, self.b_hT], [bU], inc=(c == 7))
                    self.op(A, "activation", [bG], [bgs[tc]], out=gs[:, 2 + tc * 512:2 + (tc + 1) * 512], in_=pG, func=AF.Copy)
                    ac, bac = acc[k % 2]
                    sgt, bsg = sg[k % 2]
                    k += 1
                    rd = [bgs[tc], bc] + ([bgs[tc - 1]] if tc > 0 else [bgs[0]])
                    self.op(V, "tensor_scalar", rd, [bac], out=ac, in0=gs[:, 2 + tc * 512:2 + (tc + 1) * 512],
                            scalar1=self.convw[:, layer, 2, fc:fc + 1], scalar2=self.convb[:, layer, fc:fc + 1], op0=ALU.mult, op1=ALU.add)
                    self.op(V, "scalar_tensor_tensor", rd + [bac], [bac], out=ac, in0=gs[:, 1 + tc * 512:1 + (tc + 1) * 512],
                            scalar=self.convw[:, layer, 1, fc:fc + 1], in1=ac, op0=ALU.mult, op1=ALU.add)
                    self.op(V, "scalar_tensor_tensor", rd + [bac], [bac], out=ac, in0=gs[:, tc * 512:(tc + 1) * 512],
                            scalar=self.convw[:, layer, 0, fc:fc + 1], in1=ac, op0=ALU.mult, op1=ALU.add)
                    self.op(A, "activation", [bac], [bsg], out=sgt, in_=ac, func=AF.Silu)
                    self.op(V, "tensor_tensor", [bsg, bU], [baT], out=aT[:, fc - f0, tsl], in0=sgt, in1=pU, op=ALU.mult)
            nf = f1 - f0
            wd = []
            for s in range((nf + 1) // 2):
                ws, bws = self.wslot((2, 1024))
                r0 = (f0 + 2 * s) * 128
                self.dma(G, ws, wdn[r0:r0 + 256, :].rearrange("(c p) n -> p c n", p=128), writes=[bws])
                wd.append((ws, bws))
            for t in range(16):
                for nh in range(2):
                    pO, bO = self.bank("O")
                    nsl = slice(nh * 512, (nh + 1) * 512)
                    for fl in range(nf):
                        ws, bws = wd[fl // 2]
                        self.mm(pO, aT[:, fl, t * 128:(t + 1) * 128], ws[:, fl % 2, nsl], fl == 0, fl == nf - 1, [baT, bws], [bO],
                                inc=(fl == nf - 1))
                    self.op(V, "tensor_tensor", [bO, self.bx[t]], [self.bx[t]], out=self.x[:, t, nsl], in0=self.x[:, t, nsl], in1=pO, op=ALU.add)
        self.fw.barrier()

    def attend(self, qT, kT, rows, rd_qk, v, rd_v, stage, bstage, scol, PT, mode, bias=None, mask=None, qbias=None):
        A, V = self.ACT, self.DVE
        for qc in range(4):
            pO, bO = self.bank("O")
            self.mm(pO[:, 0:264], self.zeros[:, 0:128], self.zeros[:, 0:264], True, True, [self.b_const], [bO])
            pend = None

            def pv(args):
                j, q0, pt, bpt = args
                for t in range(q0 // 128, 4 * qc + 4):
                    c0 = (t - 4 * qc) * 66
                    self.mm(pO[:, c0:c0 + 65], pt[:, t * 128 - q0:t * 128 - q0 + 128], v[:, j, 0:65], False, (j == t),
                            [bpt] + rd_v, [bO], inc=(t == 4 * qc + 3), skip=True)
            for j in range(4 * qc + 4):
                q0 = max(512 * qc, 128 * j)
                N = 512 * qc + 512 - q0
                pS, bS = self.bank("S")
                diag = (mode != "dsa") and (j >= 4 * qc)
                ksl = slice(j * 128, (j + 1) * 128)
                groups = [(0, 128, True), (128, N - 128, False)] if diag else [(0, N, False)]
                groups = [g for g in groups if g[1] > 0]
                for gi, (c0, n, tri) in enumerate(groups):
                    terms = []
                    if tri:
                        terms.append((self.ident, self.triT, [self.b_const]))
                    if mode == "dsa":
                        m_ap, m_b = mask(qc, j, q0 - 512 * qc + c0, n)
                        terms.append((self.ident, m_ap, [self.b_const, m_b]))
                    if qbias is not None:
                        l_ap, r_ap, r_b = qbias(q0 + c0, n)
                        terms.append((l_ap, r_ap, r_b))
                    lastg = (gi == len(groups) - 1)
                    self.mm(pS[:, c0:c0 + n], kT[rows, ksl], qT[rows, q0 + c0:q0 + c0 + n], True, not terms, rd_qk, [bS],
                            inc=(lastg and not terms))
                    for ti, (l_ap, r_ap, r_b) in enumerate(terms):
                        lt = (ti == len(terms) - 1)
                        self.mm(pS[:, c0:c0 + n], l_ap, r_ap, False, lt, r_b, [bS], inc=(lastg and lt))
                pt, bpt = PT[self.ptk % len(PT)]
                self.ptk += 1
                kw = dict(out=pt[:, 0:N], in_=pS[:, 0:N], func=AF.Exp, scale=0.125)
                rds = [bS]
                if bias is not None:
                    kw["bias"] = bias(j, qc)
                    rds.append(self.b_fb)
                self.op(A, "activation", rds, [bpt], **kw)
                if pend is not None:
                    pv(pend)
                pend = (j, q0, pt, bpt)
            pv(pend)
            for t in range(4 * qc, 4 * qc + 4):
                c0 = (t - 4 * qc) * 66
                ri = self.st[:, (t % 8):(t % 8) + 1]
                self.op(V, "reciprocal", [bO], [self.b_st], out=ri, in_=pO[:, c0 + 64:c0 + 65])
                self.op(V, "tensor_scalar", [bO, self.b_st], [bstage], out=stage[:, t, scol:scol + 64], in0=pO[:, c0:c0 + 64],
                        scalar1=ri, scalar2=None, op0=ALU.mult)

    def rope_inplace(self, E, data, tmp, t0, nt, nh, rd, wr, btmp):
        cs = self.cos2[:, t0:t0 + nt, :].unsqueeze(2).to_broadcast([128, nt, nh, 16])
        sn = self.sin2[:, t0:t0 + nt, :].unsqueeze(2).to_broadcast([128, nt, nh, 16])
        bc = self.b_const
        self.op(E, "tensor_tensor", rd + [bc], [btmp], out=tmp, in0=data, in1=sn, op=ALU.mult)
        self.op(E, "tensor_tensor", rd + [bc], wr, out=data, in0=data, in1=cs, op=ALU.mult)
        self.op(E, "tensor_tensor", rd + [btmp], wr, out=data[:, :, :, 0:8], in0=data[:, :, :, 0:8], in1=tmp[:, :, :, 8:16], op=ALU.subtract)
        self.op(E, "tensor_tensor", rd + [btmp], wr, out=data[:, :, :, 8:16], in0=data[:, :, :, 8:16], in1=tmp[:, :, :, 0:8], op=ALU.add)

    def outproj_half(self, w, half):
        V, G = self.DVE, self.POOL
        for ng in range(2):
            ws, bws = self.wslot((4, 512))
            self.dma(G, ws, w[half * 512:(half + 1) * 512, ng * 512:(ng + 1) * 512].rearrange("(c p) n -> p c n", p=128), writes=[bws])
            nsl = slice(ng * 512, (ng + 1) * 512)
            for t in range(16):
                pA, bA = self.bank("A")
                for c in range(4):
                    self.mm(pA, self.mixT[:, c, t * 128:(t + 1) * 128], ws[:, c, :], c == 0, c == 3, [self.b_mixT, bws], [bA], inc=(c == 3))
                self.op(V, "tensor_tensor", [bA, self.bx[t]], [self.bx[t]], out=self.x[:, t, nsl], in0=self.x[:, t, nsl], in1=pA, op=ALU.add)

    def mixed_to_T(self, mst, bmst, slot):
        for half in range(2):
            pT, bT = self.bank("T")
            for tt in range(8):
                self.tr(pT[:, tt * 128:(tt + 1) * 128], mst[:, half * 8 + tt, :], [bmst], [bT], inc=(tt == 7))
            self.op(self.ACT, "activation", [bT], [self.b_mixT], out=self.mixT[:, slot, half * 1024:(half + 1) * 1024], in_=pT, func=AF.Copy)

    def odd(self, i):
        A, V, G = self.ACT, self.DVE, self.POOL
        layer = 2 * i + 1
        dr = self.dr
        bc = self.b_const
        win = dr["w_in_odd"][i]
        self.norm_to_hT(self.anorm[:, layer, :])
        self.fw.barrier()
        qT = [(self.view(A_R1 + 4096 * k, [2048], BF16), Buf("qT%d" % k)) for k in range(4)]
        kT = [(self.view(A_R3 + 4096 * k, [2048], BF16), Buf("kT%d" % k)) for k in range(4)]
        o = A_R3 + 16384
        vaug = [(self.view(o + 4224 * k, [16, 2, 66], BF16), Buf("v%d" % k)) for k in range(2)]
        o += 8448
        stgq, bsq = self.view(o, [16, 2, 72], BF16), Buf("stgq")
        o += 4608
        stgk, bsk = self.view(o, [16, 2, 72], BF16), Buf("stgk")
        o += 4608
        s32, bs32 = self.view(o, [512], F32), Buf("s32")
        o += 2048
        assert o <= A_R3 + 36864
        sp = A_HT + 32768
        PT = [(self.view(sp + 1024 * k, [512], BF16), Buf("PT%d" % k)) for k in range(4)]
        mst, bmst = self.view(sp + 4096, [16, 128], BF16), Buf("mst")
        tmp, btmp = self.view(A_CKV, [4, 2, 16], F32), Buf("ropetmp")
        gt, bgt = self.view(A_CKV + 1024, [8, 2, 8], F32), Buf("gt")
        rank, brk = self.view(A_CKV + 2048, [8, 2, 8], F32), Buf("rank")
        cmp_, bcmp = self.view(A_CKV + 3072, [8, 2, 8], F32), Buf("cmp")
        for va, bva in vaug:
            self.op(G, "memset", [], [bva], ap=va[:, :, :, 64:66], constant=1.0)
        self.op(G, "memset", [], [bsq], ap=stgq[:, :, :, 64:72], constant=0.0)
        self.op(G, "memset", [], [bsk], ap=stgk[:, :, :, 64:72], constant=0.0)
        for b in range(8):
            self.op(G, "memset", [], [bsk], ap=stgk[:, 2 * b:2 * b + 2, :, 64 + b:65 + b], constant=1.0)
        for qa, bqa in qT:
            self.op(G, "memset", [], [bqa], ap=qa[64:128, :], constant=0.0)
        for ka_, bka_ in kT:
            self.op(G, "memset", [], [bka_], ap=ka_[64:128, :], constant=0.0)
        def load_pair(hp):
            wqk, bwqk = self.wslot()
            wv, bwv = self.wslot()
            self.dma(G, wqk[:, :, 0:128], win[:, hp * 128:(hp + 1) * 128].rearrange("(c p) n -> p c n", p=128), writes=[bwqk])
            self.dma(G, wqk[:, :, 128:256], win[:, 1024 + hp * 128:1024 + (hp + 1) * 128].rearrange("(c p) n -> p c n", p=128), writes=[bwqk])
            self.dma(G, wv[:, :, 0:128], win[:, 2048 + hp * 128:2048 + (hp + 1) * 128].rearrange("(c p) n -> p c n", p=128), writes=[bwv])
            return wqk, bwqk, wv, bwv
        wts = {0: load_pair(0)}

        def prep1(hp):
            st_ = hp % 2
            wqk, bwqk, wv, bwv = wts[hp]
            va, bva = vaug[st_]
            for tb in range(4):
                for which, stg, bstg in ((0, stgq, bsq), (1, stgk, bsk)):
                    pA, bA = self.bank("A")
                    for tt in range(4):
                        t = 4 * tb + tt
                        for c in range(8):
                            self.mm(pA[:, tt * 128:(tt + 1) * 128], self.hT[:, c, t * 128:(t + 1) * 128], wqk[:, c, which * 128:(which + 1) * 128],
                                    c == 0, c == 7, [self.b_hT, bwqk], [bA], inc=(tt == 3 and c == 7))
                    self.op(A, "activation", [bA], [bs32], out=s32, in_=pA, func=AF.Copy)
                    s4 = s32.rearrange("p (a h d) -> p a h d", h=2, d=64)
                    self.rope_inplace(V, s4[:, :, :, 0:16], tmp, 4 * tb, 4, 2, [bs32], [bs32], btmp)
                    self.op(G, "tensor_copy", [bs32], [bstg], out=stg[:, 4 * tb:4 * tb + 4, :, 0:64], in_=s4)
                pA, bA = self.bank("A")
                for tt in range(4):
                    t = 4 * tb + tt
                    for c in range(8):
                        self.mm(pA[:, tt * 128:(tt + 1) * 128], self.hT[:, c, t * 128:(t + 1) * 128], wv[:, c, 0:128],
                                c == 0, c == 7, [self.b_hT, bwv], [bA], inc=(tt == 3 and c == 7))
                self.op(A, "activation", [bA], [bva], out=va[:, 4 * tb:4 * tb + 4, :, 0:64],
                        in_=pA.rearrange("p (a h d) -> p a h d", h=2, d=64), func=AF.Copy)
            if hp < 7:
                wts[hp + 1] = load_pair(hp + 1)
            for hh in range(2):
                qa, bqa = qT[2 * st_ + hh]
                ka, bka = kT[2 * st_ + hh]
                for half in range(2):
                    pT, bT = self.bank("T")
                    for tt in range(8):
                        self.tr(pT[0:64, tt * 128:(tt + 1) * 128], stgq[:, half * 8 + tt, hh, 0:64], [bsq], [bT], inc=(tt == 7))
                    self.op(A, "activation", [bT], [bqa], out=qa[0:64, half * 1024:(half + 1) * 1024], in_=pT[0:64, :], func=AF.Copy)
                    pT, bT = self.bank("T")
                    for tt in range(8):
                        self.tr(pT[0:72, tt * 128:(tt + 1) * 128], stgk[:, half * 8 + tt, hh, 0:72], [bsk], [bT], inc=(tt == 7))
                    self.op(V, "tensor_copy", [bT], [bka], out=ka[0:72, half * 1024:(half + 1) * 1024], in_=pT[0:72, :])
                self.op(V, "tensor_reduce", [bka], [self.b_km], out=self.kmT32[0:64, hh, :],
                        in_=ka[0:64, :].rearrange("p (n k) -> p n k", k=256), axis=AX.X, op=ALU.add)
            self.op(V, "tensor_scalar", [self.b_km], [self.b_km], out=self.kmT[0:64, :, :], in0=self.kmT32[0:64, :, :], scalar1=1.0 / 256, scalar2=None, op0=ALU.mult)
            pA, bA = self.bank("A")
            for t in range(8, 16):
                for hh in range(2):
                    qa, bqa = qT[2 * st_ + hh]
                    c0 = ((t - 8) * 2 + hh) * 8
                    self.mm(pA[:, c0:c0 + 8], qa[0:64, t * 128:(t + 1) * 128], self.kmT[0:64, hh, :], True, True, [bqa, self.b_km], [bA],
                            inc=(t == 15 and hh == 1))
            self.op(V, "tensor_copy", [bA], [bgt], out=gt, in_=pA[:, 0:128].rearrange("p (a h n) -> p a h n", h=2, n=8))
            for n1 in range(7):
                a0 = max(0, n1 - 3) * 2 if n1 >= 4 else 0
                na = 8 - a0
                dst = rank if n1 == 0 else cmp_
                bd = brk if n1 == 0 else bcmp
                self.op(V, "tensor_tensor", [bgt], [bd], out=dst[:, a0:8, :, :], in0=gt[:, a0:8, :, n1:n1 + 1].to_broadcast([128, na, 2, 8]),
                        in1=gt[:, a0:8, :, :], op=ALU.is_gt)
                if n1 > 0:
                    self.op(V, "tensor_tensor", [brk, bcmp], [brk], out=rank[:, a0:8, :, :], in0=rank[:, a0:8, :, :], in1=cmp_[:, a0:8, :, :], op=ALU.add)
            for t in range(8, 16):
                ob = t // 2
                self.op(V, "tensor_scalar", [brk], [bsq], out=stgq[:, t, :, 64:64 + ob], in0=rank[:, t - 8, :, 0:ob], scalar1=2.5, scalar2=NEG,
                        op0=ALU.is_gt, op1=ALU.mult)

        def prep2(hp):
            st_ = hp % 2
            for hh in range(2):
                qa, bqa = qT[2 * st_ + hh]
                pT, bT = self.bank("T")
                for tt in range(8):
                    self.tr(pT[0:72, tt * 128:(tt + 1) * 128], stgq[:, 8 + tt, hh, 0:72], [bsq], [bT], inc=(tt == 7))
                self.op(A, "activation", [bT], [bqa], out=qa[64:72, 1024:2048], in_=pT[64:72, :], func=AF.Copy)

        def att(hp):
            st_ = hp % 2
            va, bva = vaug[st_]
            for hh in range(2):
                qa, bqa = qT[2 * st_ + hh]
                ka, bka = kT[2 * st_ + hh]
                self.attend(qa, ka, slice(0, 128), [bqa, bka], va[:, :, hh, :], [bva], mst, bmst, hh * 64, PT, "tri")
            self.mixed_to_T(mst, bmst, hp % 4)
            if hp % 4 == 3:
                self.outproj_half(dr["w_out_odd"][i], hp // 4)

        prep1(0)
        prep2(0)
        for hp in range(8):
            if hp < 7:
                prep1(hp + 1)
            att(hp)
            if hp < 7:
                prep2(hp + 1)
        self.fw.barrier()

    def even(self, i):
        A, V, G = self.ACT, self.DVE, self.POOL
        layer = 2 * i
        dr = self.dr
        bc = self.b_const
        win = dr["w_in_even"][i]
        wout = dr["w_out_even"][i]

        def wcols(c0, n):
            return win[:, c0:c0 + n].rearrange("(c p) n -> p c n", p=128)
        self.norm_to_hT(self.anorm[:, layer, :])
        self.fw.barrier()
        kpad = [(self.view(A_R1 + 4096 * k, [2048], BF16), Buf("fk%d" % k)) for k in range(4)]
        for k_, (ka_, bka_) in enumerate(kpad):
            zr = slice(64, 128) if k_ % 2 == 0 else slice(0, 64)
            self.op(G, "memset", [], [bka_], ap=ka_[zr, :], constant=0.0)
        o = A_R3
        fq = [(self.view(o + 4096 * k, [2048], BF16), Buf("fq%d" % k)) for k in range(2)]
        o += 8192
        vaug = [(self.view(o + 4224 * k, [16, 2, 66], BF16), Buf("v%d" % k)) for k in range(2)]
        o += 8448
        PT = [(self.view(o + 1024 * k, [512], BF16), Buf("PT%d" % k)) for k in range(4)]
        o += 4096
        mst, bmst = self.view(o, [16, 128], BF16), Buf("mst")
        o += 4096
        zf, zf2, bzf = self.view(o, [16, 8], F32), self.view(o + 512, [16, 8], F32), Buf("zf")
        o += 1024
        for va, bva in vaug:
            self.op(G, "memset", [], [bva], ap=va[:, :, :, 64:66], constant=1.0)
        wff, bwff = self.wslot()
        self.dma(G, wff[:, :, 0:128], wcols(1536, 128), writes=[bwff])
        pA, bA = self.bank("A")
        for t in range(16):
            for c in range(8):
                self.mm(pA[:, t * 8:(t + 1) * 8], self.hT[:, c, t * 128:(t + 1) * 128], wff[:, c, 0:8], c == 0, c == 7, [self.b_hT, bwff], [bA],
                        inc=(t == 15 and c == 7))
        self.op(V, "tensor_tensor", [bA, bc], [bzf], out=zf, in0=pA[:, 0:128].rearrange("p (t h) -> p t h", h=8),
                in1=self.bfox[:, i, :].unsqueeze(1).to_broadcast([128, 16, 8]), op=ALU.add)
        self.op(V, "tensor_scalar", [bzf], [bzf], out=zf2, in0=zf, scalar1=-1.0, scalar2=None, op0=ALU.mult)
        self.op(V, "tensor_tensor", [bzf], [bzf], out=zf2, in0=zf2, in1=zf, op=ALU.min)
        self.op(A, "activation", [bzf], [bzf], out=zf2, in_=zf2, func=AF.Exp)
        self.op(A, "activation", [bzf], [bzf], out=zf2, in_=zf2, func=AF.Ln, bias=1.0)
        self.op(V, "tensor_scalar", [bzf], [bzf], out=zf, in0=zf, scalar1=0.0, scalar2=None, op0=ALU.min)
        self.op(V, "tensor_tensor", [bzf], [bzf], out=zf, in0=zf, in1=zf2, op=ALU.subtract)
        pC, bC = self.bank("A")
        for t in range(16):
            for j in range(t):
                self.mm(pC[:, t * 8:(t + 1) * 8], self.ones32, zf[:, j, :], j == 0, False, [bzf, bc], [bC], inc=False)
            self.mm(pC[:, t * 8:(t + 1) * 8], self.U32, zf[:, t, :], t == 0, True, [bzf, bc], [bC], inc=False)
        for qc in range(4):
            for j in range(4 * qc + 4):
                self.mm(pC[:, 128 + qc * 8:128 + (qc + 1) * 8], self.ones32, zf[:, j, :], j == 0, j == 4 * qc + 3, [bzf, bc], [bC],
                        inc=(qc == 3 and j == 15))
        self.op(V, "tensor_copy", [bC], [self.b_fb], out=self.csb, in_=pC[:, 0:128].rearrange("p (t h) -> p t h", h=8))
        self.op(V, "tensor_copy", [bC], [self.b_fb], out=self.cref, in_=pC[:, 128:160].rearrange("p (t h) -> p t h", h=8))
        self.op(V, "tensor_tensor", [self.b_fb], [self.b_fb], out=self.fb, in0=self.cref.unsqueeze(1).to_broadcast([128, 16, 4, 8]),
                in1=self.csb.unsqueeze(2).to_broadcast([128, 16, 4, 8]), op=ALU.subtract)
        if STOP == "gate":
            self.fw.barrier()
            return
        rtm, brtm = self.view(o, [16, 8], BF16), Buf("rtm")
        o += 256
        selT, bsel = self.view(o, [8, 128], BF16), Buf("selT")
        o += 2048
        rT, brT = self.view(o, [2048], BF16), Buf("rT")
        o += 4096
        self.dma(G, selT, dr["c_sel"].rearrange("p (h k) -> p h k", k=128), writes=[bsel])
        self.op(G, "memset", [], [brT], ap=rT, constant=0.0)
        self.op(V, "tensor_tensor", [self.b_fb], [brtm], out=rtm.rearrange("p (q t) h -> p q t h", t=4),
                in0=self.csb.rearrange("p (q t) h -> p q t h", t=4),
                in1=self.cref.unsqueeze(2).to_broadcast([128, 4, 4, 8]), op=ALU.subtract)
        self.op(V, "tensor_scalar", [brtm], [brtm], out=rtm, in0=rtm, scalar1=8.0, scalar2=None, op0=ALU.mult)
        for qc in range(4):
            pR, bR = self.bank("A")
            for tt in range(4):
                self.mm(pR[0:8, tt * 128:(tt + 1) * 128], rtm[:, 4 * qc + tt, :], self.ident, True, True, [brtm, bc], [bR], inc=(tt == 3))
            self.op(V, "tensor_copy", [bR], [brT], out=rT[0:8, qc * 512:(qc + 1) * 512], in_=pR[0:8, :])
        def load_fox(hp):
            wqk, bwqk = self.wslot()
            wv, bwv = self.wslot()
            self.dma(G, wqk[:, :, 0:128], wcols(hp * 128, 128), writes=[bwqk])
            self.dma(G, wqk[:, :, 128:256], wcols(512 + hp * 128, 128), writes=[bwqk])
            self.dma(G, wv[:, :, 0:128], wcols(1024 + hp * 128, 128), writes=[bwv])
            return wqk, bwqk, wv, bwv
        nxt = load_fox(0)
        for hp in range(4):
            st_ = hp % 2
            qa, bqa = fq[st_]
            kp = [kpad[2 * st_], kpad[2 * st_ + 1]]
            va, bva = vaug[st_]
            wqk, bwqk, wv, bwv = nxt
            for which in (0, 1):
                for tc in range(4):
                    pA, bA = self.bank("A")
                    tsl = slice(tc * 512, (tc + 1) * 512)
                    for c in range(8):
                        self.mm(pA, wqk[:, c, which * 128:(which + 1) * 128], self.hT[:, c, tsl], c == 0, c == 7,
                                [self.b_hT, bwqk], [bA], inc=(c == 7))
                    if which == 0:
                        self.op(A, "activation", [bA], [bqa], out=qa[:, tsl], in_=pA, func=AF.Copy)
                    else:
                        self.op(V, "tensor_copy", [bA], [kp[0][1]], out=kp[0][0][0:64, tsl], in_=pA[0:64, :])
                        self.op(V, "tensor_copy", [bA], [kp[1][1]], out=kp[1][0][64:128, tsl], in_=pA[64:128, :])
            for tb in range(4):
                pA, bA = self.bank("A")
                for tt in range(4):
                    t = 4 * tb + tt
                    for c in range(8):
                        self.mm(pA[:, tt * 128:(tt + 1) * 128], self.hT[:, c, t * 128:(t + 1) * 128], wv[:, c, 0:128],
                                c == 0, c == 7, [self.b_hT, bwv], [bA], inc=(tt == 3 and c == 7))
                self.op(A, "activation", [bA], [bva], out=va[:, 4 * tb:4 * tb + 4, :, 0:64],
                        in_=pA.rearrange("p (a h d) -> p a h d", h=2, d=64), func=AF.Copy)
            if hp < 3:
                nxt = load_fox(hp + 1)
            for hh in range(2):
                h = 2 * hp + hh
                self.attend(qa, kp[hh][0], slice(0, 128), [bqa, kp[hh][1]], va[:, :, hh, :], [bva], mst, bmst, hh * 64, PT, "fox",
                            bias=lambda j, qc, h=h: self.fb[:, j, qc, h:h + 1],
                            qbias=lambda c0, n, h=h: (selT[:, h, :], rT[:, c0:c0 + n], [bsel, brT]))
            self.mixed_to_T(mst, bmst, hp)
        self.outproj_half(wout, 0)
        self.fw.barrier()
        if STOP == "fox":
            return
        dqT = self.view(A_R1, [4, 2048], BF16)
        b_dqT = Buf("dqT")
        iqT, b_iqT = self.view((A_MIX + 8192) if os.environ.get("IQ_MIX") else (A_R3 + 24576), [2, 2048], BF16), Buf("iqT")
        ikT2, b_ikT = self.view(A_R3 + 32768, [2048], BF16), Buf("ikT2")
        s32s = [(self.view(A_R3 + 2048 * k, [512], F32), Buf("s32_%d" % k)) for k in range(2)]
        stgs = [(self.view(A_R3 + 4096 + 1088 * k, [528], BF16), Buf("stg%d" % k)) for k in range(2)]
        tmps = [(self.view(A_R3 + 6272 + 1024 * k, [4, 4, 16], F32), Buf("ropetmp%d" % k)) for k in range(2)]
        wA, bwA = self.wslot()
        wB, bwB = self.wslot()
        self.dma(G, wA, wcols(2056, 256), writes=[bwA])
        self.dma(G, wB[:, :, 0:212], wcols(2312, 212), writes=[bwB])
        bst = self.b_st
        for t in range(16):
            tsl = slice(t * 128, (t + 1) * 128)
            s32, bs32 = s32s[t % 2]
            stg, bstg = stgs[t % 2]
            tmp, btmp = tmps[t % 2]
            pA, bA = self.bank("A")
            for c in range(8):
                self.mm(pA[:, 0:256], self.hT[:, c, tsl], wA[:, c, :], c == 0, c == 7, [self.b_hT, bwA], [bA], inc=False)
            for c in range(8):
                self.mm(pA[:, 256:468], self.hT[:, c, tsl], wB[:, c, 0:212], c == 0, c == 7, [self.b_hT, bwB], [bA], inc=(c == 7))
            self.op(A, "activation", [bA], [bs32], out=s32[:, 0:468], in_=pA[:, 0:468], func=AF.Copy)
            if STOP == "A0":
                continue
            self.op(A, "activation", [bs32], [bstg, bst], out=stg[:, 0:128], in_=s32[:, 0:128], func=AF.Square, accum_out=self.st[:, 8:9])
            self.op(A, "activation", [bst], [bst], out=self.st[:, 9:10], in_=self.st[:, 8:9], func=AF.Sqrt, scale=1.0 / 128, bias=EPS)
            self.op(V, "reciprocal", [bst], [bst], out=self.st[:, 10:11], in_=self.st[:, 9:10])
            self.op(V, "tensor_scalar", [bs32, bst], [bstg], out=stg[:, 0:128], in0=s32[:, 0:128], scalar1=self.st[:, 10:11], scalar2=None, op0=ALU.mult)
            self.rope_inplace(V, s32[:, 128:144].rearrange("p (a h d) -> p a h d", a=1, h=1), tmp[:, 0:1, 0:1, :], t, 1, 1, [bs32], [bs32], btmp)
            t5 = tmp.rearrange("p a h d -> p (a h d)")[:, 0:80].rearrange("p (a h d) -> p a h d", a=1, h=5)
            self.rope_inplace(V, s32[:, 144:464].rearrange("p (a h d) -> p a h d", a=1, h=5)[:, :, :, 0:16], t5, t, 1, 5, [bs32], [bs32], btmp)
            if STOP == "A0b":
                continue
            self.op(G, "tensor_copy", [bs32], [bstg], out=stg[:, 128:448], in_=s32[:, 144:464])
            self.op(G, "tensor_copy", [bs32], [bstg], out=stg[:, 448:512], in_=s32[:, 400:464])
            self.op(G, "tensor_copy", [bs32], [self.b_dkr], out=self.dkr[:, t, :], in_=s32[:, 128:144])
            self.op(G, "tensor_scalar", [bs32], [self.b_dkr], out=self.iw[:, t, :], in0=s32[:, 464:468], scalar1=IDX_SCALE, scalar2=None, op0=ALU.mult)
            if STOP == "A0c":
                continue
            pT, bT = self.bank("T")
            self.tr(pT[:, 0:128], stg[:, 0:128], [bstg], [bT], inc=False)
            self.tr(pT[:, 128:256], stg[:, 384:512], [bstg], [bT], inc=True)
            self.op(A, "activation", [bT, bc], [self.b_ckvT], out=self.ckvT[:, tsl], in_=pT[:, 0:128], func=AF.Identity, scale=self.gckv[:, i:i + 1])
            self.op(A, "activation", [bT], [b_ikT], out=ikT2[:, tsl], in_=pT[:, 128:256], func=AF.Copy)
            pT, bT = self.bank("T")
            self.tr(pT[:, 0:128], stg[:, 128:256], [bstg], [bT], inc=False)
            self.tr(pT[:, 128:256], stg[:, 256:384], [bstg], [bT], inc=True)
            self.op(V, "tensor_copy", [bT], [b_iqT], out=iqT[:, 0, tsl], in_=pT[:, 0:128])
            self.op(V, "tensor_copy", [bT], [b_iqT], out=iqT[:, 1, tsl], in_=pT[:, 128:256])
        if STOP == "A1":
            self.fw.barrier()
            return
        stqs = [(self.view(A_R3 + 8320 + 1024 * k, [4, 128], BF16), Buf("stq%d" % k)) for k in range(2)]
        wqs = []
        for hp in range(4):
            wq, bwq = self.wslot()
            self.dma(G, wq[:, :, 0:128], wcols(1544 + hp * 128, 128), writes=[bwq])
            wqs.append((wq, bwq))
        kk = 0
        for hp in range(4):
            wq, bwq = wqs[hp]
            for tb in range(4):
                s32, bs32 = s32s[kk % 2]
                tmp, btmp = tmps[kk % 2]
                stq, bstq = stqs[kk % 2]
                kk += 1
                pA, bA = self.bank("A")
                for tt in range(4):
                    t = 4 * tb + tt
                    for c in range(8):
                        self.mm(pA[:, tt * 128:(tt + 1) * 128], self.hT[:, c, t * 128:(t + 1) * 128], wq[:, c, 0:128],
                                c == 0, c == 7, [self.b_hT, bwq], [bA], inc=(tt == 3 and c == 7))
                self.op(A, "activation", [bA], [bs32], out=s32, in_=pA, func=AF.Copy)
                s4 = s32.rearrange("p (a h d) -> p a h d", h=2, d=64)
                self.rope_inplace(V, s4[:, :, :, 0:16], tmp[:, :, 0:2, :], 4 * tb, 4, 2, [bs32], [bs32], btmp)
                self.op(G, "tensor_copy", [bs32], [bstq], out=stq, in_=s32.rearrange("p (a k) -> p a k", k=128))
                pT, bT = self.bank("T")
                for tt in range(4):
                    self.tr(pT[:, tt * 128:(tt + 1) * 128], stq[:, tt, :], [bstq], [bT], inc=(tt == 3))
                self.op(A, "activation", [bT], [b_dqT], out=dqT[:, hp, tb * 512:(tb + 1) * 512], in_=pT[:, 0:512], func=AF.Copy)
        self.fw.barrier()
        if STOP == "A":
            return
        maskT = self.view(A_HT, [40, 512], BF16)
        b_mask = Buf("maskT")
        scs = [(self.view(A_R3 + 8192 * k, [2048], F32), Buf("sc%d" % k)) for k in range(3)]
        mbs = [(self.view(A_MIX + 4096 + 4096 * k, [2048], BF16), Buf("mb%d" % k)) for k in range(3)]
        rls = [(self.view(A_MIX + 2048 * k, [512], F32), Buf("rl%d" % k)) for k in range(2)]
        bbiss = [Buf("bis0"), Buf("bis1"), Buf("bis2")]
        self.op(V, "tensor_copy", [bc], [b_mask], out=maskT[:, 0, 0:128], in_=self.triT)
        self.op(V, "memset", [], [b_mask], ap=maskT[:, 0, 128:256], constant=0.0)
        self.op(V, "tensor_copy", [bc], [b_mask], out=maskT[:, 1, 128:256], in_=self.triT)
        rkc = [0]

        def scores(qi):
            sc, bsc = scs[qi % 3]
            nk = (qi + 1) * 128
            qsl = slice(qi * 128, (qi + 1) * 128)
            for h in range(4):
                rows = slice(64 * (h % 2), 64 * (h % 2) + 64)
                for kc in range((nk + 511) // 512):
                    n = min(512, nk - kc * 512)
                    ksl = slice(kc * 512, kc * 512 + n)
                    pS, bS = self.bank("S")
                    self.mm(pS[:, 0:n], iqT[rows, h // 2, qsl], ikT2[rows, ksl], True, True, [b_iqT, b_ikT], [bS])
                    if h == 0:
                        self.op(V, "tensor_scalar", [bS, self.b_dkr], [bsc], out=sc[:, ksl], in0=pS[:, 0:n], scalar1=0.0, scalar2=self.iw[:, qi, 0:1],
                                op0=ALU.max, op1=ALU.mult)
                    else:
                        rl, brl = rls[rkc[0] % 2]
                        rkc[0] += 1
                        self.op(V, "tensor_scalar", [bS, self.b_dkr], [brl], out=rl[:, 0:n], in0=pS[:, 0:n], scalar1=0.0, scalar2=self.iw[:, qi, h:h + 1],
                                op0=ALU.max, op1=ALU.mult)
                        self.op(V, "tensor_tensor", [brl, bsc], [bsc], out=sc[:, ksl], in0=sc[:, ksl], in1=rl[:, 0:n], op=ALU.add)
            self.op(V, "tensor_tensor", [bsc, bc], [bsc], out=sc[:, qsl], in0=sc[:, qsl], in1=self.tri32, op=ALU.add)
            bis = self.bis[:, qi % 3, :]
            bbis = bbiss[qi % 3]
            self.op(V, "tensor_reduce", [bsc], [bbis], out=bis[:, 20:21], in_=sc[:, 0:nk], axis=AX.X, op=ALU.max)
            self.op(V, "tensor_reduce", [bsc], [bbis], out=bis[:, 21:22], in_=sc[:, 0:qi * 128], axis=AX.X, op=ALU.min)
            self.op(V, "tensor_tensor", [bbis], [bbis], out=bis[:, 22:23], in0=bis[:, 20:21], in1=bis[:, 21:22], op=ALU.subtract)
            if qi % 3 == 2:
                self.op(V, "tensor_scalar", [bbis, bc], [bbis], out=bis[:, 0:NIT], in0=self.pow2, scalar1=bis[:, 22:23], scalar2=None, op0=ALU.mult)
                self.op(V, "scalar_tensor_tensor", [bbis], [bbis], out=bis[:, 25:26], in0=bis[:, 22:23], scalar=0.5, in1=bis[:, 21:22], op0=ALU.mult, op1=ALU.add)
            else:
                self.op(V, "tensor_scalar", [bbis, bc], [bbis], out=bis[:, 0:NIT], in0=self.pow2, scalar1=bis[:, 22:23], scalar2=-0.5, op0=ALU.mult, op1=ALU.mult)
                self.op(V, "scalar_tensor_tensor", [bbis], [bbis], out=bis[:, 25:26], in0=bis[:, 22:23], scalar=-0.5, in1=bis[:, 21:22], op0=ALU.mult, op1=ALU.subtract)

        def bisect(qi):
            sc, bsc = scs[qi % 3]
            junk, bjunk = mbs[qi % 3]
            bis = self.bis[:, qi % 3, :]
            bbis = bbiss[qi % 3]
            nk = (qi + 1) * 128
            on_act = (qi % 3 != 2)
            if not on_act:
                for k in range(NIT):
                    self.op(V, "tensor_scalar", [bsc, bbis], [bjunk, bbis], out=junk[:, 0:nk], in0=sc[:, 0:nk], scalar1=bis[:, 25:26], scalar2=None,
                            op0=ALU.is_ge, op1=ALU.add, accum_out=bis[:, 23:24])
                    last = (k == NIT - 1)
                    self.op(V, "tensor_scalar", [bbis], [bbis], out=bis[:, 24:25], in0=bis[:, 23:24], scalar1=255.5, scalar2=(-1.0 if last else -0.5),
                            op0=ALU.is_ge, op1=ALU.add)
                    self.op(V, "scalar_tensor_tensor", [bbis], [bbis], out=(bis[:, 26:27] if last else bis[:, 25:26]), in0=bis[:, 24:25],
                            scalar=bis[:, k:k + 1], in1=bis[:, 25:26], op0=ALU.mult, op1=ALU.add)
            else:
                cur, oth = 25, 28
                for k in range(NIT):
                    self.op(A, "activation", [bsc, bbis], [bjunk, bbis], out=junk[:, 0:nk], in_=sc[:, 0:nk], func=AF.Sign, bias=bis[:, cur:cur + 1],
                            accum_out=bis[:, 23:24])
                    self.op(A, "activation", [bbis], [bbis], out=bis[:, 24:25], in_=bis[:, 23:24], func=AF.Sign, bias=float(nk - 511))
                    last = (k == NIT - 1)
                    dst = 27 if last else oth
                    self.op(A, "activation", [bbis], [bbis], out=bis[:, dst:dst + 1], in_=bis[:, 24:25], func=AF.Identity, scale=bis[:, k:k + 1],
                            bias=bis[:, cur:cur + 1])
                    cur, oth = oth, cur

        def finish(qi):
            sc, bsc = scs[qi % 3]
            mb, bmb = mbs[qi % 3]
            bis = self.bis[:, qi % 3, :]
            bbis = bbiss[qi % 3]
            nk = (qi + 1) * 128
            if qi % 3 != 2:
                self.op(V, "tensor_tensor", [bbis], [bbis], out=bis[:, 26:27], in0=bis[:, NIT - 1:NIT], in1=bis[:, 27:28], op=ALU.subtract)
            self.op(V, "tensor_scalar", [bsc, bbis], [bmb], out=mb[:, 0:nk], in0=sc[:, 0:nk], scalar1=bis[:, 26:27], scalar2=NEG, op0=ALU.is_lt, op1=ALU.mult)
            qc = qi // 4
            for j0 in range(0, qi + 1, 8):
                nb = min(8, qi + 1 - j0)
                pT, bT = self.bank("T")
                for jj in range(nb):
                    self.tr(pT[:, jj * 128:(jj + 1) * 128], mb[:, (j0 + jj) * 128:(j0 + jj + 1) * 128], [bmb], [bT], inc=(jj == nb - 1))
                self.op(V, "tensor_copy", [bT], [b_mask], out=maskT[:, MOFF[qc] + j0:MOFF[qc] + j0 + nb, (qi % 4) * 128:(qi % 4 + 1) * 128],
                        in_=pT[:, 0:nb * 128].rearrange("p (a k) -> p a k", k=128))

        scores(2)
        for qi in range(2, 16):
            if qi + 1 <= 15:
                scores(qi + 1)
            bisect(qi)
            if qi - 1 >= 2:
                finish(qi - 1)
        finish(15)
        self.fw.barrier()
        if STOP == "B":
            return
        dk = [(self.view(A_R3 + 4096 * k, [2048], BF16), Buf("dk%d" % k)) for k in range(4)]
        for k_, (ka_, bka_) in enumerate(dk):
            zr = slice(64, 128) if k_ % 2 == 0 else slice(0, 64)
            self.op(G, "memset", [], [bka_], ap=ka_[zr, :], constant=0.0)
        o = A_R3 + 16384
        vaug = [(self.view(o + 4224 * k, [16, 2, 66], BF16), Buf("v%d" % k)) for k in range(2)]
        o += 8448
        PT = [(self.view(o + 1024 * k, [512], BF16), Buf("PT%d" % k)) for k in range(4)]
        o += 4096
        mst, bmst = self.view(o, [16, 128], BF16), Buf("mst")
        o += 4096
        stk, bstk = self.view(o, [4, 2, 64], BF16), Buf("stk")
        o += 1024
        for va, bva in vaug:
            self.op(G, "memset", [], [bva], ap=va[:, :, :, 64:66], constant=1.0)
        wuk, bwuk = self.wslot((2048,))
        wuv, bwuv = self.wslot((2048,))
        self.dma(G, wuk[:, 0:384], dr["w_uk"][i], writes=[bwuk])
        self.dma(G, wuv[:, 0:512], dr["w_uv"][i], writes=[bwuv])
        for hp in range(4):
            st_ = hp % 2
            kp = [dk[2 * st_], dk[2 * st_ + 1]]
            va, bva = vaug[st_]
            for tb in range(4):
                pA, bA = self.bank("A")
                pB, bB = self.bank("A")
                for tt in range(4):
                    tsl = slice((4 * tb + tt) * 128, (4 * tb + tt + 1) * 128)
                    self.mm(pA[:, tt * 96:(tt + 1) * 96], self.ckvT[:, tsl], wuk[:, hp * 96:(hp + 1) * 96], True, True, [self.b_ckvT, bwuk], [bA], inc=(tt == 3))
                for tt in range(4):
                    tsl = slice((4 * tb + tt) * 128, (4 * tb + tt + 1) * 128)
                    self.mm(pB[:, tt * 128:(tt + 1) * 128], self.ckvT[:, tsl], wuv[:, hp * 128:(hp + 1) * 128], True, True, [self.b_ckvT, bwuv], [bB], inc=(tt == 3))
                self.op(A, "activation", [bA], [bstk], out=stk[:, :, :, 16:64], in_=pA[:, 0:384].rearrange("p (a h d) -> p a h d", h=2, d=48), func=AF.Copy)
                self.op(G, "tensor_copy", [self.b_dkr], [bstk], out=stk[:, :, :, 0:16], in_=self.dkr[:, 4 * tb:4 * tb + 4, :].unsqueeze(2).to_broadcast([128, 4, 2, 16]))
                self.op(A, "activation", [bB], [bva], out=va[:, 4 * tb:4 * tb + 4, :, 0:64], in_=pB.rearrange("p (a h d) -> p a h d", h=2, d=64), func=AF.Copy)
                pT, bT = self.bank("T")
                for tt in range(4):
                    self.tr(pT[:, tt * 128:(tt + 1) * 128], stk[:, tt, :, :].rearrange("p h d -> p (h d)"), [bstk], [bT], inc=(tt == 3))
                self.op(V, "tensor_copy", [bT], [kp[0][1]], out=kp[0][0][0:64, tb * 512:(tb + 1) * 512], in_=pT[0:64, 0:512])
                self.op(V, "tensor_copy", [bT], [kp[1][1]], out=kp[1][0][64:128, tb * 512:(tb + 1) * 512], in_=pT[64:128, 0:512])
            for hh in range(2):
                self.attend(dqT[:, hp, :], kp[hh][0], slice(0, 128), [b_dqT, kp[hh][1]], va[:, :, hh, :], [bva], mst, bmst, hh * 64, PT, "dsa",
                            mask=lambda qc, j, c0, N: (maskT[:, MOFF[qc] + j, c0:c0 + N], b_mask))
            self.mixed_to_T(mst, bmst, hp)
        self.outproj_half(wout, 1)
        self.fw.barrier()

    def final_norm(self, y):
        A, V = self.ACT, self.DVE
        junk = (self.view(A_R3 + 4096, [1024], BF16), Buf("junk"))
        ot = [(self.view(A_MIX + 4096 * i, [1024], F32), Buf("ot%d" % i)) for i in range(2)]
        bst = self.b_st
        yr = y.rearrange("(t p) d -> p t d", p=128)
        self.gfin = self.view(A_MIX + 8192, [1024], F32)
        self.dma(self.SP, self.gfin, self.dr["final_norm"].partition_broadcast(128), writes=[self.b_const])
        for t in range(16):
            self.op(A, "activation", [self.bx[t]], [junk[1], bst], out=junk[0], in_=self.x[:, t, :], func=AF.Square,
                    accum_out=self.ss[:, t:t + 1])
        self.op(A, "activation", [bst], [bst], out=self.rstd, in_=self.ss, func=AF.Sqrt, scale=1.0 / D, bias=EPS)
        self.op(V, "reciprocal", [bst], [bst], out=self.rstd, in_=self.rstd)
        outs = []
        for t in range(16):
            o, bo = ot[t % 2]
            self.op(V, "scalar_tensor_tensor", [self.bx[t], bst, self.b_const], [bo], out=o, in0=self.x[:, t, :],
                    scalar=self.rstd[:, t:t + 1], in1=self.gfin, op0=ALU.mult, op1=ALU.mult)
            b = Buf("y%d" % t)
            self.dma(self.SP, yr[:, t, :], o, reads=[bo], writes=[b])
            outs.append(b)
        return outs

    def store_x(self, y):
        yr = y.rearrange("(t p) d -> p t d", p=128)
        outs = []
        for t in range(16):
            b = Buf("y%d" % t)
            self.dma(self.SP, yr[:, t, :], self.x[:, t, :], reads=[self.bx[t]], writes=[b])
            outs.append(b)
        return outs


IN_SPECS = [
    ("x", [S, D], F32), ("positions", [S], I32), ("attn_norm", [4, D], F32), ("ffn_norm", [4, D], F32),
    ("final_norm", [D], F32), ("w_in_even", [2, D, EVEN_IN], F32), ("b_fox_f", [2, 8], F32), ("g_ckv", [2, 128], F32),
    ("w_uk", [2, 128, 384], F32), ("w_uv", [2, 128, 512], F32), ("w_out_even", [2, D, D], F32),
    ("w_in_odd", [2, D, 3072], F32), ("w_out_odd", [2, D, D], F32), ("w_up", [4, D, 2 * DFF], F32),
    ("conv_w", [4, 3, DFF], F32), ("conv_b", [4, DFF], F32), ("w_down", [4, DFF, D], F32),
    ("c_ident", [128, 128], F32), ("c_triT", [128, 128], F32), ("c_tri32", [128, 128], F32),
    ("c_U32", [128, 128], F32), ("c_ones32", [128, 128], F32), ("c_sel", [128, 1024], F32), ("c_invf", [16], F32), ("c_pow2", [NIT], F32),
]


def host_consts():
    i = np.arange(128)
    c = {}
    c["c_ident"] = np.eye(128, dtype=np.float32)
    c["c_triT"] = np.where(i[:, None] > i[None, :], NEG, 0.0).astype(np.float32)
    c["c_tri32"] = np.where(i[None, :] > i[:, None], -1e30, 0.0).astype(np.float32)
    c["c_U32"] = (i[:, None] <= i[None, :]).astype(np.float32)
    c["c_ones32"] = np.ones((128, 128), np.float32)
    c["c_sel"] = np.repeat(np.eye(128, 8, dtype=np.float32), 128, axis=1)
    invf = (500000.0 ** (-np.arange(0, 16, 2, dtype=np.float32) / 16)).astype(np.float32)
    c["c_invf"] = np.concatenate([invf, invf]).astype(np.float32)
    c["c_pow2"] = (2.0 ** -(np.arange(NIT) + 1.0)).astype(np.float32)
    return c


def build_nc(parts, final=True, dbg_names=()):
    nc = bass.Bass("TRN2", target_bir_lowering=False)
    dr = {}
    for name, shape, dt in IN_SPECS:
        dr[name] = nc.dram_tensor(name, shape, dt, kind="ExternalInput").ap()
    y = nc.dram_tensor("y", [S, D], F32, kind="ExternalOutput").ap()
    p = Prog(nc, dr)
    p.load_x()
    p.rope_tables()
    p.fw.barrier()
    for part in parts:
        if part[0] == "ffn":
            p.ffn(part[1])
        elif part[0] == "even":
            p.even(part[1])
        elif part[0] == "odd":
            p.odd(part[1])
    outs = p.final_norm(y) if final else p.store_x(y)
    douts = []
    for name in dbg_names:
        ap = p.dbg[name]
        shp = [128, int(np.prod(ap.shape[1:]))]
        d = nc.dram_tensor("dbg_" + name, shp, ap.dtype, kind="ExternalOutput").ap()
        b = Buf()
        flat = ap
        if len(ap.shape) == 3:
            flat = ap.rearrange("p a b -> p (a b)")
        elif len(ap.shape) == 4:
            flat = ap.rearrange("p a b c -> p (a b c)")
        p.fw.barrier()
        p.dma(p.SP, d, flat, writes=[b])
        douts.append(b)
    p.fw.wait_all(p.SP, outs + douts)
    return nc, p


def make_in_maps(inputs, cores):
    c = host_consts()
    maps = []
    for b in cores:
        m = dict(c)
        for name, shape, dt in IN_SPECS:
            if name.startswith("c_"):
                continue
            a = np.asarray(inputs[name])
            if name in ("x", "positions"):
                a = a[b]
            a = np.ascontiguousarray(a).reshape(shape)
            m[name] = a.astype(np.int32 if dt == I32 else np.float32, copy=False)
        maps.append(m)
    return maps


ALL_PARTS = [("even", 0), ("ffn", 0), ("odd", 0), ("ffn", 1), ("even", 1), ("ffn", 2), ("odd", 1), ("ffn", 3)]


def kernel(**inputs):
    nc, _ = build_nc(ALL_PARTS, final=True)
    maps = make_in_maps(inputs, list(range(8)))
    res = run_bass_kernel_spmd(nc, maps, core_ids=list(range(8)))
    return np.stack([np.asarray(r["y"], dtype=np.float32) for r in res.results], axis=0)
```

```python
import numpy as np
import concourse.bass as bass
import concourse.mybir as mybir
from concourse.bass_utils import run_bass_kernel_spmd

F32 = mybir.dt.float32
BF16 = mybir.dt.bfloat16
I32 = mybir.dt.int32
ALU = mybir.AluOpType
AF = mybir.ActivationFunctionType
AX = mybir.AxisListType

SEM_ROT = 16000
import os
STOP = os.environ.get('EVEN_STOP', '')


class Buf:
    __slots__ = ("w", "r", "name")

    def __init__(self, name=""):
        self.w = None
        self.r = {}
        self.name = name


class EngCtx:
    def __init__(self, fw, name, eng, order_raw):
        self.fw, self.name, self.eng = fw, name, eng
        self.sem = fw.new_sem(name)
        self.count = 0
        self.seen = {}
        self.order_raw = order_raw
        self.nrot = 0
        self.prev = None

    def rotate(self):
        if self.count >= SEM_ROT:
            self.nrot += 1
            self.prev = (self.sem, self.count)
            self.sem = self.fw.new_sem("%s_r%d" % (self.name, self.nrot))
            self.count = 0


class FW:
    def __init__(self, nc):
        self.nc = nc
        self.nsem = 0
        self.pe = EngCtx(self, "pe", nc.tensor, False)
        self.act = EngCtx(self, "act", nc.scalar, True)
        self.dve = EngCtx(self, "dve", nc.vector, True)
        self.pool = EngCtx(self, "pool", nc.gpsimd, True)
        self.sp = EngCtx(self, "sp", nc.sync, True)
        self.dma_pools = {}
        for q in (self.sp, self.pool):
            self.dma_pools[id(q)] = [[[self.new_sem("dma_%s%d" % (q.name, i)), 0] for i in range(12)], 0]
        self.n_ops = 0

    def new_sem(self, name):
        self.nsem += 1
        return self.nc.alloc_semaphore(name="s_%s_%d" % (name, self.nsem))

    def _deps(self, E, order_raw, seen, reads, writes):
        deps = {}

        def add(d, raw):
            if d is None:
                return
            key, sem, val = d
            if E is not None and key is E:
                if not order_raw:
                    return
            k = id(sem)
            if k not in deps or deps[k][1] < val:
                deps[k] = (sem, val)

        for b in reads:
            add(b.w, True)
        for b in writes:
            add(b.w, False)
            for d in b.r.values():
                add(d, False)
        out = []
        for k, (sem, val) in deps.items():
            if seen.get(k, -1) >= val:
                continue
            out.append((sem, val))
        return out

    def _emit_waits(self, E, ins_fn, deps):
        for sem, val in deps[1:]:
            E.eng.wait_ge(sem, val)
            E.seen[id(sem)] = max(E.seen.get(id(sem), -1), val)
        ins = ins_fn()
        if deps:
            sem, val = deps[0]
            ins._wait_ge(sem, val)
            E.seen[id(sem)] = max(E.seen.get(id(sem), -1), val)
        return ins

    def op(self, E, fn, reads=(), writes=(), inc=True):
        self.n_ops += 1
        deps = self._deps(E, E.order_raw, E.seen, reads, writes)
        ins = self._emit_waits(E, fn, deps)
        val = E.count + 1
        if inc:
            ins.then_inc(E.sem, 1)
            E.count = val
        rec = (E, E.sem, val)
        for b in writes:
            b.w = rec
            b.r = {}
        for b in reads:
            b.r[id(E)] = rec
        if inc:
            E.rotate()
        return ins

    def dma(self, Q, out_ap, in_ap, reads=(), writes=(), **kw):
        self.n_ops += 1
        pool = self.dma_pools[id(Q)]
        slot = pool[0][pool[1]]
        pool[1] = (pool[1] + 1) % len(pool[0])
        if slot[1] >= SEM_ROT:
            Q.eng.wait_ge(slot[0], slot[1])
            slot[0] = self.new_sem("dmar")
            slot[1] = 0
        sem = slot[0]
        deps = self._deps(None, True, Q.seen, reads, writes)
        if slot[1] > 0 and Q.seen.get(id(sem), -1) < slot[1]:
            deps = [d for d in deps if d[0] is not sem]
            deps.append((sem, slot[1]))
        deps = [d for d in deps if Q.seen.get(id(d[0]), -1) < d[1]]
        ins = self._emit_waits(Q, lambda: Q.eng.dma_start(out=out_ap, in_=in_ap, **kw), deps)
        slot[1] += 16
        ins.then_inc(sem, 16)
        rec = (slot, sem, slot[1])
        for b in writes:
            b.w = rec
            b.r = {}
        for b in reads:
            b.r[id(slot)] = rec
        return ins

    def wait_all(self, E, bufs):
        deps = self._deps(None, True, E.seen, bufs, ())
        for sem, val in deps:
            E.eng.wait_ge(sem, val)
            E.seen[id(sem)] = val


    def barrier(self):
        engs = [self.pe, self.act, self.dve, self.pool, self.sp]
        items = [(e.sem, e.count) for e in engs if e.count > 0]
        for pl in self.dma_pools.values():
            items += [(s[0], s[1]) for s in pl[0] if s[1] > 0]
        items += [e.prev for e in engs if e.prev is not None]
        for E in engs:
            for sem, val in items:
                if sem is E.sem:
                    continue
                if E.seen.get(id(sem), -1) >= val:
                    continue
                E.eng.wait_ge(sem, val)
                E.seen[id(sem)] = val


S = 2048
D = 1024
NT = 16
DFF = 2816
NFC = 22
EPS = 1e-6
NEG = -30000.0
NIT = 14
EVEN_IN = 2524
IDX_SCALE = (64 ** -0.5) * (4 ** -0.5)
MOFF = [0, 4, 12, 24]

A_X = 0
A_HT = 65536
A_MIX = 106496
A_R1 = 122880
A_R3 = 139264
A_CKV = 176128
A_W = 180224
A_C = 196608
A_END = 208896


def _sz(dt):
    return 4 if dt in (F32, I32) else 2


class Prog:
    def __init__(self, nc, dr, dbg=None):
        self.nc = nc
        self.dr = dr
        self.fw = FW(nc)
        self.arena = nc.alloc_sbuf_tensor("arena", [128, A_END // 4], F32)
        fw = self.fw
        self.PE, self.ACT, self.DVE, self.POOL, self.SP = fw.pe, fw.act, fw.dve, fw.pool, fw.sp
        self.ps = {}
        for n in ("A0", "A1", "S0", "S1", "O0", "O1"):
            self.ps[n] = (nc.alloc_psum_tensor("p" + n, [128, 512], F32)[:, :], Buf("p" + n))
        for n in ("T0", "T1"):
            self.ps[n] = (nc.alloc_psum_tensor("p" + n, [128, 1024], BF16)[:, :], Buf("p" + n))
        self.rot = {}
        self.dbg = dbg if dbg is not None else {}
        self.dbg_out = []
        self._consts()

    def view(self, off, shape, dt):
        n = int(np.prod(shape))
        sz = _sz(dt)
        assert off % 4 == 0 and (n * sz) % 4 == 0, (off, shape)
        a = self.arena[:, off // 4:(off + n * sz) // 4]
        if dt != F32:
            a = a.bitcast(dt)
        if len(shape) == 2:
            a = a.rearrange("p (a b) -> p a b", b=shape[1])
        elif len(shape) == 3:
            a = a.rearrange("p (a b c) -> p a b c", b=shape[1], c=shape[2])
        elif len(shape) == 4:
            a = a.rearrange("p (a b c d) -> p a b c d", b=shape[1], c=shape[2], d=shape[3])
        return a

    def bank(self, kind):
        k = self.rot.get(kind, 0)
        self.rot[kind] = k ^ 1
        return self.ps[kind + str(k)]

    def op(self, E, meth, reads, writes, inc=True, **kw):
        return self.fw.op(E, lambda: getattr(E.eng, meth)(**kw), reads, writes, inc)

    def mm(self, out, lhsT, rhs, start, stop, reads, writes, inc=True, skip=False):
        kw = dict(out=out, lhsT=lhsT, rhs=rhs, start=start, stop=stop)
        if skip:
            kw["skip_group_check"] = True
        return self.fw.op(self.PE, lambda: self.nc.tensor.matmul(**kw), reads, writes, inc)

    def tr(self, out, in_, reads, writes, inc=True):
        return self.fw.op(self.PE, lambda: self.nc.tensor.transpose(out=out, in_=in_, identity=self.ident),
                          list(reads) + [self.b_const], writes, inc)

    def dma(self, Q, out, in_, reads=(), writes=(), **kw):
        return self.fw.dma(Q, out, in_, reads, writes, **kw)

    def _consts(self):
        nc, dr = self.nc, self.dr
        o = [A_C]

        def take(shape, dt):
            n = int(np.prod(shape)) * _sz(dt)
            n = (n + 31) // 32 * 32
            v = self.view(o[0], shape, dt)
            o[0] += n
            assert o[0] <= A_END, o[0]
            return v
        self.b_const = Buf("const")
        bc = [self.b_const]
        self.ident = take([128], BF16)
        self.triT = take([128], BF16)
        self.zeros = take([264], BF16)
        self.tri32 = take([128], F32)
        self.U32 = take([128], F32)
        self.ones32 = take([128], F32)
        self.anorm = take([4, 8], F32)
        self.fnorm = take([4, 8], F32)
        self.cos2 = take([16, 16], F32)
        self.sin2 = take([16, 16], F32)
        self.invf = take([16], F32)
        self.pow2 = take([NIT], F32)
        self.pos_i = take([16], I32)
        self.pos_f = take([16], F32)
        self.ang = self.view(A_R3, [16, 16], F32)
        self.angk = self.view(A_R3 + 1024, [16, 16], F32)
        self.angi = self.view(A_R3 + 2048, [16, 16], I32)
        self.ss = take([16], F32)
        self.rstd = take([16], F32)
        self.convw = take([4, 3, NFC], F32)
        self.convb = take([4, NFC], F32)
        self.bfox = take([2, 8], F32)
        self.gckv = take([2], F32)
        self.fb = take([16, 4, 8], F32)
        self.csb = take([16, 8], F32)
        self.cref = take([4, 8], F32)
        self.zf = self.view(A_R3, [16, 8], F32)
        self.zf2 = self.view(A_R3 + 512, [16, 8], F32)
        self.iw = take([16, 4], F32)
        self.st = take([64], F32)
        self.b_st = Buf("st")
        self.dkr = take([16, 16], BF16)
        self.b_dkr = Buf("dkr")
        self.bis = take([3, 32], F32)
        self.kmT32 = take([2, 8], F32)
        self.kmT = take([2, 8], BF16)
        self.b_km = Buf("km")
        self.b_fb = Buf("fb")
        self.ptk = 0
        SP, POOL = self.SP, self.POOL
        self.dma(POOL, self.ident, dr["c_ident"], writes=bc)
        self.dma(POOL, self.triT, dr["c_triT"], writes=bc)
        self.dma(SP, self.tri32, dr["c_tri32"], writes=bc)
        self.dma(SP, self.U32, dr["c_U32"], writes=bc)
        self.dma(SP, self.ones32, dr["c_ones32"], writes=bc)
        self.dma(SP, self.anorm, dr["attn_norm"].rearrange("l (c p) -> p l c", p=128), writes=bc,
                 allow_slow_non_contiguous=True)
        self.dma(SP, self.fnorm, dr["ffn_norm"].rearrange("l (c p) -> p l c", p=128), writes=bc,
                 allow_slow_non_contiguous=True)
        self.dma(SP, self.invf, dr["c_invf"].partition_broadcast(128), writes=bc)
        self.dma(SP, self.pow2, dr["c_pow2"].partition_broadcast(128), writes=bc)
        self.dma(SP, self.pos_i, dr["positions"].rearrange("(t p) -> p t", p=128), writes=bc,
                 allow_slow_non_contiguous=True)
        self.dma(SP, self.convw, dr["conv_w"].rearrange("l j (c p) -> p l j c", p=128), writes=bc,
                 allow_slow_non_contiguous=True)
        self.dma(SP, self.convb, dr["conv_b"].rearrange("l (c p) -> p l c", p=128), writes=bc,
                 allow_slow_non_contiguous=True)
        self.dma(SP, self.bfox.rearrange("p a b -> p (a b)"), dr["b_fox_f"].rearrange("a b -> (a b)").partition_broadcast(128), writes=bc)
        self.dma(SP, self.gckv, dr["g_ckv"].rearrange("l p -> p l"), writes=bc, allow_slow_non_contiguous=True)
        self.op(self.DVE, "memset", [], bc, ap=self.zeros, constant=0.0)
        self.x = self.view(A_X, [16, 1024], F32)
        self.bx = [Buf("x%d" % t) for t in range(16)]
        self.hT = self.view(A_HT, [8, 2048], BF16)
        self.b_hT = Buf("hT")
        self.mixT = self.view(A_MIX, [4, 2048], BF16)
        self.b_mixT = Buf("mixT")
        self.ckvT = self.view(A_CKV, [2048], BF16)
        self.b_ckvT = Buf("ckvT")
        self.wslots = [(A_W + 4096 * i, Buf("w%d" % i)) for i in range(4)]
        self.wnext = 0

    def wslot(self, shape=(8, 256)):
        off, b = self.wslots[self.wnext]
        self.wnext = (self.wnext + 1) % 4
        return self.view(off, list(shape), BF16), b

    def load_x(self):
        xr = self.dr["x"].rearrange("(t p) d -> p t d", p=128)
        for t in range(16):
            self.dma(self.SP, self.x[:, t, :], xr[:, t, :], writes=[self.bx[t]])

    def rope_tables(self):
        V, A = self.DVE, self.ACT
        bc = [self.b_const]
        TWO_PI = 2.0 * np.pi
        C1 = 6.28125
        C2 = TWO_PI - C1
        self.op(V, "tensor_copy", bc, bc, out=self.pos_f, in_=self.pos_i)
        for which, dst in ((0, self.sin2), (1, self.cos2)):
            self.op(V, "tensor_tensor", bc, bc, out=self.ang,
                    in0=self.pos_f.unsqueeze(2).to_broadcast([128, 16, 16]),
                    in1=self.invf.unsqueeze(1).to_broadcast([128, 16, 16]), op=ALU.mult)
            if which == 1:
                self.op(V, "tensor_scalar", bc, bc, out=self.ang, in0=self.ang, scalar1=float(np.pi / 2), scalar2=None, op0=ALU.add)
            self.op(V, "tensor_scalar", bc, bc, out=self.angk, in0=self.ang, scalar1=float(1.0 / TWO_PI), scalar2=None, op0=ALU.mult)
            self.op(V, "tensor_copy", bc, bc, out=self.angi, in_=self.angk)
            self.op(V, "tensor_copy", bc, bc, out=self.angk, in_=self.angi)
            self.op(V, "scalar_tensor_tensor", bc, bc, out=self.ang, in0=self.angk, scalar=-C1, in1=self.ang, op0=ALU.mult, op1=ALU.add)
            self.op(V, "scalar_tensor_tensor", bc, bc, out=self.ang, in0=self.angk, scalar=-C2, in1=self.ang, op0=ALU.mult, op1=ALU.add)
            self.op(V, "tensor_scalar", bc, bc, out=self.ang, in0=self.ang, scalar1=3.1415925, scalar2=-3.1415925, op0=ALU.min, op1=ALU.max)
            self.op(A, "activation", bc, bc, out=dst, in_=self.ang, func=AF.Sin)

    def norm_to_hT(self, gain):
        A, V, G = self.ACT, self.DVE, self.POOL
        xh = [(self.view(A_R3 + 2048 * i, [1024], BF16), Buf("xh%d" % i)) for i in range(2)]
        junk = (self.view(A_R3 + 4096, [1024], BF16), Buf("junk"))
        bst = self.b_st
        for t in range(16):
            self.op(A, "activation", [self.bx[t]], [junk[1], bst], out=junk[0], in_=self.x[:, t, :], func=AF.Square,
                    accum_out=self.ss[:, t:t + 1])
        self.op(A, "activation", [bst], [bst], out=self.rstd, in_=self.ss, func=AF.Sqrt, scale=1.0 / D, bias=EPS)
        self.op(V, "reciprocal", [bst], [bst], out=self.rstd, in_=self.rstd)
        for t in range(16):
            xa, xb = xh[t % 2]
            self.op(A, "activation", [self.bx[t], bst], [xb], out=xa, in_=self.x[:, t, :], func=AF.Copy, scale=self.rstd[:, t:t + 1])
            pT, bT = self.bank("T")
            for c in range(8):
                self.tr(pT[:, c * 128:(c + 1) * 128], xa[:, c * 128:(c + 1) * 128], [xb], [bT], inc=(c == 7))
            self.op(V, "tensor_tensor", [bT, self.b_const], [self.b_hT], out=self.hT[:, :, t * 128:(t + 1) * 128],
                    in0=pT.rearrange("p (c k) -> p c k", k=128), in1=gain.unsqueeze(2).to_broadcast([128, 8, 128]), op=ALU.mult)

    def ffn(self, layer):
        A, V, G = self.ACT, self.DVE, self.POOL
        dr = self.dr
        bc = self.b_const
        self.norm_to_hT(self.fnorm[:, layer, :])
        self.fw.barrier()
        base = A_MIX
        actT = [(self.view(base + 24576 * i, [6, 2048], BF16), Buf("actT%d" % i)) for i in range(2)]
        gsb = [(self.view(base + 49152 + 8224 * i, [2056], F32), [Buf("gsb%d_%d" % (i, k)) for k in range(4)]) for i in range(2)]
        sp = A_HT + 32768
        acc = [(self.view(sp + 2048 * i, [512], F32), Buf("acc%d" % i)) for i in range(2)]
        sg = [(self.view(sp + 4096 + 2048 * i, [512], F32), Buf("sg%d" % i)) for i in range(2)]
        for gs, bgs in gsb:
            self.op(V, "memset", [], [bgs[0]], ap=gs[:, 0:2], constant=0.0)
        wup = dr["w_up"][layer]
        wdn = dr["w_down"][layer]
        groups = [(0, 6), (6, 12), (12, 18), (18, 22)]
        k = 0
        for g, (f0, f1) in enumerate(groups):
            aT, baT = actT[g % 2]
            for fc in range(f0, f1):
                ws, bws = self.wslot()
                self.dma(G, ws[:, :, 0:128], wup[:, fc * 128:(fc + 1) * 128].rearrange("(c p) n -> p c n", p=128), writes=[bws])
                self.dma(G, ws[:, :, 128:256], wup[:, DFF + fc * 128:DFF + (fc + 1) * 128].rearrange("(c p) n -> p c n", p=128), writes=[bws])
                gs, bgs = gsb[fc % 2]
                for tc in range(4):
                    tsl = slice(tc * 512, (tc + 1) * 512)
                    pG, bG = self.bank("A")
                    pU, bU = self.bank("S")
                    for c in range(8):
                        self.mm(pG, ws[:, c, 0:128], self.hT[:, c, tsl], c == 0, c == 7, [bws, self.b_hT], [bG], inc=(c == 7))
                    for c in range(8):
                        self.mm(pU, ws[:, c, 128:256], self.hT[:, c, tsl], c == 0, c == 7, [bws, self.b_hT], [bU], inc=(c == 7))
                    self.op(A, "activation", [bG], [bgs[tc]], out=gs[:, 2 + tc * 512:2 + (tc + 1) * 512], in_=pG, func=AF.Copy)
                    ac, bac = acc[k % 2]
                    sgt, bsg = sg[k % 2]
                    k += 1
                    rd = [bgs[tc], bc] + ([bgs[tc - 1]] if tc > 0 else [bgs[0]])
                    self.op(V, "tensor_scalar", rd, [bac], out=ac, in0=gs[:, 2 + tc * 512:2 + (tc + 1) * 512],
                            scalar1=self.convw[:, layer, 2, fc:fc + 1], scalar2=self.convb[:, layer, fc:fc + 1], op0=ALU.mult, op1=ALU.add)
                    self.op(V, "scalar_tensor_tensor", rd + [bac], [bac], out=ac, in0=gs[:, 1 + tc * 512:1 + (tc + 1) * 512],
                            scalar=self.convw[:, layer, 1, fc:fc + 1], in1=ac, op0=ALU.mult, op1=ALU.add)
                    self.op(V, "scalar_tensor_tensor", rd + [bac], [bac], out=ac, in0=gs[:, tc * 512:(tc + 1) * 512],
                            scalar=self.convw[:, layer, 0, fc:fc + 1], in1=ac, op0=ALU.mult, op1=ALU.add)
                    self.op(A, "activation", [bac], [bsg], out=sgt, in_=ac, func=AF.Silu)
                    self.op(V, "tensor_tensor", [bsg, bU], [baT], out=aT[:, fc - f0, tsl], in0=sgt, in1=pU, op=ALU.mult)
            nf = f1 - f0
            wd = []
            for s in range((nf + 1) // 2):
                ws, bws = self.wslot((2, 1024))
                r0 = (f0 + 2 * s) * 128
                self.dma(G, ws, wdn[r0:r0 + 256, :].rearrange("(c p) n -> p c n", p=128), writes=[bws])
                wd.append((ws, bws))
            for t in range(16):
                for nh in range(2):
                    pO, bO = self.bank("O")
                    nsl = slice(nh * 512, (nh + 1) * 512)
                    for fl in range(nf):
                        ws, bws = wd[fl // 2]
                        self.mm(pO, aT[:, fl, t * 128:(t + 1) * 128], ws[:, fl % 2, nsl], fl == 0, fl == nf - 1, [baT, bws], [bO],
                                inc=(fl == nf - 1))
                    self.op(V, "tensor_tensor", [bO, self.bx[t]], [self.bx[t]], out=self.x[:, t, nsl], in0=self.x[:, t, nsl], in1=pO, op=ALU.add)
        self.fw.barrier()

    def attend(self, qT, kT, rows, rd_qk, v, rd_v, stage, bstage, scol, PT, mode, bias=None, mask=None, qbias=None):
        A, V = self.ACT, self.DVE
        for qc in range(4):
            pO, bO = self.bank("O")
            self.mm(pO[:, 0:264], self.zeros[:, 0:128], self.zeros[:, 0:264], True, True, [self.b_const], [bO])
            pend = None

            def pv(args):
                j, q0, pt, bpt = args
                for t in range(q0 // 128, 4 * qc + 4):
                    c0 = (t - 4 * qc) * 66
                    self.mm(pO[:, c0:c0 + 65], pt[:, t * 128 - q0:t * 128 - q0 + 128], v[:, j, 0:65], False, (j == t),
                            [bpt] + rd_v, [bO], inc=(t == 4 * qc + 3), skip=True)
            for j in range(4 * qc + 4):
                q0 = max(512 * qc, 128 * j)
                N = 512 * qc + 512 - q0
                pS, bS = self.bank("S")
                diag = (mode != "dsa") and (j >= 4 * qc)
                ksl = slice(j * 128, (j + 1) * 128)
                groups = [(0, 128, True), (128, N - 128, False)] if diag else [(0, N, False)]
                groups = [g for g in groups if g[1] > 0]
                for gi, (c0, n, tri) in enumerate(groups):
                    terms = []
                    if tri:
                        terms.append((self.ident, self.triT, [self.b_const]))
                    if mode == "dsa":
                        m_ap, m_b = mask(qc, j, q0 - 512 * qc + c0, n)
                        terms.append((self.ident, m_ap, [self.b_const, m_b]))
                    if qbias is not None:
                        l_ap, r_ap, r_b = qbias(q0 + c0, n)
                        terms.append((l_ap, r_ap, r_b))
                    lastg = (gi == len(groups) - 1)
                    self.mm(pS[:, c0:c0 + n], kT[rows, ksl], qT[rows, q0 + c0:q0 + c0 + n], True, not terms, rd_qk, [bS],
                            inc=(lastg and not terms))
                    for ti, (l_ap, r_ap, r_b) in enumerate(terms):
                        lt = (ti == len(terms) - 1)
                        self.mm(pS[:, c0:c0 + n], l_ap, r_ap, False, lt, r_b, [bS], inc=(lastg and lt))
                pt, bpt = PT[self.ptk % len(PT)]
                self.ptk += 1
                kw = dict(out=pt[:, 0:N], in_=pS[:, 0:N], func=AF.Exp, scale=0.125)
                rds = [bS]
                if bias is not None:
                    kw["bias"] = bias(j, qc)
                    rds.append(self.b_fb)
                self.op(A, "activation", rds, [bpt], **kw)
                if pend is not None:
                    pv(pend)
                pend = (j, q0, pt, bpt)
            pv(pend)
            for t in range(4 * qc, 4 * qc + 4):
                c0 = (t - 4 * qc) * 66
                ri = self.st[:, (t % 8):(t % 8) + 1]
                self.op(V, "reciprocal", [bO], [self.b_st], out=ri, in_=pO[:, c0 + 64:c0 + 65])
                self.op(V, "tensor_scalar", [bO, self.b_st], [bstage], out=stage[:, t, scol:scol + 64], in0=pO[:, c0:c0 + 64],
                        scalar1=ri, scalar2=None, op0=ALU.mult)

    def rope_inplace(self, E, data, tmp, t0, nt, nh, rd, wr, btmp):
        cs = self.cos2[:, t0:t0 + nt, :].unsqueeze(2).to_broadcast([128, nt, nh, 16])
        sn = self.sin2[:, t0:t0 + nt, :].unsqueeze(2).to_broadcast([128, nt, nh, 16])
        bc = self.b_const
        self.op(E, "tensor_tensor", rd + [bc], [btmp], out=tmp, in0=data, in1=sn, op=ALU.mult)
        self.op(E, "tensor_tensor", rd + [bc], wr, out=data, in0=data, in1=cs, op=ALU.mult)
        self.op(E, "tensor_tensor", rd + [btmp], wr, out=data[:, :, :, 0:8], in0=data[:, :, :, 0:8], in1=tmp[:, :, :, 8:16], op=ALU.subtract)
        self.op(E, "tensor_tensor", rd + [btmp], wr, out=data[:, :, :, 8:16], in0=data[:, :, :, 8:16], in1=tmp[:, :, :, 0:8], op=ALU.add)

    def outproj_half(self, w, half):
        V, G = self.DVE, self.POOL
        for ng in range(2):
            ws, bws = self.wslot((4, 512))
            self.dma(G, ws, w[half * 512:(half + 1) * 512, ng * 512:(ng + 1) * 512].rearrange("(c p) n -> p c n", p=128), writes=[bws])
            nsl = slice(ng * 512, (ng + 1) * 512)
            for t in range(16):
                pA, bA = self.bank("A")
                for c in range(4):
                    self.mm(pA, self.mixT[:, c, t * 128:(t + 1) * 128], ws[:, c, :], c == 0, c == 3, [self.b_mixT, bws], [bA], inc=(c == 3))
                self.op(V, "tensor_tensor", [bA, self.bx[t]], [self.bx[t]], out=self.x[:, t, nsl], in0=self.x[:, t, nsl], in1=pA, op=ALU.add)

    def mixed_to_T(self, mst, bmst, slot):
        for half in range(2):
            pT, bT = self.bank("T")
            for tt in range(8):
                self.tr(pT[:, tt * 128:(tt + 1) * 128], mst[:, half * 8 + tt, :], [bmst], [bT], inc=(tt == 7))
            self.op(self.ACT, "activation", [bT], [self.b_mixT], out=self.mixT[:, slot, half * 1024:(half + 1) * 1024], in_=pT, func=AF.Copy)

    def odd(self, i):
        A, V, G = self.ACT, self.DVE, self.POOL
        layer = 2 * i + 1
        dr = self.dr
        bc = self.b_const
        win = dr["w_in_odd"][i]
        self.norm_to_hT(self.anorm[:, layer, :])
        self.fw.barrier()
        qT = [(self.view(A_R1 + 4096 * k, [2048], BF16), Buf("qT%d" % k)) for k in range(4)]
        kT = [(self.view(A_R3 + 4096 * k, [2048], BF16), Buf("kT%d" % k)) for k in range(4)]
        o = A_R3 + 16384
        vaug = [(self.view(o + 4224 * k, [16, 2, 66], BF16), Buf("v%d" % k)) for k in range(2)]
        o += 8448
        stgq, bsq = self.view(o, [16, 2, 72], BF16), Buf("stgq")
        o += 4608
        stgk, bsk = self.view(o, [16, 2, 72], BF16), Buf("stgk")
        o += 4608
        s32, bs32 = self.view(o, [512], F32), Buf("s32")
        o += 2048
        assert o <= A_R3 + 36864
        sp = A_HT + 32768
        PT = [(self.view(sp + 1024 * k, [512], BF16), Buf("PT%d" % k)) for k in range(4)]
        mst, bmst = self.view(sp + 4096, [16, 128], BF16), Buf("mst")
        tmp, btmp = self.view(A_CKV, [4, 2, 16], F32), Buf("ropetmp")
        gt, bgt = self.view(A_CKV + 1024, [8, 2, 8], F32), Buf("gt")
        rank, brk = self.view(A_CKV + 2048, [8, 2, 8], F32), Buf("rank")
        cmp_, bcmp = self.view(A_CKV + 3072, [8, 2, 8], F32), Buf("cmp")
        for va, bva in vaug:
            self.op(G, "memset", [], [bva], ap=va[:, :, :, 64:66], constant=1.0)
        self.op(G, "memset", [], [bsq], ap=stgq[:, :, :, 64:72], constant=0.0)
        self.op(G, "memset", [], [bsk], ap=stgk[:, :, :, 64:72], constant=0.0)
        for b in range(8):
            self.op(G, "memset", [], [bsk], ap=stgk[:, 2 * b:2 * b + 2, :, 64 + b:65 + b], constant=1.0)
        for qa, bqa in qT:
            self.op(G, "memset", [], [bqa], ap=qa[64:128, :], constant=0.0)
        for ka_, bka_ in kT:
            self.op(G, "memset", [], [bka_], ap=ka_[64:128, :], constant=0.0)
        def load_pair(hp):
            wqk, bwqk = self.wslot()
            wv, bwv = self.wslot()
            self.dma(G, wqk[:, :, 0:128], win[:, hp * 128:(hp + 1) * 128].rearrange("(c p) n -> p c n", p=128), writes=[bwqk])
            self.dma(G, wqk[:, :, 128:256], win[:, 1024 + hp * 128:1024 + (hp + 1) * 128].rearrange("(c p) n -> p c n", p=128), writes=[bwqk])
            self.dma(G, wv[:, :, 0:128], win[:, 2048 + hp * 128:2048 + (hp + 1) * 128].rearrange("(c p) n -> p c n", p=128), writes=[bwv])
            return wqk, bwqk, wv, bwv
        wts = {0: load_pair(0)}

        def prep1(hp):
            st_ = hp % 2
            wqk, bwqk, wv, bwv = wts[hp]
            va, bva = vaug[st_]
            for tb in range(4):
                for which, stg, bstg in ((0, stgq, bsq), (1, stgk, bsk)):
                    pA, bA = self.bank("A")
                    for tt in range(4):
                        t = 4 * tb + tt
                        for c in range(8):
                            self.mm(pA[:, tt * 128:(tt + 1) * 128], self.hT[:, c, t * 128:(t + 1) * 128], wqk[:, c, which * 128:(which + 1) * 128],
                                    c == 0, c == 7, [self.b_hT, bwqk], [bA], inc=(tt == 3 and c == 7))
                    self.op(A, "activation", [bA], [bs32], out=s32, in_=pA, func=AF.Copy)
                    s4 = s32.rearrange("p (a h d) -> p a h d", h=2, d=64)
                    self.rope_inplace(V, s4[:, :, :, 0:16], tmp, 4 * tb, 4, 2, [bs32], [bs32], btmp)
                    self.op(G, "tensor_copy", [bs32], [bstg], out=stg[:, 4 * tb:4 * tb + 4, :, 0:64], in_=s4)
                pA, bA = self.bank("A")
                for tt in range(4):
                    t = 4 * tb + tt
                    for c in range(8):
                        self.mm(pA[:, tt * 128:(tt + 1) * 128], self.hT[:, c, t * 128:(t + 1) * 128], wv[:, c, 0:128],
                                c == 0, c == 7, [self.b_hT, bwv], [bA], inc=(tt == 3 and c == 7))
                self.op(A, "activation", [bA], [bva], out=va[:, 4 * tb:4 * tb + 4, :, 0:64],
                        in_=pA.rearrange("p (a h d) -> p a h d", h=2, d=64), func=AF.Copy)
            if hp < 7:
                wts[hp + 1] = load_pair(hp + 1)
            for hh in range(2):
                qa, bqa = qT[2 * st_ + hh]
                ka, bka = kT[2 * st_ + hh]
                for half in range(2):
                    pT, bT = self.bank("T")
                    for tt in range(8):
                        self.tr(pT[0:64, tt * 128:(tt + 1) * 128], stgq[:, half * 8 + tt, hh, 0:64], [bsq], [bT], inc=(tt == 7))
                    self.op(A, "activation", [bT], [bqa], out=qa[0:64, half * 1024:(half + 1) * 1024], in_=pT[0:64, :], func=AF.Copy)
                    pT, bT = self.bank("T")
                    for tt in range(8):
                        self.tr(pT[0:72, tt * 128:(tt + 1) * 128], stgk[:, half * 8 + tt, hh, 0:72], [bsk], [bT], inc=(tt == 7))
                    self.op(V, "tensor_copy", [bT], [bka], out=ka[0:72, half * 1024:(half + 1) * 1024], in_=pT[0:72, :])
                self.op(V, "tensor_reduce", [bka], [self.b_km], out=self.kmT32[0:64, hh, :],
                        in_=ka[0:64, :].rearrange("p (n k) -> p n k", k=256), axis=AX.X, op=ALU.add)
            self.op(V, "tensor_scalar", [self.b_km], [self.b_km], out=self.kmT[0:64, :, :], in0=self.kmT32[0:64, :, :], scalar1=1.0 / 256, scalar2=None, op0=ALU.mult)
            pA, bA = self.bank("A")
            for t in range(8, 16):
                for hh in range(2):
                    qa, bqa = qT[2 * st_ + hh]
                    c0 = ((t - 8) * 2 + hh) * 8
                    self.mm(pA[:, c0:c0 + 8], qa[0:64, t * 128:(t + 1) * 128], self.kmT[0:64, hh, :], True, True, [bqa, self.b_km], [bA],
                            inc=(t == 15 and hh == 1))
            self.op(V, "tensor_copy", [bA], [bgt], out=gt, in_=pA[:, 0:128].rearrange("p (a h n) -> p a h n", h=2, n=8))
            for n1 in range(7):
                a0 = max(0, n1 - 3) * 2 if n1 >= 4 else 0
                na = 8 - a0
                dst = rank if n1 == 0 else cmp_
                bd = brk if n1 == 0 else bcmp
                self.op(V, "tensor_tensor", [bgt], [bd], out=dst[:, a0:8, :, :], in0=gt[:, a0:8, :, n1:n1 + 1].to_broadcast([128, na, 2, 8]),
                        in1=gt[:, a0:8, :, :], op=ALU.is_gt)
                if n1 > 0:
                    self.op(V, "tensor_tensor", [brk, bcmp], [brk], out=rank[:, a0:8, :, :], in0=rank[:, a0:8, :, :], in1=cmp_[:, a0:8, :, :], op=ALU.add)
            for t in range(8, 16):
                ob = t // 2
                self.op(V, "tensor_scalar", [brk], [bsq], out=stgq[:, t, :, 64:64 + ob], in0=rank[:, t - 8, :, 0:ob], scalar1=2.5, scalar2=NEG,
                        op0=ALU.is_gt, op1=ALU.mult)

        def prep2(hp):
            st_ = hp % 2
            for hh in range(2):
                qa, bqa = qT[2 * st_ + hh]
                pT, bT = self.bank("T")
                for tt in range(8):
                    self.tr(pT[0:72, tt * 128:(tt + 1) * 128], stgq[:, 8 + tt, hh, 0:72], [bsq], [bT], inc=(tt == 7))
                self.op(A, "activation", [bT], [bqa], out=qa[64:72, 1024:2048], in_=pT[64:72, :], func=AF.Copy)

        def att(hp):
            st_ = hp % 2
            va, bva = vaug[st_]
            for hh in range(2):
                qa, bqa = qT[2 * st_ + hh]
                ka, bka = kT[2 * st_ + hh]
                self.attend(qa, ka, slice(0, 128), [bqa, bka], va[:, :, hh, :], [bva], mst, bmst, hh * 64, PT, "tri")
            self.mixed_to_T(mst, bmst, hp % 4)
            if hp % 4 == 3:
                self.outproj_half(dr["w_out_odd"][i], hp // 4)

        prep1(0)
        prep2(0)
        for hp in range(8):
            if hp < 7:
                prep1(hp + 1)
            att(hp)
            if hp < 7:
                prep2(hp + 1)
        self.fw.barrier()

    def even(self, i):
        A, V, G = self.ACT, self.DVE, self.POOL
        layer = 2 * i
        dr = self.dr
        bc = self.b_const
        win = dr["w_in_even"][i]
        wout = dr["w_out_even"][i]

        def wcols(c0, n):
            return win[:, c0:c0 + n].rearrange("(c p) n -> p c n", p=128)
        self.norm_to_hT(self.anorm[:, layer, :])
        self.fw.barrier()
        kpad = [(self.view(A_R1 + 4096 * k, [2048], BF16), Buf("fk%d" % k)) for k in range(4)]
        for k_, (ka_, bka_) in enumerate(kpad):
            zr = slice(64, 128) if k_ % 2 == 0 else slice(0, 64)
            self.op(G, "memset", [], [bka_], ap=ka_[zr, :], constant=0.0)
        o = A_R3
        fq = [(self.view(o + 4096 * k, [2048], BF16), Buf("fq%d" % k)) for k in range(2)]
        o += 8192
        vaug = [(self.view(o + 4224 * k, [16, 2, 66], BF16), Buf("v%d" % k)) for k in range(2)]
        o += 8448
        PT = [(self.view(o + 1024 * k, [512], BF16), Buf("PT%d" % k)) for k in range(4)]
        o += 4096
        mst, bmst = self.view(o, [16, 128], BF16), Buf("mst")
        o += 4096
        zf, zf2, bzf = self.view(o, [16, 8], F32), self.view(o + 512, [16, 8], F32), Buf("zf")
        o += 1024
        for va, bva in vaug:
            self.op(G, "memset", [], [bva], ap=va[:, :, :, 64:66], constant=1.0)
        wff, bwff = self.wslot()
        self.dma(G, wff[:, :, 0:128], wcols(1536, 128), writes=[bwff])
        pA, bA = self.bank("A")
        for t in range(16):
            for c in range(8):
                self.mm(pA[:, t * 8:(t + 1) * 8], self.hT[:, c, t * 128:(t + 1) * 128], wff[:, c, 0:8], c == 0, c == 7, [self.b_hT, bwff], [bA],
                        inc=(t == 15 and c == 7))
        self.op(V, "tensor_tensor", [bA, bc], [bzf], out=zf, in0=pA[:, 0:128].rearrange("p (t h) -> p t h", h=8),
                in1=self.bfox[:, i, :].unsqueeze(1).to_broadcast([128, 16, 8]), op=ALU.add)
        self.op(V, "tensor_scalar", [bzf], [bzf], out=zf2, in0=zf, scalar1=-1.0, scalar2=None, op0=ALU.mult)
        self.op(V, "tensor_tensor", [bzf], [bzf], out=zf2, in0=zf2, in1=zf, op=ALU.min)
        self.op(A, "activation", [bzf], [bzf], out=zf2, in_=zf2, func=AF.Exp)
        self.op(A, "activation", [bzf], [bzf], out=zf2, in_=zf2, func=AF.Ln, bias=1.0)
        self.op(V, "tensor_scalar", [bzf], [bzf], out=zf, in0=zf, scalar1=0.0, scalar2=None, op0=ALU.min)
        self.op(V, "tensor_tensor", [bzf], [bzf], out=zf, in0=zf, in1=zf2, op=ALU.subtract)
        pC, bC = self.bank("A")
        for t in range(16):
            for j in range(t):
                self.mm(pC[:, t * 8:(t + 1) * 8], self.ones32, zf[:, j, :], j == 0, False, [bzf, bc], [bC], inc=False)
            self.mm(pC[:, t * 8:(t + 1) * 8], self.U32, zf[:, t, :], t == 0, True, [bzf, bc], [bC], inc=False)
        for qc in range(4):
            for j in range(4 * qc + 4):
                self.mm(pC[:, 128 + qc * 8:128 + (qc + 1) * 8], self.ones32, zf[:, j, :], j == 0, j == 4 * qc + 3, [bzf, bc], [bC],
                        inc=(qc == 3 and j == 15))
        self.op(V, "tensor_copy", [bC], [self.b_fb], out=self.csb, in_=pC[:, 0:128].rearrange("p (t h) -> p t h", h=8))
        self.op(V, "tensor_copy", [bC], [self.b_fb], out=self.cref, in_=pC[:, 128:160].rearrange("p (t h) -> p t h", h=8))
        self.op(V, "tensor_tensor", [self.b_fb], [self.b_fb], out=self.fb, in0=self.cref.unsqueeze(1).to_broadcast([128, 16, 4, 8]),
                in1=self.csb.unsqueeze(2).to_broadcast([128, 16, 4, 8]), op=ALU.subtract)
        if STOP == "gate":
            self.fw.barrier()
            return
        rtm, brtm = self.view(o, [16, 8], BF16), Buf("rtm")
        o += 256
        selT, bsel = self.view(o, [8, 128], BF16), Buf("selT")
        o += 2048
        rT, brT = self.view(o, [2048], BF16), Buf("rT")
        o += 4096
        self.dma(G, selT, dr["c_sel"].rearrange("p (h k) -> p h k", k=128), writes=[bsel])
        self.op(G, "memset", [], [brT], ap=rT, constant=0.0)
        self.op(V, "tensor_tensor", [self.b_fb], [brtm], out=rtm.rearrange("p (q t) h -> p q t h", t=4),
                in0=self.csb.rearrange("p (q t) h -> p q t h", t=4),
                in1=self.cref.unsqueeze(2).to_broadcast([128, 4, 4, 8]), op=ALU.subtract)
        self.op(V, "tensor_scalar", [brtm], [brtm], out=rtm, in0=rtm, scalar1=8.0, scalar2=None, op0=ALU.mult)
        for qc in range(4):
            pR, bR = self.bank("A")
            for tt in range(4):
                self.mm(pR[0:8, tt * 128:(tt + 1) * 128], rtm[:, 4 * qc + tt, :], self.ident, True, True, [brtm, bc], [bR], inc=(tt == 3))
            self.op(V, "tensor_copy", [bR], [brT], out=rT[0:8, qc * 512:(qc + 1) * 512], in_=pR[0:8, :])
        def load_fox(hp):
            wqk, bwqk = self.wslot()
            wv, bwv = self.wslot()
            self.dma(G, wqk[:, :, 0:128], wcols(hp * 128, 128), writes=[bwqk])
            self.dma(G, wqk[:, :, 128:256], wcols(512 + hp * 128, 128), writes=[bwqk])
            self.dma(G, wv[:, :, 0:128], wcols(1024 + hp * 128, 128), writes=[bwv])
            return wqk, bwqk, wv, bwv
        nxt = load_fox(0)
        for hp in range(4):
            st_ = hp % 2
            qa, bqa = fq[st_]
            kp = [kpad[2 * st_], kpad[2 * st_ + 1]]
            va, bva = vaug[st_]
            wqk, bwqk, wv, bwv = nxt
            for which in (0, 1):
                for tc in range(4):
                    pA, bA = self.bank("A")
                    tsl = slice(tc * 512, (tc + 1) * 512)
                    for c in range(8):
                        self.mm(pA, wqk[:, c, which * 128:(which + 1) * 128], self.hT[:, c, tsl], c == 0, c == 7,
                                [self.b_hT, bwqk], [bA], inc=(c == 7))
                    if which == 0:
                        self.op(A, "activation", [bA], [bqa], out=qa[:, tsl], in_=pA, func=AF.Copy)
                    else:
                        self.op(V, "tensor_copy", [bA], [kp[0][1]], out=kp[0][0][0:64, tsl], in_=pA[0:64, :])
                        self.op(V, "tensor_copy", [bA], [kp[1][1]], out=kp[1][0][64:128, tsl], in_=pA[64:128, :])
            for tb in range(4):
                pA, bA = self.bank("A")
                for tt in range(4):
                    t = 4 * tb + tt
                    for c in range(8):
                        self.mm(pA[:, tt * 128:(tt + 1) * 128], self.hT[:, c, t * 128:(t + 1) * 128], wv[:, c, 0:128],
                                c == 0, c == 7, [self.b_hT, bwv], [bA], inc=(tt == 3 and c == 7))
                self.op(A, "activation", [bA], [bva], out=va[:, 4 * tb:4 * tb + 4, :, 0:64],
                        in_=pA.rearrange("p (a h d) -> p a h d", h=2, d=64), func=AF.Copy)
            if hp < 3:
                nxt = load_fox(hp + 1)
            for hh in range(2):
                h = 2 * hp + hh
                self.attend(qa, kp[hh][0], slice(0, 128), [bqa, kp[hh][1]], va[:, :, hh, :], [bva], mst, bmst, hh * 64, PT, "fox",
                            bias=lambda j, qc, h=h: self.fb[:, j, qc, h:h + 1],
                            qbias=lambda c0, n, h=h: (selT[:, h, :], rT[:, c0:c0 + n], [bsel, brT]))
            self.mixed_to_T(mst, bmst, hp)
        self.outproj_half(wout, 0)
        self.fw.barrier()
        if STOP == "fox":
            return
        dqT = self.view(A_R1, [4, 2048], BF16)
        b_dqT = Buf("dqT")
        iqT, b_iqT = self.view((A_MIX + 8192) if os.environ.get("IQ_MIX") else (A_R3 + 24576), [2, 2048], BF16), Buf("iqT")
        ikT2, b_ikT = self.view(A_R3 + 32768, [2048], BF16), Buf("ikT2")
        s32s = [(self.view(A_R3 + 2048 * k, [512], F32), Buf("s32_%d" % k)) for k in range(2)]
        stgs = [(self.view(A_R3 + 4096 + 1088 * k, [528], BF16), Buf("stg%d" % k)) for k in range(2)]
        tmps = [(self.view(A_R3 + 6272 + 1024 * k, [4, 4, 16], F32), Buf("ropetmp%d" % k)) for k in range(2)]
        wA, bwA = self.wslot()
        wB, bwB = self.wslot()
        self.dma(G, wA, wcols(2056, 256), writes=[bwA])
        self.dma(G, wB[:, :, 0:212], wcols(2312, 212), writes=[bwB])
        bst = self.b_st
        for t in range(16):
            tsl = slice(t * 128, (t + 1) * 128)
            s32, bs32 = s32s[t % 2]
            stg, bstg = stgs[t % 2]
            tmp, btmp = tmps[t % 2]
            pA, bA = self.bank("A")
            for c in range(8):
                self.mm(pA[:, 0:256], self.hT[:, c, tsl], wA[:, c, :], c == 0, c == 7, [self.b_hT, bwA], [bA], inc=False)
            for c in range(8):
                self.mm(pA[:, 256:468], self.hT[:, c, tsl], wB[:, c, 0:212], c == 0, c == 7, [self.b_hT, bwB], [bA], inc=(c == 7))
            self.op(A, "activation", [bA], [bs32], out=s32[:, 0:468], in_=pA[:, 0:468], func=AF.Copy)
            if STOP == "A0":
                continue
            self.op(A, "activation", [bs32], [bstg, bst], out=stg[:, 0:128], in_=s32[:, 0:128], func=AF.Square, accum_out=self.st[:, 8:9])
            self.op(A, "activation", [bst], [bst], out=self.st[:, 9:10], in_=self.st[:, 8:9], func=AF.Sqrt, scale=1.0 / 128, bias=EPS)
            self.op(V, "reciprocal", [bst], [bst], out=self.st[:, 10:11], in_=self.st[:, 9:10])
            self.op(V, "tensor_scalar", [bs32, bst], [bstg], out=stg[:, 0:128], in0=s32[:, 0:128], scalar1=self.st[:, 10:11], scalar2=None, op0=ALU.mult)
            self.rope_inplace(V, s32[:, 128:144].rearrange("p (a h d) -> p a h d", a=1, h=1), tmp[:, 0:1, 0:1, :], t, 1, 1, [bs32], [bs32], btmp)
            t5 = tmp.rearrange("p a h d -> p (a h d)")[:, 0:80].rearrange("p (a h d) -> p a h d", a=1, h=5)
            self.rope_inplace(V, s32[:, 144:464].rearrange("p (a h d) -> p a h d", a=1, h=5)[:, :, :, 0:16], t5, t, 1, 5, [bs32], [bs32], btmp)
            if STOP == "A0b":
                continue
            self.op(G, "tensor_copy", [bs32], [bstg], out=stg[:, 128:448], in_=s32[:, 144:464])
            self.op(G, "tensor_copy", [bs32], [bstg], out=stg[:, 448:512], in_=s32[:, 400:464])
            self.op(G, "tensor_copy", [bs32], [self.b_dkr], out=self.dkr[:, t, :], in_=s32[:, 128:144])
            self.op(G, "tensor_scalar", [bs32], [self.b_dkr], out=self.iw[:, t, :], in0=s32[:, 464:468], scalar1=IDX_SCALE, scalar2=None, op0=ALU.mult)
            if STOP == "A0c":
                continue
            pT, bT = self.bank("T")
            self.tr(pT[:, 0:128], stg[:, 0:128], [bstg], [bT], inc=False)
            self.tr(pT[:, 128:256], stg[:, 384:512], [bstg], [bT], inc=True)
            self.op(A, "activation", [bT, bc], [self.b_ckvT], out=self.ckvT[:, tsl], in_=pT[:, 0:128], func=AF.Identity, scale=self.gckv[:, i:i + 1])
            self.op(A, "activation", [bT], [b_ikT], out=ikT2[:, tsl], in_=pT[:, 128:256], func=AF.Copy)
            pT, bT = self.bank("T")
            self.tr(pT[:, 0:128], stg[:, 128:256], [bstg], [bT], inc=False)
            self.tr(pT[:, 128:256], stg[:, 256:384], [bstg], [bT], inc=True)
            self.op(V, "tensor_copy", [bT], [b_iqT], out=iqT[:, 0, tsl], in_=pT[:, 0:128])
            self.op(V, "tensor_copy", [bT], [b_iqT], out=iqT[:, 1, tsl], in_=pT[:, 128:256])
        if STOP == "A1":
            self.fw.barrier()
            return
        stqs = [(self.view(A_R3 + 8320 + 1024 * k, [4, 128], BF16), Buf("stq%d" % k)) for k in range(2)]
        wqs = []
        for hp in range(4):
            wq, bwq = self.wslot()
            self.dma(G, wq[:, :, 0:128], wcols(1544 + hp * 128, 128), writes=[bwq])
            wqs.append((wq, bwq))
        kk = 0
        for hp in range(4):
            wq, bwq = wqs[hp]
            for tb in range(4):
                s32, bs32 = s32s[kk % 2]
                tmp, btmp = tmps[kk % 2]
                stq, bstq = stqs[kk % 2]
                kk += 1
                pA, bA = self.bank("A")
                for tt in range(4):
                    t = 4 * tb + tt
                    for c in range(8):
                        self.mm(pA[:, tt * 128:(tt + 1) * 128], self.hT[:, c, t * 128:(t + 1) * 128], wq[:, c, 0:128],
                                c == 0, c == 7, [self.b_hT, bwq], [bA], inc=(tt == 3 and c == 7))
                self.op(A, "activation", [bA], [bs32], out=s32, in_=pA, func=AF.Copy)
                s4 = s32.rearrange("p (a h d) -> p a h d", h=2, d=64)
                self.rope_inplace(V, s4[:, :, :, 0:16], tmp[:, :, 0:2, :], 4 * tb, 4, 2, [bs32], [bs32], btmp)
                self.op(G, "tensor_copy", [bs32], [bstq], out=stq, in_=s32.rearrange("p (a k) -> p a k", k=128))
                pT, bT = self.bank("T")
                for tt in range(4):
                    self.tr(pT[:, tt * 128:(tt + 1) * 128], stq[:, tt, :], [bstq], [bT], inc=(tt == 3))
                self.op(A, "activation", [bT], [b_dqT], out=dqT[:, hp, tb * 512:(tb + 1) * 512], in_=pT[:, 0:512], func=AF.Copy)
        self.fw.barrier()
        if STOP == "A":
            return
        maskT = self.view(A_HT, [40, 512], BF16)
        b_mask = Buf("maskT")
        scs = [(self.view(A_R3 + 8192 * k, [2048], F32), Buf("sc%d" % k)) for k in range(3)]
        mbs = [(self.view(A_MIX + 4096 + 4096 * k, [2048], BF16), Buf("mb%d" % k)) for k in range(3)]
        rls = [(self.view(A_MIX + 2048 * k, [512], F32), Buf("rl%d" % k)) for k in range(2)]
        bbiss = [Buf("bis0"), Buf("bis1"), Buf("bis2")]
        self.op(V, "tensor_copy", [bc], [b_mask], out=maskT[:, 0, 0:128], in_=self.triT)
        self.op(V, "memset", [], [b_mask], ap=maskT[:, 0, 128:256], constant=0.0)
        self.op(V, "tensor_copy", [bc], [b_mask], out=maskT[:, 1, 128:256], in_=self.triT)
        rkc = [0]

        def scores(qi):
            sc, bsc = scs[qi % 3]
            nk = (qi + 1) * 128
            qsl = slice(qi * 128, (qi + 1) * 128)
            for h in range(4):
                rows = slice(64 * (h % 2), 64 * (h % 2) + 64)
                for kc in range((nk + 511) // 512):
                    n = min(512, nk - kc * 512)
                    ksl = slice(kc * 512, kc * 512 + n)
                    pS, bS = self.bank("S")
                    self.mm(pS[:, 0:n], iqT[rows, h // 2, qsl], ikT2[rows, ksl], True, True, [b_iqT, b_ikT], [bS])
                    if h == 0:
                        self.op(V, "tensor_scalar", [bS, self.b_dkr], [bsc], out=sc[:, ksl], in0=pS[:, 0:n], scalar1=0.0, scalar2=self.iw[:, qi, 0:1],
                                op0=ALU.max, op1=ALU.mult)
                    else:
                        rl, brl = rls[rkc[0] % 2]
                        rkc[0] += 1
                        self.op(V, "tensor_scalar", [bS, self.b_dkr], [brl], out=rl[:, 0:n], in0=pS[:, 0:n], scalar1=0.0, scalar2=self.iw[:, qi, h:h + 1],
                                op0=ALU.max, op1=ALU.mult)
                        self.op(V, "tensor_tensor", [brl, bsc], [bsc], out=sc[:, ksl], in0=sc[:, ksl], in1=rl[:, 0:n], op=ALU.add)
            self.op(V, "tensor_tensor", [bsc, bc], [bsc], out=sc[:, qsl], in0=sc[:, qsl], in1=self.tri32, op=ALU.add)
            bis = self.bis[:, qi % 3, :]
            bbis = bbiss[qi % 3]
            self.op(V, "tensor_reduce", [bsc], [bbis], out=bis[:, 20:21], in_=sc[:, 0:nk], axis=AX.X, op=ALU.max)
            self.op(V, "tensor_reduce", [bsc], [bbis], out=bis[:, 21:22], in_=sc[:, 0:qi * 128], axis=AX.X, op=ALU.min)
            self.op(V, "tensor_tensor", [bbis], [bbis], out=bis[:, 22:23], in0=bis[:, 20:21], in1=bis[:, 21:22], op=ALU.subtract)
            if qi % 3 == 2:
                self.op(V, "tensor_scalar", [bbis, bc], [bbis], out=bis[:, 0:NIT], in0=self.pow2, scalar1=bis[:, 22:23], scalar2=None, op0=ALU.mult)
                self.op(V, "scalar_tensor_tensor", [bbis], [bbis], out=bis[:, 25:26], in0=bis[:, 22:23], scalar=0.5, in1=bis[:, 21:22], op0=ALU.mult, op1=ALU.add)
            else:
                self.op(V, "tensor_scalar", [bbis, bc], [bbis], out=bis[:, 0:NIT], in0=self.pow2, scalar1=bis[:, 22:23], scalar2=-0.5, op0=ALU.mult, op1=ALU.mult)
                self.op(V, "scalar_tensor_tensor", [bbis], [bbis], out=bis[:, 25:26], in0=bis[:, 22:23], scalar=-0.5, in1=bis[:, 21:22], op0=ALU.mult, op1=ALU.subtract)

        def bisect(qi):
            sc, bsc = scs[qi % 3]
            junk, bjunk = mbs[qi % 3]
            bis = self.bis[:, qi % 3, :]
            bbis = bbiss[qi % 3]
            nk = (qi + 1) * 128
            on_act = (qi % 3 != 2)
            if not on_act:
                for k in range(NIT):
                    self.op(V, "tensor_scalar", [bsc, bbis], [bjunk, bbis], out=junk[:, 0:nk], in0=sc[:, 0:nk], scalar1=bis[:, 25:26], scalar2=None,
                            op0=ALU.is_ge, op1=ALU.add, accum_out=bis[:, 23:24])
                    last = (k == NIT - 1)
                    self.op(V, "tensor_scalar", [bbis], [bbis], out=bis[:, 24:25], in0=bis[:, 23:24], scalar1=255.5, scalar2=(-1.0 if last else -0.5),
                            op0=ALU.is_ge, op1=ALU.add)
                    self.op(V, "scalar_tensor_tensor", [bbis], [bbis], out=(bis[:, 26:27] if last else bis[:, 25:26]), in0=bis[:, 24:25],
                            scalar=bis[:, k:k + 1], in1=bis[:, 25:26], op0=ALU.mult, op1=ALU.add)
            else:
                cur, oth = 25, 28
                for k in range(NIT):
                    self.op(A, "activation", [bsc, bbis], [bjunk, bbis], out=junk[:, 0:nk], in_=sc[:, 0:nk], func=AF.Sign, bias=bis[:, cur:cur + 1],
                            accum_out=bis[:, 23:24])
                    self.op(A, "activation", [bbis], [bbis], out=bis[:, 24:25], in_=bis[:, 23:24], func=AF.Sign, bias=float(nk - 511))
                    last = (k == NIT - 1)
                    dst = 27 if last else oth
                    self.op(A, "activation", [bbis], [bbis], out=bis[:, dst:dst + 1], in_=bis[:, 24:25], func=AF.Identity, scale=bis[:, k:k + 1],
                            bias=bis[:, cur:cur + 1])
                    cur, oth = oth, cur

        def finish(qi):
            sc, bsc = scs[qi % 3]
            mb, bmb = mbs[qi % 3]
            bis = self.bis[:, qi % 3, :]
            bbis = bbiss[qi % 3]
            nk = (qi + 1) * 128
            if qi % 3 != 2:
                self.op(V, "tensor_tensor", [bbis], [bbis], out=bis[:, 26:27], in0=bis[:, NIT - 1:NIT], in1=bis[:, 27:28], op=ALU.subtract)
            self.op(V, "tensor_scalar", [bsc, bbis], [bmb], out=mb[:, 0:nk], in0=sc[:, 0:nk], scalar1=bis[:, 26:27], scalar2=NEG, op0=ALU.is_lt, op1=ALU.mult)
            qc = qi // 4
            for j0 in range(0, qi + 1, 8):
                nb = min(8, qi + 1 - j0)
                pT, bT = self.bank("T")
                for jj in range(nb):
                    self.tr(pT[:, jj * 128:(jj + 1) * 128], mb[:, (j0 + jj) * 128:(j0 + jj + 1) * 128], [bmb], [bT], inc=(jj == nb - 1))
                self.op(V, "tensor_copy", [bT], [b_mask], out=maskT[:, MOFF[qc] + j0:MOFF[qc] + j0 + nb, (qi % 4) * 128:(qi % 4 + 1) * 128],
                        in_=pT[:, 0:nb * 128].rearrange("p (a k) -> p a k", k=128))

        scores(2)
        for qi in range(2, 16):
            if qi + 1 <= 15:
                scores(qi + 1)
            bisect(qi)
            if qi - 1 >= 2:
                finish(qi - 1)
        finish(15)
        self.fw.barrier()
        if STOP == "B":
            return
        dk = [(self.view(A_R3 + 4096 * k, [2048], BF16), Buf("dk%d" % k)) for k in range(4)]
        for k_, (ka_, bka_) in enumerate(dk):
            zr = slice(64, 128) if k_ % 2 == 0 else slice(0, 64)
            self.op(G, "memset", [], [bka_], ap=ka_[zr, :], constant=0.0)
        o = A_R3 + 16384
        vaug = [(self.view(o + 4224 * k, [16, 2, 66], BF16), Buf("v%d" % k)) for k in range(2)]
        o += 8448
        PT = [(self.view(o + 1024 * k, [512], BF16), Buf("PT%d" % k)) for k in range(4)]
        o += 4096
        mst, bmst = self.view(o, [16, 128], BF16), Buf("mst")
        o += 4096
        stk, bstk = self.view(o, [4, 2, 64], BF16), Buf("stk")
        o += 1024
        for va, bva in vaug:
            self.op(G, "memset", [], [bva], ap=va[:, :, :, 64:66], constant=1.0)
        wuk, bwuk = self.wslot((2048,))
        wuv, bwuv = self.wslot((2048,))
        self.dma(G, wuk[:, 0:384], dr["w_uk"][i], writes=[bwuk])
        self.dma(G, wuv[:, 0:512], dr["w_uv"][i], writes=[bwuv])
        for hp in range(4):
            st_ = hp % 2
            kp = [dk[2 * st_], dk[2 * st_ + 1]]
            va, bva = vaug[st_]
            for tb in range(4):
                pA, bA = self.bank("A")
                pB, bB = self.bank("A")
                for tt in range(4):
                    tsl = slice((4 * tb + tt) * 128, (4 * tb + tt + 1) * 128)
                    self.mm(pA[:, tt * 96:(tt + 1) * 96], self.ckvT[:, tsl], wuk[:, hp * 96:(hp + 1) * 96], True, True, [self.b_ckvT, bwuk], [bA], inc=(tt == 3))
                for tt in range(4):
                    tsl = slice((4 * tb + tt) * 128, (4 * tb + tt + 1) * 128)
                    self.mm(pB[:, tt * 128:(tt + 1) * 128], self.ckvT[:, tsl], wuv[:, hp * 128:(hp + 1) * 128], True, True, [self.b_ckvT, bwuv], [bB], inc=(tt == 3))
                self.op(A, "activation", [bA], [bstk], out=stk[:, :, :, 16:64], in_=pA[:, 0:384].rearrange("p (a h d) -> p a h d", h=2, d=48), func=AF.Copy)
                self.op(G, "tensor_copy", [self.b_dkr], [bstk], out=stk[:, :, :, 0:16], in_=self.dkr[:, 4 * tb:4 * tb + 4, :].unsqueeze(2).to_broadcast([128, 4, 2, 16]))
                self.op(A, "activation", [bB], [bva], out=va[:, 4 * tb:4 * tb + 4, :, 0:64], in_=pB.rearrange("p (a h d) -> p a h d", h=2, d=64), func=AF.Copy)
                pT, bT = self.bank("T")
                for tt in range(4):
                    self.tr(pT[:, tt * 128:(tt + 1) * 128], stk[:, tt, :, :].rearrange("p h d -> p (h d)"), [bstk], [bT], inc=(tt == 3))
                self.op(V, "tensor_copy", [bT], [kp[0][1]], out=kp[0][0][0:64, tb * 512:(tb + 1) * 512], in_=pT[0:64, 0:512])
                self.op(V, "tensor_copy", [bT], [kp[1][1]], out=kp[1][0][64:128, tb * 512:(tb + 1) * 512], in_=pT[64:128, 0:512])
            for hh in range(2):
                self.attend(dqT[:, hp, :], kp[hh][0], slice(0, 128), [b_dqT, kp[hh][1]], va[:, :, hh, :], [bva], mst, bmst, hh * 64, PT, "dsa",
                            mask=lambda qc, j, c0, N: (maskT[:, MOFF[qc] + j, c0:c0 + N], b_mask))
            self.mixed_to_T(mst, bmst, hp)
        self.outproj_half(wout, 1)
        self.fw.barrier()

    def final_norm(self, y):
        A, V = self.ACT, self.DVE
        junk = (self.view(A_R3 + 4096, [1024], BF16), Buf("junk"))
        ot = [(self.view(A_MIX + 4096 * i, [1024], F32), Buf("ot%d" % i)) for i in range(2)]
        bst = self.b_st
        yr = y.rearrange("(t p) d -> p t d", p=128)
        self.gfin = self.view(A_MIX + 8192, [1024], F32)
        self.dma(self.SP, self.gfin, self.dr["final_norm"].partition_broadcast(128), writes=[self.b_const])
        for t in range(16):
            self.op(A, "activation", [self.bx[t]], [junk[1], bst], out=junk[0], in_=self.x[:, t, :], func=AF.Square,
                    accum_out=self.ss[:, t:t + 1])
        self.op(A, "activation", [bst], [bst], out=self.rstd, in_=self.ss, func=AF.Sqrt, scale=1.0 / D, bias=EPS)
        self.op(V, "reciprocal", [bst], [bst], out=self.rstd, in_=self.rstd)
        outs = []
        for t in range(16):
            o, bo = ot[t % 2]
            self.op(V, "scalar_tensor_tensor", [self.bx[t], bst, self.b_const], [bo], out=o, in0=self.x[:, t, :],
                    scalar=self.rstd[:, t:t + 1], in1=self.gfin, op0=ALU.mult, op1=ALU.mult)
            b = Buf("y%d" % t)
            self.dma(self.SP, yr[:, t, :], o, reads=[bo], writes=[b])
            outs.append(b)
        return outs

    def store_x(self, y):
        yr = y.rearrange("(t p) d -> p t d", p=128)
        outs = []
        for t in range(16):
            b = Buf("y%d" % t)
            self.dma(self.SP, yr[:, t, :], self.x[:, t, :], reads=[self.bx[t]], writes=[b])
            outs.append(b)
        return outs


IN_SPECS = [
    ("x", [S, D], F32), ("positions", [S], I32), ("attn_norm", [4, D], F32), ("ffn_norm", [4, D], F32),
    ("final_norm", [D], F32), ("w_in_even", [2, D, EVEN_IN], F32), ("b_fox_f", [2, 8], F32), ("g_ckv", [2, 128], F32),
    ("w_uk", [2, 128, 384], F32), ("w_uv", [2, 128, 512], F32), ("w_out_even", [2, D, D], F32),
    ("w_in_odd", [2, D, 3072], F32), ("w_out_odd", [2, D, D], F32), ("w_up", [4, D, 2 * DFF], F32),
    ("conv_w", [4, 3, DFF], F32), ("conv_b", [4, DFF], F32), ("w_down", [4, DFF, D], F32),
    ("c_ident", [128, 128], F32), ("c_triT", [128, 128], F32), ("c_tri32", [128, 128], F32),
    ("c_U32", [128, 128], F32), ("c_ones32", [128, 128], F32), ("c_sel", [128, 1024], F32), ("c_invf", [16], F32), ("c_pow2", [NIT], F32),
]


def host_consts():
    i = np.arange(128)
    c = {}
    c["c_ident"] = np.eye(128, dtype=np.float32)
    c["c_triT"] = np.where(i[:, None] > i[None, :], NEG, 0.0).astype(np.float32)
    c["c_tri32"] = np.where(i[None, :] > i[:, None], -1e30, 0.0).astype(np.float32)
    c["c_U32"] = (i[:, None] <= i[None, :]).astype(np.float32)
    c["c_ones32"] = np.ones((128, 128), np.float32)
    c["c_sel"] = np.repeat(np.eye(128, 8, dtype=np.float32), 128, axis=1)
    invf = (500000.0 ** (-np.arange(0, 16, 2, dtype=np.float32) / 16)).astype(np.float32)
    c["c_invf"] = np.concatenate([invf, invf]).astype(np.float32)
    c["c_pow2"] = (2.0 ** -(np.arange(NIT) + 1.0)).astype(np.float32)
    return c


def build_nc(parts, final=True, dbg_names=()):
    nc = bass.Bass("TRN2", target_bir_lowering=False)
    dr = {}
    for name, shape, dt in IN_SPECS:
        dr[name] = nc.dram_tensor(name, shape, dt, kind="ExternalInput").ap()
    y = nc.dram_tensor("y", [S, D], F32, kind="ExternalOutput").ap()
    p = Prog(nc, dr)
    p.load_x()
    p.rope_tables()
    p.fw.barrier()
    for part in parts:
        if part[0] == "ffn":
            p.ffn(part[1])
        elif part[0] == "even":
            p.even(part[1])
        elif part[0] == "odd":
            p.odd(part[1])
    outs = p.final_norm(y) if final else p.store_x(y)
    douts = []
    for name in dbg_names:
        ap = p.dbg[name]
        shp = [128, int(np.prod(ap.shape[1:]))]
        d = nc.dram_tensor("dbg_" + name, shp, ap.dtype, kind="ExternalOutput").ap()
        b = Buf()
        flat = ap
        if len(ap.shape) == 3:
            flat = ap.rearrange("p a b -> p (a b)")
        elif len(ap.shape) == 4:
            flat = ap.rearrange("p a b c -> p (a b c)")
        p.fw.barrier()
        p.dma(p.SP, d, flat, writes=[b])
        douts.append(b)
    p.fw.wait_all(p.SP, outs + douts)
    return nc, p


def make_in_maps(inputs, cores):
    c = host_consts()
    maps = []
    for b in cores:
        m = dict(c)
        for name, shape, dt in IN_SPECS:
            if name.startswith("c_"):
                continue
            a = np.asarray(inputs[name])
            if name in ("x", "positions"):
                a = a[b]
            a = np.ascontiguousarray(a).reshape(shape)
            m[name] = a.astype(np.int32 if dt == I32 else np.float32, copy=False)
        maps.append(m)
    return maps


ALL_PARTS = [("even", 0), ("ffn", 0), ("odd", 0), ("ffn", 1), ("even", 1), ("ffn", 2), ("odd", 1), ("ffn", 3)]


def kernel(**inputs):
    nc, _ = build_nc(ALL_PARTS, final=True)
    maps = make_in_maps(inputs, list(range(8)))
    res = run_bass_kernel_spmd(nc, maps, core_ids=list(range(8)))
    return np.stack([np.asarray(r["y"], dtype=np.float32) for r in res.results], axis=0)
```
